# Optimizing a Trainium2 kernel written in Bass

```python
import math
import jax, jax.numpy as jnp
from jax import lax
import numpy as np

D_MODEL = 2048
BATCH = 4
SEQ = 2048
DEPTH = 1
DEC_BATCH = 128
DEC_SEQ = 8
PAST_LEN = 16384
PAGE_SIZE = 128

EXPAND = 2
D_MIX = EXPAND * D_MODEL
D_SSM = D_MIX // 2
D_SHORT = D_MIX - D_SSM
SSD_HEAD_DIM = 64
SSD_HEADS = D_SSM // SSD_HEAD_DIM
SSD_D_STATE = 128
SSD_GROUPS = 4
SSD_HPG = SSD_HEADS // SSD_GROUPS
SSD_GN = SSD_GROUPS * SSD_D_STATE
SSD_CONV_DIM = D_SSM + 2 * SSD_GN
SSD_CONV_K = 4
SSD_CHUNK = 128
SHORT_CONV_K = 3
SHORT_CHANNEL_GROUPS = 32
EPS = 1e-6
D_IN_PROJ = D_SSM + SSD_CONV_DIM + SSD_HEADS + 4 * D_SHORT
SPLIT_POINTS = (D_SSM,
                D_SSM + SSD_CONV_DIM,
                D_SSM + SSD_CONV_DIM + SSD_HEADS,
                D_SSM + SSD_CONV_DIM + SSD_HEADS + D_SHORT,
                D_SSM + SSD_CONV_DIM + SSD_HEADS + 2 * D_SHORT,
                D_SSM + SSD_CONV_DIM + SSD_HEADS + 3 * D_SHORT)

kernel_name = "hymba_ssd_shortconv_decode_step"


def rmsnorm(x, w):
    xf = x.astype(jnp.float32)
    var = jnp.mean(xf * xf, axis=-1, keepdims=True)
    return (xf * lax.rsqrt(var + EPS) * w.astype(jnp.float32)).astype(x.dtype)


def grouped_rmsnorm(y, w, groups):
    b, L, d = y.shape
    yf = y.astype(jnp.float32).reshape(b, L, groups, d // groups)
    yf = yf * lax.rsqrt(jnp.mean(yf * yf, axis=-1, keepdims=True) + EPS)
    return (yf.reshape(b, L, d) * w.astype(jnp.float32)).astype(y.dtype)


def causal_dwconv(u, prev, w):
    k = w.shape[0]
    L = u.shape[1]
    full = jnp.concatenate([prev.astype(u.dtype), u], axis=1)
    out = full[:, 0:L] * w[0]
    for i in range(1, k):
        out = out + full[:, i:i + L] * w[i]
    return out, full[:, L:]


def ssd_scan(x, dt, a, bmat, cmat, d_skip, h0):
    bsz, L = x.shape[0], x.shape[1]
    chunk = min(SSD_CHUNK, L)
    pad = (-L) % chunk
    xf = x.astype(jnp.float32)
    bf = bmat.astype(jnp.float32)
    cf = cmat.astype(jnp.float32)
    dtf = dt
    if pad:
        pw = ((0, 0), (0, pad))
        xf_p = jnp.pad(xf, pw + ((0, 0), (0, 0)))
        bf = jnp.pad(bf, pw + ((0, 0), (0, 0)))
        cf = jnp.pad(cf, pw + ((0, 0), (0, 0)))
        dtf = jnp.pad(dtf, pw + ((0, 0),))
    else:
        xf_p = xf
    nc = (L + pad) // chunk
    xdt = (xf_p * dtf[..., None]).reshape(bsz, nc, chunk, SSD_GROUPS, SSD_HPG, SSD_HEAD_DIM)
    la = (dtf * a.astype(jnp.float32)).reshape(bsz, nc, chunk, SSD_GROUPS, SSD_HPG)
    bc = bf.reshape(bsz, nc, chunk, SSD_GROUPS, SSD_D_STATE)
    cc = cf.reshape(bsz, nc, chunk, SSD_GROUPS, SSD_D_STATE)
    a_cum = jnp.cumsum(la, axis=2)
    seg = a_cum[:, :, :, None] - a_cum[:, :, None, :]
    causal = jnp.tril(jnp.ones((chunk, chunk), dtype=bool))[:, :, None, None]
    decay = jnp.exp(jnp.where(causal, seg, -jnp.inf))
    cb = jnp.einsum('bclgn,bcsgn->bclsg', cc, bc)
    y_diag = jnp.einsum('bclsg,bclsgh,bcsghp->bclghp', cb, decay, xdt)
    decay_end = jnp.exp(a_cum[:, :, -1:] - a_cum)
    states = jnp.einsum('bclgn,bclgh,bclghp->bcghpn', bc, decay_end, xdt)
    chunk_decay = jnp.exp(a_cum[:, :, -1])

    def step(h, inp):
        s, dcy = inp
        h_next = h * dcy[..., None, None] + s
        return h_next, h

    h_init = h0.astype(jnp.float32).reshape(bsz, SSD_GROUPS, SSD_HPG, SSD_HEAD_DIM, SSD_D_STATE)
    h_fin, h_prev = lax.scan(step, h_init,
                             (jnp.moveaxis(states, 1, 0), jnp.moveaxis(chunk_decay, 1, 0)))
    h_prev = jnp.moveaxis(h_prev, 0, 1)
    y_off = jnp.einsum('bclgn,bcghpn,bclgh->bclghp', cc, h_prev, jnp.exp(a_cum))
    y = (y_diag + y_off).reshape(bsz, nc * chunk, SSD_HEADS, SSD_HEAD_DIM)[:, :L]
    y = y + xf * d_skip.astype(jnp.float32)[:, None]
    return y.astype(x.dtype), h_fin.reshape(bsz, SSD_HEADS, SSD_HEAD_DIM, SSD_D_STATE)


def mixer_layer(x, h_ssm, buf_ssd, buf_short, norm_w, w_in, conv_ssd_w, conv_ssd_b,
                dt_bias, a_log, d_skip, ssd_norm_w, conv_short_w, w_out):
    bsz, L, _ = x.shape
    h = rmsnorm(x, norm_w)
    proj = jnp.einsum('bld,de->ble', h, w_in)
    z_s, xbc, dt_raw, z_c, b_c, c_c, v_c = jnp.split(proj, SPLIT_POINTS, axis=-1)
    xbc, buf_ssd_new = causal_dwconv(xbc, buf_ssd, conv_ssd_w)
    xbc = jax.nn.silu(xbc + conv_ssd_b)
    xs, bm, cm = jnp.split(xbc, (D_SSM, D_SSM + SSD_GN), axis=-1)
    xs = xs.reshape(bsz, L, SSD_HEADS, SSD_HEAD_DIM)
    bm = bm.reshape(bsz, L, SSD_GROUPS, SSD_D_STATE)
    cm = cm.reshape(bsz, L, SSD_GROUPS, SSD_D_STATE)
    dt = jax.nn.softplus(dt_raw.astype(jnp.float32) + dt_bias.astype(jnp.float32))
    a = -jnp.exp(a_log.astype(jnp.float32))
    y_s, h_new = ssd_scan(xs, dt, a, bm, cm, d_skip, h_ssm)
    y_s = grouped_rmsnorm(y_s.reshape(bsz, L, D_SSM) * jax.nn.silu(z_s), ssd_norm_w, SSD_GROUPS)
    conv_out, buf_short_new = causal_dwconv(c_c * v_c, buf_short, conv_short_w)
    y_c = b_c * conv_out * jax.nn.silu(z_c)
    y = jnp.einsum('ble,ed->bld', jnp.concatenate([y_s, y_c], axis=-1), w_out)
    return x + y, h_new.astype(x.dtype), buf_ssd_new, buf_short_new


def setup_inputs(seed: int = 0) -> dict:
    key = jax.random.key(seed)
    ks = jax.random.split(key, 20)
    f32 = jnp.float32
    x_prompt = jax.random.normal(ks[0], (BATCH, SEQ, D_MODEL), f32)
    x_sample = jax.random.normal(ks[1], (DEC_BATCH, DEC_SEQ, D_MODEL), f32)
    state_ssm = 0.1 * jax.random.normal(ks[2], (DEPTH, DEC_BATCH, SSD_HEADS, SSD_HEAD_DIM, SSD_D_STATE), f32)
    state_conv_ssd = jax.random.normal(ks[3], (DEPTH, DEC_BATCH, SSD_CONV_K - 1, SSD_CONV_DIM), f32)
    state_conv_short = jax.random.normal(ks[4], (DEPTH, DEC_BATCH, SHORT_CONV_K - 1, D_SHORT), f32)
    norm_w = 1.0 + 0.02 * jax.random.normal(ks[5], (DEPTH, D_MODEL), f32)
    w_in = jax.random.normal(ks[6], (DEPTH, D_MODEL, D_IN_PROJ), f32) * D_MODEL ** -0.5
    conv_ssd_w = jax.random.normal(ks[7], (DEPTH, SSD_CONV_K, SSD_CONV_DIM), f32) * SSD_CONV_K ** -0.5
    conv_ssd_b = 0.02 * jax.random.normal(ks[8], (DEPTH, SSD_CONV_DIM), f32)
    dt0 = jnp.exp(jax.random.uniform(ks[9], (DEPTH, SSD_HEADS), f32,
                                     math.log(1e-3), math.log(1e-1)))
    dt_bias = dt0 + jnp.log(-jnp.expm1(-dt0))
    a_log = jnp.log(jax.random.uniform(ks[10], (DEPTH, SSD_HEADS), f32, 1.0, 16.0))
    d_skip = 1.0 + 0.1 * jax.random.normal(ks[11], (DEPTH, SSD_HEADS), f32)
    ssd_norm_w = 1.0 + 0.02 * jax.random.normal(ks[12], (DEPTH, D_SSM), f32)
    conv_short_w = jax.random.normal(ks[13], (DEPTH, SHORT_CONV_K, D_SHORT), f32) * SHORT_CONV_K ** -0.5
    w_out = jax.random.normal(ks[14], (DEPTH, D_MIX, D_MODEL), f32) * D_MIX ** -0.5
    final_norm_w = 1.0 + 0.02 * jax.random.normal(ks[15], (D_MODEL,), f32)
    return {"x_prompt": x_prompt, "x_sample": x_sample,
            "state_ssm": state_ssm, "state_conv_ssd": state_conv_ssd,
            "state_conv_short": state_conv_short,
            "norm_w": norm_w, "w_in": w_in, "conv_ssd_w": conv_ssd_w, "conv_ssd_b": conv_ssd_b,
            "dt_bias": dt_bias, "a_log": a_log, "d_skip": d_skip, "ssd_norm_w": ssd_norm_w,
            "conv_short_w": conv_short_w, "w_out": w_out, "final_norm_w": final_norm_w}


def reference(x_prompt, x_sample, state_ssm, state_conv_ssd, state_conv_short,
              norm_w, w_in, conv_ssd_w, conv_ssd_b, dt_bias, a_log, d_skip, ssd_norm_w,
              conv_short_w, w_out, final_norm_w):
    hp = x_prompt
    hs = x_sample
    ssm_p, cs_p, csh_p, ssm_s, cs_s, csh_s = [], [], [], [], [], []
    for layer in range(DEPTH):
        params = (norm_w[layer], w_in[layer], conv_ssd_w[layer], conv_ssd_b[layer],
                  dt_bias[layer], a_log[layer], d_skip[layer], ssd_norm_w[layer],
                  conv_short_w[layer], w_out[layer])
        h0 = jnp.zeros((BATCH, SSD_HEADS, SSD_HEAD_DIM, SSD_D_STATE), hp.dtype)
        b0 = jnp.zeros((BATCH, SSD_CONV_K - 1, SSD_CONV_DIM), hp.dtype)
        c0 = jnp.zeros((BATCH, SHORT_CONV_K - 1, D_SHORT), hp.dtype)
        hp, a1, a2, a3 = mixer_layer(hp, h0, b0, c0, *params)
        hs, s1, s2, s3 = mixer_layer(hs, state_ssm[layer], state_conv_ssd[layer],
                                     state_conv_short[layer], *params)
        ssm_p.append(a1); cs_p.append(a2); csh_p.append(a3)
        ssm_s.append(s1); cs_s.append(s2); csh_s.append(s3)
    y_prompt = rmsnorm(hp, final_norm_w)
    y_sample = rmsnorm(hs, final_norm_w)
    return (y_prompt, y_sample,
            jnp.stack(ssm_p), jnp.stack(cs_p), jnp.stack(csh_p),
            jnp.stack(ssm_s), jnp.stack(cs_s), jnp.stack(csh_s))
```

```python
import numpy as np
from contextlib import ExitStack
import concourse.bass as bass
import concourse.mybir as mybir
from concourse.bass_utils import run_bass_kernel_spmd

F32 = mybir.dt.float32
BF16 = mybir.dt.bfloat16
AF = mybir.ActivationFunctionType
ALU = mybir.AluOpType

DM = 2048
NP_TOK = 1024
NS_TOK = 128
NT = NP_TOK + NS_TOK
DIN = 13344
C_ZS, C_XS, C_B, C_C, C_DT, C_ZC, C_BC, C_CC, C_VC = 0, 2048, 4096, 4608, 5120, 5152, 7200, 9248, 11296
EPS = 1e-6
K_NW, K_CSW, K_CSB, K_CHW, K_SNW, K_DTB, K_ALOG, K_DSK, K_FLAG, K_END = 0, 16, 112, 136, 184, 200, 232, 264, 296, 297
M_ID, M_TRI, M_MS, M_TRIB, M_ONEB, M_BLK, M_END = 0, 128, 256, 384, 512, 640, 656


class Reg:
    __slots__ = ("w", "r")

    def __init__(self):
        self.w = None
        self.r = {}


class Chan:
    def __init__(self, sem, name):
        self.sem = sem
        self.name = name
        self.cnt = 0


class KB:
    def __init__(self, nc, es):
        self.nc = nc
        self.es = es
        self.semh = {}
        self.E = {}
        for name, h in (("pe", nc.tensor), ("act", nc.scalar), ("dve", nc.vector),
                        ("pool", nc.gpsimd), ("sp", nc.sync)):
            s = es.enter_context(nc.semaphore("e_" + name))
            self.semh["e_" + name] = s
            self.E[name] = dict(h=h, sem="e_" + name, cnt=0, waited={})
        self.nch = 0
        self.chans = []

    def chan(self):
        n = "c%d" % self.nch
        self.nch += 1
        s = self.es.enter_context(self.nc.semaphore(n))
        self.semh[n] = s
        c = Chan(n, n)
        self.chans.append(c)
        return c

    def _waits(self, e, reads, writes, skip=None):
        need = {}
        own = e["sem"]
        for d in reads:
            if d.w is not None:
                if d.w[0] == own and own == "e_pe":
                    continue
                need[d.w[0]] = max(need.get(d.w[0], 0), d.w[1])
        for d in writes:
            if d.w is not None and d.w[0] != own:
                need[d.w[0]] = max(need.get(d.w[0], 0), d.w[1])
            for s, v in d.r.items():
                if s != own:
                    need[s] = max(need.get(s, 0), v)
        for s, v in need.items():
            if s == skip:
                continue
            if e["waited"].get(s, 0) < v:
                e["h"].wait_ge(self.semh[s], v)
                e["waited"][s] = v

    def _mark(self, tok, reads, writes):
        for d in reads:
            d.r[tok[0]] = max(d.r.get(tok[0], 0), tok[1])
        for d in writes:
            d.w = tok
            d.r = {}

    def op(self, eng, fn, reads=(), writes=(), inc=True):
        e = self.E[eng]
        self._waits(e, reads, writes)
        ins = fn(e["h"])
        if inc:
            ins.then_inc(self.semh[e["sem"]], 1)
            e["cnt"] += 1
            tok = (e["sem"], e["cnt"])
        else:
            tok = (e["sem"], e["cnt"] + 1)
        self._mark(tok, reads, writes)

    def barrier(self):
        for name, e in self.E.items():
            for n2, e2 in self.E.items():
                if n2 != name and e2["cnt"] > e["waited"].get(e2["sem"], 0):
                    e["h"].wait_ge(self.semh[e2["sem"]], e2["cnt"])
                    e["waited"][e2["sem"]] = e2["cnt"]
            for ch in self.chans:
                if ch.cnt > e["waited"].get(ch.sem, 0):
                    e["h"].wait_ge(self.semh[ch.sem], ch.cnt)
                    e["waited"][ch.sem] = ch.cnt

    def dma(self, q, out, in_, ch, reads=(), writes=()):
        e = self.E[q]
        self._waits(e, reads, writes, skip=ch.sem)
        e["h"].dma_start(out=out, in_=in_).then_inc(self.semh[ch.sem], 16)
        ch.cnt += 16
        self._mark((ch.sem, ch.cnt), reads, writes)


def build_nc():
    nc = bass.Bass("TRN2", target_bir_lowering=False)

    def din(n, s):
        return nc.dram_tensor(n, s, F32, kind="ExternalInput").ap()

    def dout(n, s):
        return nc.dram_tensor(n, s, F32, kind="ExternalOutput").ap()

    xall = din("xall", [2180, DM])
    st_ssm = din("st_ssm", [16, 2048, 128])
    st_cs = din("st_cs", [48, 3072])
    st_csh = din("st_csh", [32, 2048])
    w_in = din("w_in", [128, 16 * DIN])
    w_out = din("w_out", [128, 16 * 4096])
    cst_d = din("cst", [128, K_END])
    msk_d = din("msk", [128, M_END])
    selm_d = din("selm", [128, 2048])
    fnw_d = din("fnw", [128, DM])
    y_o = dout("y", [NT, DM])
    ssm_p_o = dout("ssm_p", [2048, 128])
    cs_p_o = dout("cs_p", [4, 3072])
    csh_p_o = dout("csh_p", [4, 2048])
    ssm_s_o = dout("ssm_s", [16, 2048, 128])
    cs_s_o = dout("cs_s", [48, 3072])
    csh_s_o = dout("csh_s", [32, 2048])
    ysd = nc.dram_tensor("ysd", [32, 128, NT], BF16, kind="Internal").ap()

    with ExitStack() as es:
        kb = KB(nc, es)
        op, dma = kb.op, kb.dma

        nmc = dict(n=0)

        def sb(name, shape, dt, stack=es):
            nmc["n"] += 1
            return stack.enter_context(nc.sbuf_tensor("s%d_%s" % (nmc["n"], name), shape, dt))

        PS = [es.enter_context(nc.psum_tensor("ps%d" % i, [128, 512], F32)) for i in range(8)]
        PR = [Reg() for _ in range(8)]

        def psb(i):
            return PS[i][:].bitcast(BF16)

        cst = sb("cst", [128, K_END], F32)
        msk = sb("msk", [128, M_END], F32)
        identb = sb("identb", [128, 128], BF16)
        selm = sb("selm", [128, 2048], BF16)
        abc = sb("abc", [128, 32], F32)
        ones = sb("ones", [128, 128], F32)
        R_c = Reg()
        cch = kb.chan()
        cchp = kb.chan()
        R_c2 = Reg()
        dma("sp", cst[:], cst_d, cch, writes=[R_c])
        dma("sp", msk[:], msk_d, cch, writes=[R_c])
        dma("pool", identb[:], msk_d[:, M_ID:M_ID + 128], cchp, writes=[R_c2])
        dma("pool", selm[:], selm_d, cchp, writes=[R_c2])
        op("act", lambda e: e.activation(abc[:], cst[:, K_ALOG:K_ALOG + 32], AF.Exp), reads=[R_c, R_c2], writes=[R_c])
        op("dve", lambda e: e.tensor_scalar(abc[:], abc[:], -1.0, None, ALU.mult), reads=[R_c], writes=[R_c])
        op("dve", lambda e: e.memset(ones[:], 1.0), writes=[R_c])
        MSb = sb("MSb", [128, 128], BF16)
        op("dve", lambda e: e.tensor_copy(MSb[:], msk[:, M_MS:M_MS + 128]), reads=[R_c], writes=[R_c2])
        IDF = msk[:, M_ID:M_ID + 128]
        TRI = msk[:, M_TRI:M_TRI + 128]
        MS = msk[:, M_MS:M_MS + 128]
        TRIB = msk[:, M_TRIB:M_TRIB + 128]
        ONEB = msk[:, M_ONEB:M_ONEB + 128]
        BLK = msk[:, M_BLK:M_BLK + 16]

        dtm = sb("dtm", [128, 5, 32], F32)
        dtq = sb("dtq", [128, 8, 32], F32)
        R_dtm, R_dtq = Reg(), Reg()
        hmid = sb("hmid", [128, 4, 512], F32)
        R_hmid = Reg()
        csT = sb("csT", [128, 24, 4], F32)
        csTs = sb("csTs", [128, 24, 48], F32)
        chT = sb("chT", [128, 16, 4], F32)
        chTs = sb("chTs", [128, 16, 32], F32)
        R_cs, R_css, R_ch, R_chs = Reg(), Reg(), Reg(), Reg()

        NWB = 3
        WB = [sb("wb%d" % i, [128, 16, 256], BF16) for i in range(NWB)]
        WR = [Reg() for _ in range(NWB)]
        WC = [kb.chan() for _ in range(NWB)]
        wstate = dict(n=0)

        def w_load(col0, ncols):
            i = wstate["n"] % NWB
            wstate["n"] += 1
            src = w_in[:, 16 * col0:16 * (col0 + ncols)].rearrange("p (k c) -> p k c", c=ncols)
            for kk in range(0, 16, 8):
                dma("pool", WB[i][:, kk:kk + 8, 0:ncols], src[:, kk:kk + 8, :], WC[i], writes=[WR[i]])
            return i

        WSEQ = [(C_DT, 32)]
        for g_ in range(4):
            WSEQ += [(C_B + g_ * 128, 128), (C_XS + g_ * 512, 256), (C_XS + g_ * 512 + 256, 256)]
        for sb_ in range(2):
            WSEQ.append((C_DT, 32))
            for g_ in range(4):
                WSEQ += [(C_B + g_ * 128, 128), (C_C + g_ * 128, 128), (C_XS + g_ * 512, 256), (C_XS + g_ * 512 + 256, 256),
                         (C_ZS + g_ * 512, 256), (C_ZS + g_ * 512 + 256, 256)]
        for jq_ in range(8):
            WSEQ += [(C_CC + jq_ * 256, 256), (C_VC + jq_ * 256, 256), (C_BC + jq_ * 256, 256), (C_ZC + jq_ * 256, 256)]
        wq = dict(issued=0, used=0, slots={})
        PFW = 2

        def w_acquire(c0, ncl):
            k = wq["used"]
            assert WSEQ[k] == (c0, ncl), (k, WSEQ[k], c0, ncl)
            while wq["issued"] < min(len(WSEQ), k + 1 + PFW):
                q = wq["issued"]
                wq["slots"][q] = w_load(*WSEQ[q])
                wq["issued"] += 1
            wq["used"] += 1
            return wq["slots"].pop(k)

        def run_jobs(jobs, PF=2):
            slots = {}
            for j in range(min(PF, len(jobs))):
                slots[j] = w_load(jobs[j][0], jobs[j][1])
            for j, (c0, ncl, fn) in enumerate(jobs):
                if j + PF < len(jobs):
                    slots[j + PF] = w_load(jobs[j + PF][0], jobs[j + PF][1])
                fn(slots.pop(j))

        bank_rr = dict(n=0)

        def next_bank(pool):
            b = pool[bank_rr["n"] % len(pool)]
            bank_rr["n"] += 1
            return b

        xc = [kb.chan() for _ in range(3)]
        hfC = kb.chan()
        ysC = kb.chan()
        hnCh = [kb.chan(), kb.chan()]
        ystC = [kb.chan(), kb.chan()]
        R_ysd = Reg()

        def phase0(tiles):
            with ExitStack() as p0:
                xb_ = [sb("xt%d" % i, [128, DM], F32, p0) for i in range(3)]
                xr = [Reg() for _ in range(3)]
                junk = sb("junk0", [128, DM], BF16, p0)
                xn = [sb("xn%d" % i, [128, DM], BF16, p0) for i in range(2)]
                xnr = [Reg(), Reg()]
                ssq = sb("ssq0", [128, 32], F32, p0)
                R_ss, R_junk = Reg(), Reg()
                op("dve", lambda e: e.memset(ssq[:], 0.0), writes=[R_ss])
                for i, (r0, nr, dst, dreg, c0) in enumerate(tiles):
                    xt, xreg = xb_[i % 3], xr[i % 3]
                    dma("sp", xt[0:nr, :], xall[r0:r0 + nr, :], xc[i % 3], writes=[xreg])
                    op("act", lambda e: e.activation(junk[0:nr, :], xt[0:nr, :], AF.Square,
                                                     accum_out=ssq[0:nr, 3 * i:3 * i + 1]),
                       reads=[xreg], writes=[R_junk, R_ss])
                    op("act", lambda e: e.activation(ssq[0:nr, 3 * i + 1:3 * i + 2], ssq[0:nr, 3 * i:3 * i + 1], AF.Ln,
                                                     scale=1.0 / DM, bias=EPS), reads=[R_ss], writes=[R_ss])
                    op("act", lambda e: e.activation(ssq[0:nr, 3 * i + 2:3 * i + 3], ssq[0:nr, 3 * i + 1:3 * i + 2], AF.Exp,
                                                     scale=-0.5), reads=[R_ss], writes=[R_ss])
                    xnt, xnreg = xn[i % 2], xnr[i % 2]
                    op("dve", lambda e: e.tensor_scalar(xnt[0:nr, :], xt[0:nr, :], ssq[0:nr, 3 * i + 2:3 * i + 3], None, ALU.mult),
                       reads=[xreg, R_ss], writes=[xnreg])
                    for hf in range(2):
                        bk = (2 * i + hf) % 4
                        pv = psb(bk)
                        for k8 in range(8):
                            k = hf * 8 + k8
                            op("pe", lambda e: e.transpose(pv[:, k8 * 128:k8 * 128 + nr], xnt[0:nr, k * 128:(k + 1) * 128],
                                                           identb[0:nr, 0:nr]),
                               reads=[xnreg, R_c], writes=[PR[bk]], inc=(k8 == 7))
                        src = pv.rearrange("p (a b) -> p a b", b=128)[:, :, 0:nr]
                        nwv = cst[:, K_NW + hf * 8:K_NW + hf * 8 + 8].unsqueeze(2).broadcast_to([128, 8, nr])
                        op("dve", lambda e: e.tensor_tensor(dst[:, hf * 8:hf * 8 + 8, c0:c0 + nr], src, nwv, ALU.mult),
                           reads=[PR[bk], R_c], writes=[dreg])
            kb.barrier()

        def proj_fm(slot, cofs, blocks, pool, evac):
            for bi, (rhs, N, hreg) in enumerate(blocks):
                bk = next_bank(pool)
                for k in range(16):
                    op("pe", lambda e: e.matmul(PS[bk][:, 0:N], WB[slot][:, k, cofs:cofs + 128], rhs(k),
                                                start=(k == 0), stop=(k == 15)),
                       reads=[WR[slot], hreg], writes=[PR[bk]], inc=(k == 15))
                evac(bi, bk, N)

        def job_dt(src, hreg, ntl, dstt, dreg, tcol_=None):
            def fn(slot):
                bk = 6
                for t in range(ntl):
                    for k in range(16):
                        c_ = tcol_[t] if tcol_ else t * 128
                        op("pe", lambda e: e.matmul(PS[bk][:, t * 32:(t + 1) * 32], src[:, k, c_:c_ + 128],
                                                    WB[slot][:, k, 0:32], start=(k == 0), stop=(k == 15)),
                           reads=[WR[slot], hreg], writes=[PR[bk]], inc=(k == 15 and t == ntl - 1))
                dv = dstt[:, 0:ntl, :]
                pv = PS[bk][:, 0:ntl * 32].rearrange("p (t h) -> p t h", h=32)
                op("dve", lambda e: e.tensor_tensor(dv, pv, cst[:, None, K_DTB:K_DTB + 32].broadcast_to([128, ntl, 32]), ALU.add),
                   reads=[PR[bk], R_c], writes=[dreg])
                op("act", lambda e: e.activation(dv, dv, AF.Exp), reads=[dreg], writes=[dreg])
                op("act", lambda e: e.activation(dv, dv, AF.Ln, bias=1.0), reads=[dreg], writes=[dreg])
            return fn

        def stage123(B, tl, g, mode):
            hs = slice(g * 8, g * 8 + 8)
            ntl = len(tl)
            xtk, xdt, xdd, la, pex = B["xtk"], B["xdt"], B["xdd"], B["la"], B["pex"]
            for (ti, xsrc, xreg_, bsrc, breg_, dtap, trix, onx) in tl:
                bk = next_bank([0, 1])
                pv = psb(bk)
                for c in range(5):
                    srcap, sreg = (xsrc(c), xreg_) if c < 4 else (bsrc, breg_)
                    op("pe", lambda e: e.transpose(pv[:, c * 128:(c + 1) * 128], srcap, identb[:]),
                       reads=[sreg, R_c], writes=[PR[bk]], inc=(c == 4))
                op("dve", lambda e: e.tensor_copy(xtk[:, ti, :], pv[:, 0:640]), reads=[PR[bk]], writes=[B["R_xtk"]])
            for (ti, xsrc, xreg_, bsrc, breg_, dtap, trix, onx) in tl:
                op("dve", lambda e: e.tensor_tensor(la[:, ti, :], dtap, abc[:, hs], ALU.mult),
                   reads=[R_dtm, R_dtq, R_c], writes=[B["R_la"]])
            for n_, (ti, xsrc, xreg_, bsrc, breg_, dtap, trix, onx) in enumerate(tl):
                op("pe", lambda e: e.matmul(PS[2][:, ti * 16:ti * 16 + 8], trix, la[:, ti, :], start=True, stop=True),
                   reads=[B["R_la"], R_c], writes=[PR[2]], inc=False)
                op("pe", lambda e: e.matmul(PS[2][:, ti * 16 + 8:ti * 16 + 16], onx, la[:, ti, :], start=True, stop=True),
                   reads=[B["R_la"], R_c], writes=[PR[2]], inc=(n_ == ntl - 1))
            t0, t1 = tl[0][0], tl[-1][0] + 1
            pvv = PS[2][:, t0 * 16:t1 * 16].rearrange("p (t c) -> p t c", c=16)
            op("dve", lambda e: e.tensor_copy(pex[:, t0:t1, 0:16], pvv), reads=[PR[2]], writes=[B["R_pex"]])
            op("dve", lambda e: e.tensor_tensor(pex[:, t0:t1, 16:24], pex[:, t0:t1, 8:16], pex[:, t0:t1, 0:8], ALU.subtract),
               reads=[B["R_pex"]], writes=[B["R_pex"]])
            op("act", lambda e: e.activation(pex[:, t0:t1, :], pex[:, t0:t1, :], AF.Exp), reads=[B["R_pex"]], writes=[B["R_pex"]])
            for (ti, xsrc, xreg_, bsrc, breg_, dtap, trix, onx) in tl:
                xv = xtk[:, ti, 0:512].rearrange("p (h q) -> p h q", q=64)
                op("dve", lambda e: e.tensor_tensor(xdt[:, ti, :].rearrange("p (h q) -> p h q", q=64), xv,
                                                    dtap.unsqueeze(2).broadcast_to([128, 8, 64]), ALU.mult),
                   reads=[B["R_xtk"], R_dtm, R_dtq], writes=[B["R_xdt"]])
                op("pool", lambda e: e.tensor_tensor(xdd[:, ti, :].rearrange("p (h q) -> p h q", q=64),
                                                     xdt[:, ti, :].rearrange("p (h q) -> p h q", q=64),
                                                     pex[:, ti, 16:24].unsqueeze(2).broadcast_to([128, 8, 64]), ALU.mult),
                   reads=[B["R_xdt"], B["R_pex"]], writes=[B["R_xdd"]])

        def state_scan(B, tl, init):
            hcur, hbf, xtk, xdd, pex = B["hcur"], B["hbf"], B["xtk"], B["xdd"], B["pex"]
            init()
            for (ti, xsrc, xreg_, bsrc, breg_, dtap, trix, onx) in tl:
                op("act", lambda e: e.copy(hbf[:, ti, :], hcur[:]), reads=[B["R_hcur"]], writes=[B["R_hbf"]])
                bk = next_bank([3, 4])
                op("pe", lambda e: e.matmul(PS[bk][:, :], xtk[:, ti, 512:640], xdd[:, ti, :], start=True, stop=True),
                   reads=[B["R_xtk"], B["R_xdd"]], writes=[PR[bk]])
                hv = hcur[:].rearrange("p (h q) -> p h q", q=64)
                op("dve", lambda e: e.tensor_tensor(hv, hv, pex[:, ti, 8:16].unsqueeze(2).broadcast_to([128, 8, 64]), ALU.mult),
                   reads=[B["R_pex"]], writes=[B["R_hcur"]])
                op("dve", lambda e: e.tensor_tensor(hcur[:], hcur[:], PS[bk][:, :], ALU.add),
                   reads=[PR[bk]], writes=[B["R_hcur"]])

        with ExitStack() as pq:
            hTq = sb("hTq", [128, 16, NP_TOK], BF16, pq)
            R_hTq = Reg()
            phase0([(t * 128, 128, hTq, R_hTq, t * 128) for t in range(8)])
            xsTq = sb("xsTq", [128, 4, NP_TOK], BF16, pq)
            BTq = sb("BTq", [128, NP_TOK], BF16, pq)
            R_xsTq, R_BTq = Reg(), Reg()
            preq = [sb("preq%d" % i, [128, 4 + NP_TOK], F32, pq) for i in range(2)]
            accq = sb("accq", [128, NP_TOK], F32, pq)
            preqR, R_accq = [Reg(), Reg()], Reg()
            Bq = dict(xtk=sb("xtkq", [128, 8, 640], BF16, pq), xdt=sb("xdtq", [128, 8, 512], BF16, pq),
                      xdd=sb("xddq", [128, 8, 512], BF16, pq), la=sb("laq", [128, 8, 8], F32, pq),
                      pex=sb("pexq", [128, 8, 24], F32, pq), hcur=sb("hcurq", [128, 512], F32, pq),
                      hbf=sb("hbfq", [128, 8, 512], BF16, pq))
            for nm in ("xtk", "xdt", "xdd", "la", "pex", "hcur", "hbf"):
                Bq["R_" + nm] = Reg()
            cntq = dict(n=0)
            blocks_pre = [(lambda k: hTq[:, k, 0:512], 512, R_hTq), (lambda k: hTq[:, k, 512:1024], 512, R_hTq)]

            def xbc_pre(slot, cofs, j, dst, dreg):
                i = cntq["n"] % 2
                cntq["n"] += 1
                P_ = preq[i]
                wv = cst[:, K_CSW + 4 * j:K_CSW + 4 * j + 4]
                bv = cst[:, K_CSB + j:K_CSB + j + 1]
                op("pool", lambda e: e.memset(P_[:, 0:4], 0.0), writes=[preqR[i]])

                def evac(bi, bk, N):
                    op("act", lambda e: e.copy(P_[:, 4 + bi * 512:4 + bi * 512 + 512], PS[bk][:, 0:512]),
                       reads=[PR[bk]], writes=[preqR[i]])
                proj_fm(slot, cofs, blocks_pre, [3, 4, 5, 7], evac)
                op("dve", lambda e: e.tensor_scalar(accq[:], P_[:, 1:1 + NP_TOK], wv[:, 0:1], None, ALU.mult),
                   reads=[preqR[i], R_c], writes=[R_accq])
                for k in range(1, 4):
                    op("dve", lambda e: e.scalar_tensor_tensor(accq[:], P_[:, 1 + k:1 + k + NP_TOK], wv[:, k:k + 1], accq[:],
                                                               ALU.mult, ALU.add), reads=[preqR[i], R_c], writes=[R_accq])
                op("act", lambda e: e.activation(dst, accq[:], AF.Silu, bias=bv), reads=[R_accq, R_c], writes=[dreg])

            run_jobs([(C_DT, 32, job_dt(hTq, R_hTq, 8, dtq, R_dtq))])
            for g in range(4):
                def jB(slot, g=g):
                    xbc_pre(slot, 0, 16 + g, BTq[:], R_BTq)

                def jx(hf, g=g):
                    def fn(slot):
                        for c2 in range(2):
                            c = hf * 2 + c2
                            xbc_pre(slot, c2 * 128, g * 4 + c, xsTq[:, c, :], R_xsTq)
                    return fn
                run_jobs([(C_B + g * 128, 128, jB), (C_XS + g * 512, 256, jx(0)), (C_XS + g * 512 + 256, 256, jx(1))])
                tlq = [(t, (lambda c, t=t: xsTq[:, c, t * 128:(t + 1) * 128]), R_xsTq, BTq[:, t * 128:(t + 1) * 128], R_BTq,
                        dtq[:, t, g * 8:g * 8 + 8], TRI, ones[:]) for t in range(8)]
                stage123(Bq, tlq, g, "prefix")
                state_scan(Bq, tlq, lambda: op("dve", lambda e: e.memset(Bq["hcur"][:], 0.0), writes=[Bq["R_hcur"]]))
                op("dve", lambda e: e.tensor_scalar(hmid[:, g, :], Bq["hcur"][:], cst[:, K_FLAG:K_FLAG + 1], None, ALU.mult),
                   reads=[Bq["R_hcur"], R_c], writes=[R_hmid])

        kb.barrier()
        hT = sb("hT", [128, 16, NT], BF16)
        hTh = sb("hTh", [128, 16, 4], BF16)
        R_hT, R_hTh = Reg(), Reg()
        tiles0 = [(1024 + t * 128, 128, hT, R_hT, t * 128) for t in range(9)]
        tiles0.append((2176, 4, hTh, R_hTh, 0))
        phase0(tiles0)
        for sbk in range(2):
            NTL = 512 if sbk == 0 else 640
            ntile = 4 if sbk == 0 else 5
            goff = sbk * 512
            tb = sbk * 512
            tcol = [tb + t * 128 for t in range(4)] + [1024]
            blocks_main = [(lambda k, tb=tb: hT[:, k, tb:tb + 512], 512, R_hT)]
            if sbk == 1:
                blocks_main.append((lambda k: hT[:, k, 1024:1152], 128, R_hT))
                blocks_main.append((lambda k: hT[:, k, 508:512], 4, R_hT))
            else:
                blocks_main.append((lambda k: hTh[:, k, 0:4], 4, R_hTh))
            HALO = len(blocks_main) - 1
            with ExitStack() as pa:
                BT = sb("BT", [128, 640], BF16, pa)
                CT = sb("CT", [128, 640], BF16, pa)
                xsT = sb("xsT", [128, 4, 640], BF16, pa)
                szs = sb("szs", [128, 5, 512], BF16, pa)
                R_BT, R_CT, R_xsT, R_szs = Reg(), Reg(), Reg(), Reg()
                pre = [sb("pre%d" % i, [128, 516], F32, pa) for i in range(2)]
                pres = [sb("pres%d" % i, [128, 16, 11], F32, pa) for i in range(2)]
                acc = sb("acc", [128, 512], F32, pa)
                accs = sb("accs", [128, 16, 8], F32, pa)
                preR, R_acc = [Reg(), Reg()], Reg()
                stcs = sb("stcs", [128, 1536], F32, pa)
                R_stcs = Reg()
                if sbk == 1:
                    dma("sp", stcs[0:48, :], st_cs[:, 0:1536], cch, writes=[R_stcs])
                    dma("sp", stcs[64:112, :], st_cs[:, 1536:3072], cch, writes=[R_stcs])
                Bm_ = dict(xtk=sb("xtk", [128, 5, 640], BF16, pa), xdt=sb("xdt", [128, 5, 512], BF16, pa),
                           xdd=sb("xdd", [128, 5, 512], BF16, pa), la=sb("la", [128, 5, 8], F32, pa),
                           pex=sb("pex", [128, 5, 24], F32, pa), hcur=sb("hcur", [128, 512], F32, pa),
                           hbf=sb("hbf", [128, 4, 512], BF16, pa))
                for nm in ("xtk", "xdt", "xdd", "la", "pex", "hcur", "hbf"):
                    Bm_["R_" + nm] = Reg()
                xtk, xdt, xdd, la, pex, hcur, hbf = (Bm_[n_] for n_ in ("xtk", "xdt", "xdd", "la", "pex", "hcur", "hbf"))
                cbm = sb("cbm", [128, 5, 128], F32, pa)
                R_cbm = Reg()
                Lb2 = [sb("Lb%d" % i, [128, 2, 8, 128], BF16, pa) for i in range(2)]
                LbR = [Reg(), Reg()]
                Eb2 = [sb("Eb2_%d" % i, [128, 1024], BF16, pa) for i in range(2)]
                EbR = [Reg(), Reg()]
                lahi = sb("lahi", [128, 5, 8], BF16, pa)
                lalo = sb("lalo", [128, 5, 8], BF16, pa)
                latmp = sb("latmp", [128, 5, 8], F32, pa)
                R_lahl = Reg()
                DI = sb("DI", [128, 8, 128], BF16, pa)
                R_DI = Reg()
                MT = sb("MT", [128, 5, 1024], BF16, pa)
                R_MT = Reg()
                ytm = [sb("ytm%d" % i, [128, 512], F32, pa) for i in range(2)]
                ytR = [Reg(), Reg()]
                ynb = [sb("ynb%d" % i, [128, 512], BF16, pa) for i in range(2)]
                ynR = [Reg(), Reg()]
                yst = [sb("yst%d" % i, [128, 4, 128], BF16, pa) for i in range(2)]
                ystR = [Reg(), Reg()]
                tmp = sb("tmp", [128, 512], F32, pa)
                R_tmp = Reg()
                sst = sb("sst", [128, 5, 4], F32, pa)
                R_sst = Reg()
                junk2 = sb("junk2", [128, 512], BF16, pa)
                R_j2 = Reg()
                if sbk == 1:
                    h0 = [sb("h0_%d" % i, [128, 4, 128], F32, pa) for i in range(3)]
                    h0R = [Reg() for _ in range(3)]
                    h0C = [xc[0], xc[1], xc[2]]
                    h0b = [sb("h0b%d" % i, [128, 512], BF16, pa) for i in range(2)]
                    h0bR = [Reg(), Reg()]
                    h0T = [sb("h0T%d" % i, [128, 512], BF16, pa) for i in range(2)]
                    h0TR = [Reg(), Reg()]
                    hn = [sb("hn%d" % i, [128, 4, 128], F32, pa) for i in range(2)]
                    hnR = [Reg(), Reg()]
                    hnC = hnCh
                    CTm = sb("CTm", [128, 16, 128], BF16, pa)
                    Bmk = sb("Bmk", [128, 16, 128], BF16, pa)
                    larep = sb("larep", [128, 8, 64], F32, pa)
                    cdT = sb("cdT", [128, 4, 16], F32, pa)
                    R_CTm, R_Bmk, R_larep, R_cdT = Reg(), Reg(), Reg(), Reg()
                cnt = dict(n=0)

                def xbc_chunk(slot, cofs, j, dstP, dstS, dreg):
                    i = cnt["n"] % 2
                    cnt["n"] += 1
                    P_, PSm = pre[i], pres[i]
                    wv = cst[:, K_CSW + 4 * j:K_CSW + 4 * j + 4]
                    bv = cst[:, K_CSB + j:K_CSB + j + 1]
                    if sbk == 1:
                        hj, jj = j // 12, j % 12
                        bk = next_bank([6, 7])
                        op("pe", lambda e: e.matmul(PS[bk][:, 0:48], stcs[64 * hj:64 * hj + 48, jj * 128:(jj + 1) * 128],
                                                    IDF[64 * hj:64 * hj + 48, 64 * hj:64 * hj + 48], start=True, stop=True),
                           reads=[R_stcs, R_c], writes=[PR[bk]])
                        op("act", lambda e: e.copy(PSm[:, :, 0:3], PS[bk][:, 0:48].rearrange("p (b t) -> p b t", t=3)),
                           reads=[PR[bk]], writes=[preR[i]])

                    def evac(bi, bk, N):
                        if bi == HALO:
                            op("act", lambda e: e.copy(P_[:, 0:4], PS[bk][:, 0:4]), reads=[PR[bk]], writes=[preR[i]])
                        elif bi == 0:
                            op("act", lambda e: e.copy(P_[:, 4:516], PS[bk][:, 0:512]), reads=[PR[bk]], writes=[preR[i]])
                        else:
                            op("act", lambda e: e.copy(PSm[:, :, 3:11], PS[bk][:, 0:128].rearrange("p (b t) -> p b t", t=8)),
                               reads=[PR[bk]], writes=[preR[i]])
                    proj_fm(slot, cofs, blocks_main, [3, 4, 5], evac)
                    if sbk == 1:
                        op("pool", lambda e: e.tensor_copy(csT[:, j, :], P_[:, 512:516]), reads=[preR[i]], writes=[R_cs])
                        op("pool", lambda e: e.tensor_copy(csTs[:, j, :].rearrange("p (b t) -> p b t", t=3), PSm[:, :, 8:11]),
                           reads=[preR[i]], writes=[R_css])
                    op("dve", lambda e: e.tensor_scalar(acc[:], P_[:, 1:513], wv[:, 0:1], None, ALU.mult),
                       reads=[preR[i], R_c], writes=[R_acc])
                    for k in range(1, 4):
                        op("dve", lambda e: e.scalar_tensor_tensor(acc[:], P_[:, 1 + k:513 + k], wv[:, k:k + 1], acc[:],
                                                                   ALU.mult, ALU.add), reads=[preR[i], R_c], writes=[R_acc])
                    op("act", lambda e: e.activation(dstP, acc[:], AF.Silu, bias=bv), reads=[R_acc, R_c], writes=[dreg])
                    if sbk == 1:
                        op("dve", lambda e: e.tensor_scalar(accs[:], PSm[:, :, 0:8], wv[:, 0:1], None, ALU.mult),
                           reads=[preR[i], R_c], writes=[R_acc])
                        for k in range(1, 4):
                            op("dve", lambda e: e.scalar_tensor_tensor(accs[:], PSm[:, :, k:k + 8], wv[:, k:k + 1], accs[:],
                                                                        ALU.mult, ALU.add), reads=[preR[i], R_c, R_acc], writes=[R_acc])
                        op("act", lambda e: e.activation(dstS, accs[:].rearrange("p b t -> p (b t)"), AF.Silu, bias=bv),
                           reads=[R_acc, R_c], writes=[dreg])

                run_jobs([(C_DT, 32, job_dt(hT, R_hT, ntile, dtm, R_dtm, tcol))])

                for g in range(4):
                    hs = slice(g * 8, g * 8 + 8)

                    def jB(slot, g=g):
                        xbc_chunk(slot, 0, 16 + g, BT[:, 0:512], BT[:, 512:640], R_BT)

                    def jC(slot, g=g):
                        xbc_chunk(slot, 0, 20 + g, CT[:, 0:512], CT[:, 512:640], R_CT)

                    def jx(hf, g=g):
                        def fn(slot):
                            for c2 in range(2):
                                c = hf * 2 + c2
                                xbc_chunk(slot, c2 * 128, g * 4 + c, xsT[:, c, 0:512], xsT[:, c, 512:640], R_xsT)
                        return fn

                    def jz(hf, g=g):
                        def fn(slot):
                            for t in range(ntile):
                                bk = next_bank([3, 4, 5])
                                for k in range(16):
                                    op("pe", lambda e: e.matmul(PS[bk][:, 0:256], hT[:, k, tcol[t]:tcol[t] + 128], WB[slot][:, k, 0:256],
                                                                start=(k == 0), stop=(k == 15)),
                                       reads=[WR[slot], R_hT], writes=[PR[bk]], inc=(k == 15))
                                op("act", lambda e: e.activation(szs[:, t, hf * 256:(hf + 1) * 256], PS[bk][:, 0:256], AF.Silu),
                                   reads=[PR[bk]], writes=[R_szs])
                        return fn
                    run_jobs([(C_B + g * 128, 128, jB), (C_C + g * 128, 128, jC),
                              (C_XS + g * 512, 256, jx(0)), (C_XS + g * 512 + 256, 256, jx(1)),
                              (C_ZS + g * 512, 256, jz(0)), (C_ZS + g * 512 + 256, 256, jz(1))])

                    tlm = [(t, (lambda c, t=t: xsT[:, c, t * 128:(t + 1) * 128]), R_xsT, BT[:, t * 128:(t + 1) * 128], R_BT,
                            dtm[:, t, hs], (TRI if t < 4 else TRIB), (ones[:] if t < 4 else ONEB)) for t in range(ntile)]
                    stage123(Bm_, tlm, g, "main")
                    state_scan(Bm_, tlm[:4], lambda: op("dve", lambda e: e.tensor_copy(hcur[:], hmid[:, g, :]),
                                                        reads=[R_hmid], writes=[Bm_["R_hcur"]]))
                    op("dve", lambda e: e.tensor_copy(hmid[:, g, :], hcur[:]), reads=[Bm_["R_hcur"]], writes=[R_hmid])
                    if sbk == 1:
                        bk = next_bank([5])
                        for jj in range(4):
                            op("pe", lambda e: e.matmul(PS[bk][:, jj * 128:(jj + 1) * 128], hcur[:, jj * 128:(jj + 1) * 128], IDF,
                                                        start=True, stop=True), reads=[Bm_["R_hcur"], R_c], writes=[PR[bk]], inc=(jj == 3))
                        op("act", lambda e: e.copy(tmp[:], PS[bk][:, :]), reads=[PR[bk]], writes=[R_tmp])
                        dma("sp", ssm_p_o[g * 512:(g + 1) * 512, :].rearrange("(j p) n -> p j n", p=128),
                            tmp[:].rearrange("p (j n) -> p j n", n=128), ysC, reads=[R_tmp])

                    op("dve", lambda e: e.tensor_copy(lahi[:, 0:ntile, :], la[:, 0:ntile, :]), reads=[Bm_["R_la"]], writes=[R_lahl])
                    op("dve", lambda e: e.tensor_tensor(latmp[:, 0:ntile, :], la[:, 0:ntile, :], lahi[:, 0:ntile, :], ALU.subtract),
                       reads=[Bm_["R_la"], R_lahl], writes=[R_lahl])
                    op("dve", lambda e: e.tensor_copy(lalo[:, 0:ntile, :], latmp[:, 0:ntile, :]), reads=[R_lahl], writes=[R_lahl])
                    op("dve", lambda e: e.tensor_tensor(DI[:], identb[:, None, :].broadcast_to([128, 8, 128]),
                                                        cst[:, K_DSK + g * 8:K_DSK + g * 8 + 8].unsqueeze(2).broadcast_to([128, 8, 128]),
                                                        ALU.mult), reads=[R_c, R_c2], writes=[R_DI])
                    op("dve", lambda e: e.memset(sst[:], 0.0), writes=[R_sst])
                    def a4_gen():
                        for (ti, xsrc, xreg_, bsrc, breg_, dtap, trix, onx) in tlm:
                            bk = next_bank([0, 1])
                            op("pe", lambda e: e.matmul(PS[bk][:, 0:128], BT[:, ti * 128:(ti + 1) * 128],
                                                        CT[:, ti * 128:(ti + 1) * 128], start=True, stop=True),
                               reads=[R_BT, R_CT], writes=[PR[bk]])
                            op("dve", lambda e: e.tensor_tensor(cbm[:, ti, :], PS[bk][:, 0:128], trix, ALU.mult),
                               reads=[PR[bk], R_c], writes=[R_cbm])
                        def emit_L(n_):
                            (ti, xsrc, xreg_, bsrc, breg_, dtap, trix, onx) = tlm[n_]
                            L_ = Lb2[n_ % 2]
                            for q_, lsrc in enumerate((lahi, lalo)):
                                op("dve", lambda e: e.tensor_tensor(L_[:, q_, :, :], trix[:, None, :].broadcast_to([128, 8, 128]),
                                                                    lsrc[:, ti, :].unsqueeze(2).broadcast_to([128, 8, 128]), ALU.mult),
                                   reads=[R_lahl, R_c], writes=[LbR[n_ % 2]])
                        emit_L(0)
                        for n_, (ti, xsrc, xreg_, bsrc, breg_, dtap, trix, onx) in enumerate(tlm):
                            if n_ + 1 < ntile:
                                emit_L(n_ + 1)
                            L_, E_ = Lb2[n_ % 2], Eb2[n_ % 2]
                            bks = (4, 5) if n_ % 2 == 0 else (6, 3)
                            for hh in range(2):
                                bk = bks[hh]
                                for q_ in range(2):
                                    op("pe", lambda e: e.matmul(PS[bk][:, :], MSb[:], L_[:, q_, hh * 4:hh * 4 + 4, :].rearrange("p h l -> p (h l)"),
                                                                start=(q_ == 0), stop=(q_ == 1)), reads=[LbR[n_ % 2], R_c2], writes=[PR[bk]], inc=(q_ == 1))
                            for hh in range(2):
                                op("act", lambda e: e.activation(E_[:, hh * 512:(hh + 1) * 512], PS[bks[hh]][:, :], AF.Exp),
                                   reads=[PR[bks[hh]]], writes=[EbR[n_ % 2]])
                            op("dve", lambda e: e.tensor_tensor(MT[:, ti, :].rearrange("p (h l) -> p h l", l=128),
                                                                E_[:].rearrange("p (h l) -> p h l", l=128),
                                                                cbm[:, ti, None, :].broadcast_to([128, 8, 128]), ALU.mult),
                               reads=[EbR[n_ % 2], R_cbm], writes=[R_MT])
                            yield


                    def a6_gen():
                        def emit_mm(n_):
                            ti = tlm[n_][0]
                            i2 = n_ % 2
                            bd = 5 if i2 == 0 else 6
                            bo = (3 if i2 == 0 else 4) if ti < 4 else 7
                            for h in range(8):
                                op("pe", lambda e: e.matmul(PS[bd][:, h * 64:(h + 1) * 64], MT[:, ti, h * 128:(h + 1) * 128],
                                                            xdt[:, ti, h * 64:(h + 1) * 64], start=True, stop=False),
                                   reads=[R_MT, Bm_["R_xdt"]], writes=[PR[bd]], inc=False)
                                op("pe", lambda e: e.matmul(PS[bd][:, h * 64:(h + 1) * 64], DI[:, h, :],
                                                            xtk[:, ti, h * 64:(h + 1) * 64], start=False, stop=True),
                                   reads=[R_DI, Bm_["R_xtk"]], writes=[PR[bd]], inc=(h == 7))
                            if ti < 4:
                                op("pe", lambda e: e.matmul(PS[bo][:, :], CT[:, ti * 128:(ti + 1) * 128], hbf[:, ti, :],
                                                            start=True, stop=True), reads=[R_CT, Bm_["R_hbf"]], writes=[PR[bo]])
                        emit_mm(0)
                        for n_, (ti, xsrc, xreg_, bsrc, breg_, dtap, trix, onx) in enumerate(tlm):
                            i2 = n_ % 2
                            bd = 5 if i2 == 0 else 6
                            bo = (3 if i2 == 0 else 4) if ti < 4 else 7
                            if n_ + 1 < ntile:
                                emit_mm(n_ + 1)
                            Y = ytm[i2]
                            Yv = Y[:].rearrange("p (h q) -> p h q", q=64)
                            op("dve", lambda e: e.tensor_tensor(Yv, PS[bo][:, :].rearrange("p (h q) -> p h q", q=64),
                                                                pex[:, ti, 0:8].unsqueeze(2).broadcast_to([128, 8, 64]), ALU.mult),
                               reads=[PR[bo], Bm_["R_pex"]], writes=[ytR[i2]])
                            op("dve", lambda e: e.tensor_tensor(Y[:], Y[:], PS[bd][:, :], ALU.add), reads=[PR[bd]], writes=[ytR[i2]])
                            op("dve", lambda e: e.tensor_tensor(Y[:], Y[:], szs[:, ti, :], ALU.mult), reads=[R_szs], writes=[ytR[i2]])
                            op("act", lambda e: e.activation(junk2[:], Y[:], AF.Square, accum_out=sst[:, ti, 0:1]),
                               reads=[ytR[i2]], writes=[R_j2, R_sst])
                            op("act", lambda e: e.activation(sst[:, ti, 1:2], sst[:, ti, 0:1], AF.Ln, scale=1.0 / 512, bias=EPS),
                               reads=[R_sst], writes=[R_sst])
                            op("act", lambda e: e.activation(sst[:, ti, 2:3], sst[:, ti, 1:2], AF.Exp, scale=-0.5),
                               reads=[R_sst], writes=[R_sst])
                            op("act", lambda e: e.activation(ynb[i2][:], Y[:], AF.Copy, scale=sst[:, ti, 2:3]),
                               reads=[ytR[i2], R_sst], writes=[ynR[i2]])
                            bk = next_bank([0, 1])
                            pv = psb(bk)
                            for c in range(4):
                                op("pe", lambda e: e.transpose(pv[:, c * 128:(c + 1) * 128], ynb[i2][:, c * 128:(c + 1) * 128], identb[:]),
                                   reads=[ynR[i2], R_c], writes=[PR[bk]], inc=(c == 3))
                            op("dve", lambda e: e.tensor_tensor(yst[i2][:], pv[:, 0:512].rearrange("p (c t) -> p c t", t=128),
                                                                cst[:, K_SNW + g * 4:K_SNW + g * 4 + 4].unsqueeze(2).broadcast_to([128, 4, 128]),
                                                                ALU.mult), reads=[PR[bk], R_c], writes=[ystR[i2]])
                            dma("sp", ysd[g * 4:g * 4 + 4, :, goff + ti * 128:goff + (ti + 1) * 128].rearrange("c p t -> p c t"),
                                yst[i2][:], ystC[i2], reads=[ystR[i2]])
                            yield


                    a4 = a4_gen()
                    a6 = a6_gen()
                    if sbk == 0:
                        for _ in a4:
                            pass
                    if sbk == 1:
                        op("pool", lambda e: e.tensor_tensor(CTm[:], CT[:, None, 512:640].broadcast_to([128, 16, 128]),
                                                            selm[:].rearrange("p (b l) -> p b l", l=128), ALU.mult),
                           reads=[R_CT, R_c, R_c2], writes=[R_CTm])
                        op("pool", lambda e: e.tensor_tensor(Bmk[:], xtk[:, 4, None, 512:640].broadcast_to([128, 16, 128]),
                                                            BLK.unsqueeze(2).broadcast_to([128, 16, 128]), ALU.mult),
                           reads=[Bm_["R_xtk"], R_c], writes=[R_Bmk])
                        op("pool", lambda e: e.tensor_copy(larep[:], la[:, 4, :].unsqueeze(2).broadcast_to([128, 8, 64])),
                           reads=[Bm_["R_la"]], writes=[R_larep])
                        for jj in range(4):
                            op("pe", lambda e: e.matmul(PS[2][:, jj * 16:(jj + 1) * 16],
                                                        larep[:].rearrange("p h q -> p (h q)")[:, jj * 128:(jj + 1) * 128], BLK,
                                                        start=True, stop=True), reads=[R_larep, R_c], writes=[PR[2]], inc=(jj == 3))
                        op("act", lambda e: e.activation(cdT[:].rearrange("p j b -> p (j b)"), PS[2][:, 0:64], AF.Exp),
                           reads=[PR[2]], writes=[R_cdT])

                        def ld(b_):
                            i3_ = b_ % 3
                            dma("sp", h0[i3_][:], st_ssm[b_, g * 512:(g + 1) * 512, :].rearrange("(j p) n -> p j n", p=128),
                                h0C[i3_], writes=[h0R[i3_]])
                        def cast(b_):
                            op("act", lambda e: e.copy(h0b[b_ % 2][:], h0[b_ % 3][:].rearrange("p j n -> p (j n)")),
                               reads=[h0R[b_ % 3]], writes=[h0bR[b_ % 2]])
                        ld(0)
                        ld(1)
                        cast(0)
                        for b in range(16):
                            i2, i3 = b % 2, b % 3
                            if b + 2 < 16:
                                ld(b + 2)
                            if b + 1 < 16:
                                cast(b + 1)
                            bk = 0
                            pv = psb(bk)
                            for jj in range(4):
                                op("pe", lambda e: e.transpose(pv[:, jj * 128:(jj + 1) * 128], h0b[i2][:, jj * 128:(jj + 1) * 128], identb[:]),
                                   reads=[h0bR[i2], R_c2], writes=[PR[bk]], inc=(jj == 3))
                            op("act", lambda e: e.copy(h0T[i2][:], pv[:, 0:512]), reads=[PR[bk]], writes=[h0TR[i2]])
                            op("pe", lambda e: e.matmul(PS[7][:, :], CTm[:, b, :], h0T[i2][:], start=(b == 0), stop=(b == 15)),
                               reads=[R_CTm, h0TR[i2]], writes=[PR[7]], inc=True)
                            bk2 = 1
                            for jj in range(4):
                                op("pe", lambda e: e.matmul(PS[bk2][:, jj * 128:(jj + 1) * 128], xdd[:, 4, jj * 128:(jj + 1) * 128],
                                                            Bmk[:, b, :], start=True, stop=True),
                                   reads=[Bm_["R_xdd"], R_Bmk], writes=[PR[bk2]], inc=(jj == 3))
                            for jj in range(4):
                                op("dve", lambda e: e.scalar_tensor_tensor(hn[i2][:, jj, :], h0[i3][:, jj, :], cdT[:, jj, b:b + 1],
                                                                           PS[bk2][:, jj * 128:(jj + 1) * 128], ALU.mult, ALU.add),
                                   reads=[h0R[i3], R_cdT, PR[bk2]], writes=[hnR[i2]])
                            dma("sp", ssm_s_o[b, g * 512:(g + 1) * 512, :].rearrange("(j p) n -> p j n", p=128), hn[i2][:],
                                hnC[i2], reads=[hnR[i2]])
                            if b < 5:
                                next(a4, None)
                            elif b >= 6 and b % 2 == 0 and b <= 12:
                                next(a6, None)

                    for _ in a4:
                        pass
                    for _ in a6:
                        pass

                if sbk == 1:
                    for q in range(6):
                        bk = next_bank([0, 1, 2])
                        for c in range(4):
                            j = q * 4 + c
                            op("pe", lambda e: e.matmul(PS[bk][0:48, c * 128:(c + 1) * 128], csTs[:, j, :], IDF, start=True, stop=True),
                               reads=[R_css, R_c], writes=[PR[bk]], inc=(c == 3))
                        op("act", lambda e: e.copy(ytm[0][0:48, :], PS[bk][0:48, :]), reads=[PR[bk]], writes=[ytR[0]])
                        dma("sp", cs_s_o[:, q * 512:(q + 1) * 512], ytm[0][0:48, :], ysC, reads=[ytR[0]])
                        bk = next_bank([0, 1, 2])
                        for c in range(4):
                            j = q * 4 + c
                            op("pe", lambda e: e.matmul(PS[bk][0:4, c * 128:(c + 1) * 128], csT[:, j, :], IDF, start=True, stop=True),
                               reads=[R_cs, R_c], writes=[PR[bk]], inc=(c == 3))
                        op("act", lambda e: e.copy(ytm[1][0:4, :], PS[bk][0:4, :]), reads=[PR[bk]], writes=[ytR[1]])
                        dma("sp", cs_p_o[:, q * 512:(q + 1) * 512], ytm[1][0:4, :], ysC, reads=[ytR[1]])

            kb.barrier()
        kb.barrier()
        blocks_all = [(lambda k: hT[:, k, 0:512], 512, R_hT), (lambda k: hT[:, k, 512:1024], 512, R_hT),
                      (lambda k: hT[:, k, 1024:1152], 128, R_hT), (lambda k: hTh[:, k, 0:4], 4, R_hTh)]
        with ExitStack() as pb:
            cvp = sb("cvp", [128, 2, 4 + NP_TOK], F32, pb)
            cvs = sb("cvs", [128, 2, 16, 10], F32, pb)
            co = sb("co", [128, 2, NT], F32, pb)
            szb = [sb("szb%d" % i, [128, 512], BF16, pb) for i in range(2)]
            szR = [Reg(), Reg()]
            ycs = [sb("ycs%d" % i, [128, 512], BF16, pb) for i in range(2)]
            ycR = [Reg(), Reg()]
            R_cvp, R_co = Reg(), Reg()
            stch = sb("stch", [32, 2048], F32, pb)
            R_stch = Reg()
            dma("sp", stch[:], st_csh, cch, writes=[R_stch])
            rr = dict(n=0)
            BK6 = [0, 1, 2, 3, 4, 5]
            for jq in range(8):
                def job_c(slot, jq=jq):
                    for c in range(2):
                        j = jq * 2 + c
                        bk = next_bank([6, 7])
                        op("pe", lambda e: e.matmul(PS[bk][:, 0:32], stch[0:32, j * 128:(j + 1) * 128], IDF[0:32, 0:32],
                                                    start=True, stop=True), reads=[R_stch, R_c], writes=[PR[bk]])
                        op("act", lambda e: e.copy(cvs[:, c, :, 0:2], PS[bk][:, 0:32].rearrange("p (b t) -> p b t", t=2)),
                           reads=[PR[bk]], writes=[R_cvp])

                        def evac(bi, bk, N, c=c):
                            if bi == 3:
                                op("act", lambda e: e.copy(cvp[:, c, 0:4], PS[bk][:, 0:4]), reads=[PR[bk]], writes=[R_cvp])
                            elif bi < 2:
                                op("act", lambda e: e.copy(cvp[:, c, 4 + bi * 512:516 + bi * 512], PS[bk][:, 0:512]),
                                   reads=[PR[bk]], writes=[R_cvp])
                            else:
                                op("act", lambda e: e.copy(cvs[:, c, :, 2:10], PS[bk][:, 0:128].rearrange("p (b t) -> p b t", t=8)),
                                   reads=[PR[bk]], writes=[R_cvp])
                        proj_fm(slot, c * 128, blocks_all, BK6, evac)

                def job_v(slot, jq=jq):
                    for c in range(2):
                        j = jq * 2 + c
                        wv = cst[:, K_CHW + 3 * j:K_CHW + 3 * j + 3]

                        def evac(bi, bk, N, c=c):
                            if bi == 3:
                                d = cvp[:, c, 0:4]
                                op("dve", lambda e: e.tensor_tensor(d, d, PS[bk][:, 0:4], ALU.mult), reads=[PR[bk]], writes=[R_cvp])
                            elif bi < 2:
                                d = cvp[:, c, 4 + bi * 512:516 + bi * 512]
                                op("dve", lambda e: e.tensor_tensor(d, d, PS[bk][:, 0:512], ALU.mult), reads=[PR[bk]], writes=[R_cvp])
                            else:
                                d = cvs[:, c, :, 2:10]
                                op("dve", lambda e: e.tensor_tensor(d, d, PS[bk][:, 0:128].rearrange("p (b t) -> p b t", t=8), ALU.mult),
                                   reads=[PR[bk]], writes=[R_cvp])
                        proj_fm(slot, c * 128, blocks_all, BK6, evac)
                        op("pool", lambda e: e.tensor_copy(chT[:, j, :], cvp[:, c, NP_TOK:NP_TOK + 4]), reads=[R_cvp], writes=[R_ch])
                        op("pool", lambda e: e.tensor_copy(chTs[:, j, :].rearrange("p (b t) -> p b t", t=2), cvs[:, c, :, 8:10]),
                           reads=[R_cvp], writes=[R_chs])
                        for hb in range(2):
                            cp = co[:, c, hb * 512:(hb + 1) * 512]
                            o_ = 2 + hb * 512
                            op("dve", lambda e: e.tensor_scalar(cp, cvp[:, c, o_:o_ + 512], wv[:, 0:1], None, ALU.mult),
                               reads=[R_cvp, R_c], writes=[R_co])
                            for k in (1, 2):
                                op("dve", lambda e: e.scalar_tensor_tensor(cp, cvp[:, c, o_ + k:o_ + k + 512], wv[:, k:k + 1], cp,
                                                                           ALU.mult, ALU.add), reads=[R_cvp, R_c], writes=[R_co])
                        cs_ = co[:, c, NP_TOK:NT].rearrange("p (b t) -> p b t", t=8)
                        op("dve", lambda e: e.tensor_scalar(cs_, cvs[:, c, :, 0:8], wv[:, 0:1], None, ALU.mult),
                           reads=[R_cvp, R_c], writes=[R_co])
                        for k in (1, 2):
                            op("dve", lambda e: e.scalar_tensor_tensor(cs_, cvs[:, c, :, k:k + 8], wv[:, k:k + 1], cs_,
                                                                       ALU.mult, ALU.add), reads=[R_cvp, R_c, R_co], writes=[R_co])

                def job_b(slot, jq=jq):
                    for c in range(2):
                        def evac(bi, bk, N, c=c):
                            d = co[:, c, bi * 512:bi * 512 + N]
                            op("dve", lambda e: e.tensor_tensor(d, d, PS[bk][:, 0:N], ALU.mult), reads=[PR[bk]], writes=[R_co])
                        proj_fm(slot, c * 128, blocks_all[:3], BK6, evac)

                def job_z(slot, jq=jq):
                    for c in range(2):
                        j = jq * 2 + c

                        def evac(bi, bk, N, c=c, j=j):
                            i2 = rr["n"] % 2
                            rr["n"] += 1
                            op("act", lambda e: e.activation(szb[i2][:, 0:N], PS[bk][:, 0:N], AF.Silu),
                               reads=[PR[bk]], writes=[szR[i2]])
                            op("pool", lambda e: e.tensor_tensor(ycs[i2][:, 0:N], co[:, c, bi * 512:bi * 512 + N],
                                                                 szb[i2][:, 0:N], ALU.mult),
                               reads=[R_co, szR[i2]], writes=[ycR[i2]])
                            dma("sp", ysd[16 + j, :, bi * 512:bi * 512 + N], ycs[i2][:, 0:N], ystC[i2], reads=[ycR[i2]])
                        proj_fm(slot, c * 128, blocks_all[:3], BK6, evac)

                run_jobs([(C_CC + jq * 256, 256, job_c), (C_VC + jq * 256, 256, job_v),
                          (C_BC + jq * 256, 256, job_b), (C_ZC + jq * 256, 256, job_z)])
            for q in range(4):
                bk = next_bank([0, 1, 2, 3])
                for c in range(4):
                    j = q * 4 + c
                    op("pe", lambda e: e.matmul(PS[bk][0:32, c * 128:(c + 1) * 128], chTs[:, j, :], IDF, start=True, stop=True),
                       reads=[R_chs, R_c], writes=[PR[bk]], inc=(c == 3))
                op("act", lambda e: e.copy(co[0:32, 0, 0:512], PS[bk][0:32, :]), reads=[PR[bk]], writes=[R_co])
                dma("sp", csh_s_o[:, q * 512:(q + 1) * 512], co[0:32, 0, 0:512], ysC, reads=[R_co])
                bk = next_bank([0, 1, 2, 3])
                for c in range(4):
                    j = q * 4 + c
                    op("pe", lambda e: e.matmul(PS[bk][0:4, c * 128:(c + 1) * 128], chT[:, j, :], IDF, start=True, stop=True),
                       reads=[R_ch, R_c], writes=[PR[bk]], inc=(c == 3))
                op("act", lambda e: e.copy(co[0:4, 1, 0:512], PS[bk][0:4, :]), reads=[PR[bk]], writes=[R_co])
                dma("sp", csh_p_o[:, q * 512:(q + 1) * 512], co[0:4, 1, 0:512], ysC, reads=[R_co])
        kb.barrier()
        for sbk in range(2):
            NTL = 512 if sbk == 0 else 640
            ntile = 4 if sbk == 0 else 5
            goff = sbk * 512
            xrow0 = 1024 + sbk * 512
            with ExitStack() as pc:
                fnw = sb("fnw", [128, DM], F32, pc)
                R_fnw = Reg()
                dma("sp", fnw[:], fnw_d, cch, writes=[R_fnw])
                ypre = sb("ypre", [128, 5, DM], F32, pc)
                R_yp = [Reg() for _ in range(5)]
                NOB = 3
                wo = [WB[i][:].rearrange("p k c -> p (k c)").rearrange("p (k c) -> p k c", c=512) for i in range(NOB)]
                woR = WR
                woC = WC
                ytf = sb("ytf", [128, 32, 640], BF16, pc)
                R_ytf = Reg()
                ytpC = [xc[0], xc[1]]
                xres = [sb("xres%d" % i, [128, DM], F32, pc) for i in range(2)]
                xrR = [Reg(), Reg()]
                xrC = [xc[2], hfC]
                junk3 = sb("junk3", [128, DM], BF16, pc)
                R_j3 = Reg()
                sso = sb("sso", [128, 5, 4], F32, pc)
                R_sso = Reg()
                seq = [(n, eg) for n in range(4) for eg in range(4)]

                def c_load(q):
                    n, eg = seq[q]
                    i = q % NOB
                    po = (n * 4 + eg) * 4096
                    src = w_out[:, po:po + 4096].rearrange("p (k c) -> p k c", c=512)
                    dma("pool", wo[i], src, woC[i], writes=[woR[i]])
                for ch_ in ystC:
                    nc.sync.wait_ge(kb.semh[ch_.sem], ch_.cnt)
                c_load(0)
                c_load(1)
                for eg_ in range(4):
                    dma("sp", ytf[:, eg_ * 8:(eg_ + 1) * 8, 0:NTL], ysd[eg_ * 8:(eg_ + 1) * 8, :, goff:goff + NTL].rearrange("c p t -> p c t"),
                        ytpC[0], writes=[R_ytf])
                for q, (n, eg) in enumerate(seq):
                    i = q % NOB
                    if q + 2 < len(seq):
                        c_load(q + 2)
                    for t in range(ntile):
                        for k in range(8):
                            e_ = eg * 8 + k
                            op("pe", lambda e: e.matmul(PS[t][:, :], ytf[:, e_, t * 128:(t + 1) * 128], wo[i][:, k, :],
                                                        start=(e_ == 0), stop=(e_ == 31)),
                               reads=[R_ytf, woR[i]], writes=[PR[t]], inc=(k == 7))
                    if eg == 3:
                        for t in range(ntile):
                            if t % 2 == 0:
                                op("act", lambda e: e.copy(ypre[:, t, n * 512:(n + 1) * 512], PS[t][:, :]),
                                   reads=[PR[t]], writes=[R_yp[t]])
                            else:
                                op("dve", lambda e: e.tensor_copy(ypre[:, t, n * 512:(n + 1) * 512], PS[t][:, :]),
                                   reads=[PR[t]], writes=[R_yp[t]])
                op("pool", lambda e: e.memset(sso[:], 0.0), writes=[R_sso])
                for t in range(ntile):
                    i2 = t % 2
                    r0 = (xrow0 + t * 128) if t < 4 else 2048
                    o0 = (goff + t * 128) if t < 4 else 1024
                    dma("sp", xres[i2][:], xall[r0:r0 + 128, :], xrC[i2], writes=[xrR[i2]])
                    Y = ypre[:, t, :]
                    op("dve", lambda e: e.tensor_tensor(Y, Y, xres[i2][:], ALU.add), reads=[xrR[i2]], writes=[R_yp[t]])
                    op("act", lambda e: e.activation(junk3[:], Y, AF.Square, accum_out=sso[:, t, 0:1]),
                       reads=[R_yp[t]], writes=[R_j3, R_sso])
                    op("act", lambda e: e.activation(sso[:, t, 1:2], sso[:, t, 0:1], AF.Ln, scale=1.0 / DM, bias=EPS),
                       reads=[R_sso], writes=[R_sso])
                    op("act", lambda e: e.activation(sso[:, t, 2:3], sso[:, t, 1:2], AF.Exp, scale=-0.5), reads=[R_sso], writes=[R_sso])
                    op("dve", lambda e: e.scalar_tensor_tensor(xres[i2][:], Y, sso[:, t, 2:3], fnw[:], ALU.mult, ALU.mult),
                       reads=[R_yp[t], R_sso, R_fnw], writes=[xrR[i2]])
                    dma("sp", y_o[o0:o0 + 128, :], xres[i2][:], hnCh[i2], reads=[xrR[i2]])
            kb.barrier()
        for ch in kb.chans:
            if ch.cnt:
                nc.sync.wait_ge(kb.semh[ch.sem], ch.cnt)
    return nc


_NC_CACHE = {}


def _host_consts():
    l = np.arange(128)
    ident = np.eye(128, dtype=np.float32)
    tri = (l[:, None] <= l[None, :]).astype(np.float32)
    mstrict = (l[:, None] > l[None, :]).astype(np.float32)
    same = (l[:, None] // 8 == l[None, :] // 8).astype(np.float32)
    trib = tri * same
    blk = (l[:, None] // 8 == np.arange(16)[None, :]).astype(np.float32)
    msk = np.concatenate([ident, tri, mstrict, trib, same, blk], axis=1).astype(np.float32)
    selrow = (np.arange(16)[:, None] == (l[None, :] // 8)).astype(np.float32).reshape(1, 2048)
    selm = np.ascontiguousarray(np.broadcast_to(selrow, (128, 2048))).astype(np.float32)
    return msk, selm


def kernel(x_prompt, x_sample, state_ssm, state_conv_ssd, state_conv_short, norm_w, w_in, conv_ssd_w, conv_ssd_b,
           dt_bias, a_log, d_skip, ssd_norm_w, conv_short_w, w_out, final_norm_w):
    f = lambda a: np.ascontiguousarray(np.asarray(a, dtype=np.float32))
    x_prompt, x_sample = f(x_prompt), f(x_sample)
    state_ssm, state_conv_ssd, state_conv_short = f(state_ssm), f(state_conv_ssd), f(state_conv_short)
    w_in0, w_out0 = f(w_in)[0], f(w_out)[0]
    blocks = []
    for seg in (C_ZS, C_XS):
        blocks += [(seg + i * 256, 256) for i in range(8)]
    blocks += [(C_B + g * 128, 128) for g in range(4)] + [(C_C + g * 128, 128) for g in range(4)] + [(C_DT, 32)]
    for seg in (C_ZC, C_BC, C_CC, C_VC):
        blocks += [(seg + i * 256, 256) for i in range(8)]
    w3 = w_in0.reshape(16, 128, DIN)
    wpk = np.empty((128, 16 * DIN), np.float32)
    for (c0, ncl) in blocks:
        wpk[:, 16 * c0:16 * (c0 + ncl)] = w3[:, :, c0:c0 + ncl].transpose(1, 0, 2).reshape(128, 16 * ncl)
    w_in0 = wpk
    w_out0 = np.ascontiguousarray(w_out0.reshape(4, 8, 128, 4, 512).transpose(2, 3, 0, 1, 4).reshape(128, 16 * 4096))
    msk, selm = _host_consts()
    cstb = np.zeros((128, K_END), np.float32)
    cstb[:, K_NW:K_NW + 16] = f(norm_w)[0].reshape(16, 128).T
    cstb[:, K_CSW:K_CSW + 96] = f(conv_ssd_w)[0].reshape(4, 24, 128).transpose(2, 1, 0).reshape(128, 96)
    cstb[:, K_CSB:K_CSB + 24] = f(conv_ssd_b)[0].reshape(24, 128).T
    cstb[:, K_CHW:K_CHW + 48] = f(conv_short_w)[0].reshape(3, 16, 128).transpose(2, 1, 0).reshape(128, 48)
    cstb[:, K_SNW:K_SNW + 16] = f(ssd_norm_w)[0].reshape(16, 128).T
    cstb[:, K_DTB:K_DTB + 32] = f(dt_bias)[0][None, :]
    cstb[:, K_ALOG:K_ALOG + 32] = f(a_log)[0][None, :]
    cstb[:, K_DSK:K_DSK + 32] = f(d_skip)[0][None, :]
    fnw = np.ascontiguousarray(np.broadcast_to(f(final_norm_w)[None, :], (128, DM)))
    in_maps = []
    for c in range(8):
        b, half = c // 2, c % 2
        xall = np.zeros((2180, DM), np.float32)
        if half == 1:
            xall[0:1024] = x_prompt[b, 0:1024]
            xall[2176:2180] = x_prompt[b, 1020:1024]
        xall[1024:2048] = x_prompt[b, half * 1024:(half + 1) * 1024]
        xall[2048:2176] = x_sample[16 * c:16 * c + 16].reshape(128, DM)
        cc = cstb.copy()
        cc[:, K_FLAG] = float(half)
        in_maps.append({
            "xall": xall,
            "st_ssm": np.ascontiguousarray(state_ssm[0, 16 * c:16 * c + 16].reshape(16, 2048, 128)),
            "st_cs": np.ascontiguousarray(state_conv_ssd[0, 16 * c:16 * c + 16].reshape(48, 3072)),
            "st_csh": np.ascontiguousarray(state_conv_short[0, 16 * c:16 * c + 16].reshape(32, 2048)),
            "w_in": w_in0, "w_out": w_out0, "cst": cc, "msk": msk, "selm": selm, "fnw": fnw,
        })
    if "nc" not in _NC_CACHE:
        _NC_CACHE["nc"] = build_nc()
    res = run_bass_kernel_spmd(_NC_CACHE["nc"], in_maps, core_ids=list(range(8)))
    R = res.results
    y_prompt = np.zeros((4, 2048, DM), np.float32)
    y_sample = np.zeros((128, 8, DM), np.float32)
    ssm_p = np.zeros((1, 4, 32, 64, 128), np.float32)
    cs_p = np.zeros((1, 4, 3, 3072), np.float32)
    csh_p = np.zeros((1, 4, 2, 2048), np.float32)
    ssm_s = np.zeros((1, 128, 32, 64, 128), np.float32)
    cs_s = np.zeros((1, 128, 3, 3072), np.float32)
    csh_s = np.zeros((1, 128, 2, 2048), np.float32)
    for c in range(8):
        b, half = c // 2, c % 2
        r = R[c]
        y_prompt[b, half * 1024:(half + 1) * 1024] = r["y"][0:1024]
        y_sample[16 * c:16 * c + 16] = r["y"][1024:1152].reshape(16, 8, DM)
        ssm_s[0, 16 * c:16 * c + 16] = r["ssm_s"].reshape(16, 32, 64, 128)
        cs_s[0, 16 * c:16 * c + 16] = r["cs_s"].reshape(16, 3, 3072)
        csh_s[0, 16 * c:16 * c + 16] = r["csh_s"].reshape(16, 2, 2048)
        if half == 1:
            ssm_p[0, b] = r["ssm_p"].reshape(32, 64, 128)
            cs_p[0, b] = r["cs_p"][1:4]
            csh_p[0, b] = r["csh_p"][2:4]
    return (y_prompt, y_sample, ssm_p, cs_p, csh_p, ssm_s, cs_s, csh_s)
```

```python
import numpy as np
from contextlib import ExitStack
import concourse.bass as bass
import concourse.mybir as mybir
from concourse.bass_utils import run_bass_kernel_spmd

F32 = mybir.dt.float32
BF16 = mybir.dt.bfloat16
AF = mybir.ActivationFunctionType
ALU = mybir.AluOpType

DM = 2048
NP_TOK = 1024
NS_TOK = 128
NT = NP_TOK + NS_TOK
DIN = 13344
C_ZS, C_XS, C_B, C_C, C_DT, C_ZC, C_BC, C_CC, C_VC = 0, 2048, 4096, 4608, 5120, 5152, 7200, 9248, 11296
EPS = 1e-6
K_NW, K_CSW, K_CSB, K_CHW, K_SNW, K_DTB, K_ALOG, K_DSK, K_FLAG, K_END = 0, 16, 112, 136, 184, 200, 232, 264, 296, 297
M_ID, M_TRI, M_MS, M_TRIB, M_ONEB, M_BLK, M_END = 0, 128, 256, 384, 512, 640, 656


class Reg:
    __slots__ = ("w", "r")

    def __init__(self):
        self.w = None
        self.r = {}


class Chan:
    def __init__(self, sem, name):
        self.sem = sem
        self.name = name
        self.cnt = 0


class KB:
    def __init__(self, nc, es):
        self.nc = nc
        self.es = es
        self.semh = {}
        self.E = {}
        for name, h in (("pe", nc.tensor), ("act", nc.scalar), ("dve", nc.vector),
                        ("pool", nc.gpsimd), ("sp", nc.sync)):
            s = es.enter_context(nc.semaphore("e_" + name))
            self.semh["e_" + name] = s
            self.E[name] = dict(h=h, sem="e_" + name, cnt=0, waited={})
        self.nch = 0
        self.chans = []

    def chan(self):
        n = "c%d" % self.nch
        self.nch += 1
        s = self.es.enter_context(self.nc.semaphore(n))
        self.semh[n] = s
        c = Chan(n, n)
        self.chans.append(c)
        return c

    def _waits(self, e, reads, writes, skip=None):
        need = {}
        own = e["sem"]
        for d in reads:
            if d.w is not None:
                if d.w[0] == own and own == "e_pe":
                    continue
                need[d.w[0]] = max(need.get(d.w[0], 0), d.w[1])
        for d in writes:
            if d.w is not None and d.w[0] != own:
                need[d.w[0]] = max(need.get(d.w[0], 0), d.w[1])
            for s, v in d.r.items():
                if s != own:
                    need[s] = max(need.get(s, 0), v)
        for s, v in need.items():
            if s == skip:
                continue
            if e["waited"].get(s, 0) < v:
                e["h"].wait_ge(self.semh[s], v)
                e["waited"][s] = v

    def _mark(self, tok, reads, writes):
        for d in reads:
            d.r[tok[0]] = max(d.r.get(tok[0], 0), tok[1])
        for d in writes:
            d.w = tok
            d.r = {}

    def op(self, eng, fn, reads=(), writes=(), inc=True):
        e = self.E[eng]
        self._waits(e, reads, writes)
        ins = fn(e["h"])
        if inc:
            ins.then_inc(self.semh[e["sem"]], 1)
            e["cnt"] += 1
            tok = (e["sem"], e["cnt"])
        else:
            tok = (e["sem"], e["cnt"] + 1)
        self._mark(tok, reads, writes)

    def barrier(self):
        for name, e in self.E.items():
            for n2, e2 in self.E.items():
                if n2 != name and e2["cnt"] > e["waited"].get(e2["sem"], 0):
                    e["h"].wait_ge(self.semh[e2["sem"]], e2["cnt"])
                    e["waited"][e2["sem"]] = e2["cnt"]
            for ch in self.chans:
                if ch.cnt > e["waited"].get(ch.sem, 0):
                    e["h"].wait_ge(self.semh[ch.sem], ch.cnt)
                    e["waited"][ch.sem] = ch.cnt

    def dma(self, q, out, in_, ch, reads=(), writes=()):
        e = self.E[q]
        self._waits(e, reads, writes, skip=ch.sem)
        e["h"].dma_start(out=out, in_=in_).then_inc(self.semh[ch.sem], 16)
        ch.cnt += 16
        self._mark((ch.sem, ch.cnt), reads, writes)


def build_nc():
    nc = bass.Bass("TRN2", target_bir_lowering=False)

    def din(n, s):
        return nc.dram_tensor(n, s, F32, kind="ExternalInput").ap()

    def dout(n, s):
        return nc.dram_tensor(n, s, F32, kind="ExternalOutput").ap()

    xall = din("xall", [2180, DM])
    st_ssm = din("st_ssm", [16, 2048, 128])
    st_cs = din("st_cs", [48, 3072])
    st_csh = din("st_csh", [32, 2048])
    w_in = din("w_in", [128, 16 * DIN])
    w_out = din("w_out", [128, 16 * 4096])
    cst_d = din("cst", [128, K_END])
    msk_d = din("msk", [128, M_END])
    selm_d = din("selm", [128, 2048])
    fnw_d = din("fnw", [128, DM])
    y_o = dout("y", [NT, DM])
    ssm_p_o = dout("ssm_p", [2048, 128])
    cs_p_o = dout("cs_p", [4, 3072])
    csh_p_o = dout("csh_p", [4, 2048])
    ssm_s_o = dout("ssm_s", [16, 2048, 128])
    cs_s_o = dout("cs_s", [48, 3072])
    csh_s_o = dout("csh_s", [32, 2048])
    ysd = nc.dram_tensor("ysd", [32, 128, NT], BF16, kind="Internal").ap()

    with ExitStack() as es:
        kb = KB(nc, es)
        op, dma = kb.op, kb.dma

        nmc = dict(n=0)

        def sb(name, shape, dt, stack=es):
            nmc["n"] += 1
            return stack.enter_context(nc.sbuf_tensor("s%d_%s" % (nmc["n"], name), shape, dt))

        PS = [es.enter_context(nc.psum_tensor("ps%d" % i, [128, 512], F32)) for i in range(8)]
        PR = [Reg() for _ in range(8)]

        def psb(i):
            return PS[i][:].bitcast(BF16)

        cst = sb("cst", [128, K_END], F32)
        msk = sb("msk", [128, M_END], F32)
        identb = sb("identb", [128, 128], BF16)
        selm = sb("selm", [128, 2048], BF16)
        abc = sb("abc", [128, 32], F32)
        ones = sb("ones", [128, 128], F32)
        R_c = Reg()
        cch = kb.chan()
        cchp = kb.chan()
        R_c2 = Reg()
        dma("sp", cst[:], cst_d, cch, writes=[R_c])
        dma("sp", msk[:], msk_d, cch, writes=[R_c])
        dma("pool", identb[:], msk_d[:, M_ID:M_ID + 128], cchp, writes=[R_c2])
        dma("pool", selm[:], selm_d, cchp, writes=[R_c2])
        op("act", lambda e: e.activation(abc[:], cst[:, K_ALOG:K_ALOG + 32], AF.Exp), reads=[R_c, R_c2], writes=[R_c])
        op("dve", lambda e: e.tensor_scalar(abc[:], abc[:], -1.0, None, ALU.mult), reads=[R_c], writes=[R_c])
        op("dve", lambda e: e.memset(ones[:], 1.0), writes=[R_c])
        MSb = sb("MSb", [128, 128], BF16)
        op("dve", lambda e: e.tensor_copy(MSb[:], msk[:, M_MS:M_MS + 128]), reads=[R_c], writes=[R_c2])
        IDF = msk[:, M_ID:M_ID + 128]
        TRI = msk[:, M_TRI:M_TRI + 128]
        MS = msk[:, M_MS:M_MS + 128]
        TRIB = msk[:, M_TRIB:M_TRIB + 128]
        ONEB = msk[:, M_ONEB:M_ONEB + 128]
        BLK = msk[:, M_BLK:M_BLK + 16]

        dtm = sb("dtm", [128, 5, 32], F32)
        dtq = sb("dtq", [128, 8, 32], F32)
        R_dtm, R_dtq = Reg(), Reg()
        hmid = sb("hmid", [128, 4, 512], F32)
        R_hmid = Reg()
        csT = sb("csT", [128, 24, 4], F32)
        csTs = sb("csTs", [128, 24, 48], F32)
        chT = sb("chT", [128, 16, 4], F32)
        chTs = sb("chTs", [128, 16, 32], F32)
        R_cs, R_css, R_ch, R_chs = Reg(), Reg(), Reg(), Reg()

        NWB = 3
        WB = [sb("wb%d" % i, [128, 16, 256], BF16) for i in range(NWB)]
        WR = [Reg() for _ in range(NWB)]
        WC = [kb.chan() for _ in range(NWB)]
        wstate = dict(n=0)

        def w_load(col0, ncols):
            i = wstate["n"] % NWB
            wstate["n"] += 1
            src = w_in[:, 16 * col0:16 * (col0 + ncols)].rearrange("p (k c) -> p k c", c=ncols)
            for kk in range(0, 16, 8):
                dma("pool", WB[i][:, kk:kk + 8, 0:ncols], src[:, kk:kk + 8, :], WC[i], writes=[WR[i]])
            return i

        WSEQ = [(C_DT, 32)]
        for g_ in range(4):
            WSEQ += [(C_B + g_ * 128, 128), (C_XS + g_ * 512, 256), (C_XS + g_ * 512 + 256, 256)]
        for sb_ in range(2):
            WSEQ.append((C_DT, 32))
            for g_ in range(4):
                WSEQ += [(C_B + g_ * 128, 128), (C_C + g_ * 128, 128), (C_XS + g_ * 512, 256), (C_XS + g_ * 512 + 256, 256),
                         (C_ZS + g_ * 512, 256), (C_ZS + g_ * 512 + 256, 256)]
        for jq_ in range(8):
            WSEQ += [(C_CC + jq_ * 256, 256), (C_VC + jq_ * 256, 256), (C_BC + jq_ * 256, 256), (C_ZC + jq_ * 256, 256)]
        wq = dict(issued=0, used=0, slots={})
        PFW = 2

        def w_acquire(c0, ncl):
            k = wq["used"]
            assert WSEQ[k] == (c0, ncl), (k, WSEQ[k], c0, ncl)
            while wq["issued"] < min(len(WSEQ), k + 1 + PFW):
                q = wq["issued"]
                wq["slots"][q] = w_load(*WSEQ[q])
                wq["issued"] += 1
            wq["used"] += 1
            return wq["slots"].pop(k)

        def run_jobs(jobs, PF=2):
            slots = {}
            for j in range(min(PF, len(jobs))):
                slots[j] = w_load(jobs[j][0], jobs[j][1])
            for j, (c0, ncl, fn) in enumerate(jobs):
                if j + PF < len(jobs):
                    slots[j + PF] = w_load(jobs[j + PF][0], jobs[j + PF][1])
                fn(slots.pop(j))

        bank_rr = dict(n=0)

        def next_bank(pool):
            b = pool[bank_rr["n"] % len(pool)]
            bank_rr["n"] += 1
            return b

        xc = [kb.chan() for _ in range(3)]
        hfC = kb.chan()
        ysC = kb.chan()
        csC = [kb.chan(), kb.chan()]
        hnCh = [kb.chan(), kb.chan()]
        ystC = [kb.chan(), kb.chan()]
        R_ysd = Reg()

        def phase0(tiles):
            with ExitStack() as p0:
                xb_ = [sb("xt%d" % i, [128, DM], F32, p0) for i in range(3)]
                xr = [Reg() for _ in range(3)]
                junk = sb("junk0", [128, DM], BF16, p0)
                xn = [sb("xn%d" % i, [128, DM], BF16, p0) for i in range(2)]
                xnr = [Reg(), Reg()]
                ssq = sb("ssq0", [128, 32], F32, p0)
                R_ss, R_junk = Reg(), Reg()
                op("dve", lambda e: e.memset(ssq[:], 0.0), writes=[R_ss])
                for i, (r0, nr, dst, dreg, c0) in enumerate(tiles):
                    xt, xreg = xb_[i % 3], xr[i % 3]
                    dma("sp", xt[0:nr, :], xall[r0:r0 + nr, :], xc[i % 3], writes=[xreg])
                    op("act", lambda e: e.activation(junk[0:nr, :], xt[0:nr, :], AF.Square,
                                                     accum_out=ssq[0:nr, 3 * i:3 * i + 1]),
                       reads=[xreg], writes=[R_junk, R_ss])
                    op("act", lambda e: e.activation(ssq[0:nr, 3 * i + 1:3 * i + 2], ssq[0:nr, 3 * i:3 * i + 1], AF.Ln,
                                                     scale=1.0 / DM, bias=EPS), reads=[R_ss], writes=[R_ss])
                    op("act", lambda e: e.activation(ssq[0:nr, 3 * i + 2:3 * i + 3], ssq[0:nr, 3 * i + 1:3 * i + 2], AF.Exp,
                                                     scale=-0.5), reads=[R_ss], writes=[R_ss])
                    xnt, xnreg = xn[i % 2], xnr[i % 2]
                    op("dve", lambda e: e.tensor_scalar(xnt[0:nr, :], xt[0:nr, :], ssq[0:nr, 3 * i + 2:3 * i + 3], None, ALU.mult),
                       reads=[xreg, R_ss], writes=[xnreg])
                    for hf in range(2):
                        bk = (2 * i + hf) % 4
                        pv = psb(bk)
                        for k8 in range(8):
                            k = hf * 8 + k8
                            op("pe", lambda e: e.transpose(pv[:, k8 * 128:k8 * 128 + nr], xnt[0:nr, k * 128:(k + 1) * 128],
                                                           identb[0:nr, 0:nr]),
                               reads=[xnreg, R_c], writes=[PR[bk]], inc=(k8 == 7))
                        src = pv.rearrange("p (a b) -> p a b", b=128)[:, :, 0:nr]
                        nwv = cst[:, K_NW + hf * 8:K_NW + hf * 8 + 8].unsqueeze(2).broadcast_to([128, 8, nr])
                        op("dve", lambda e: e.tensor_tensor(dst[:, hf * 8:hf * 8 + 8, c0:c0 + nr], src, nwv, ALU.mult),
                           reads=[PR[bk], R_c], writes=[dreg])
            kb.barrier()

        def proj_fm(slot, cofs, blocks, pool, evac):
            for bi, (rhs, N, hreg) in enumerate(blocks):
                bk = next_bank(pool)
                for k in range(16):
                    op("pe", lambda e: e.matmul(PS[bk][:, 0:N], WB[slot][:, k, cofs:cofs + 128], rhs(k),
                                                start=(k == 0), stop=(k == 15)),
                       reads=[WR[slot], hreg], writes=[PR[bk]], inc=(k == 15))
                evac(bi, bk, N)

        def job_dt(src, hreg, ntl, dstt, dreg, tcol_=None):
            def fn(slot):
                bk = 6
                for t in range(ntl):
                    for k in range(16):
                        c_ = tcol_[t] if tcol_ else t * 128
                        op("pe", lambda e: e.matmul(PS[bk][:, t * 32:(t + 1) * 32], src[:, k, c_:c_ + 128],
                                                    WB[slot][:, k, 0:32], start=(k == 0), stop=(k == 15)),
                           reads=[WR[slot], hreg], writes=[PR[bk]], inc=(k == 15 and t == ntl - 1))
                dv = dstt[:, 0:ntl, :]
                pv = PS[bk][:, 0:ntl * 32].rearrange("p (t h) -> p t h", h=32)
                op("dve", lambda e: e.tensor_tensor(dv, pv, cst[:, None, K_DTB:K_DTB + 32].broadcast_to([128, ntl, 32]), ALU.add),
                   reads=[PR[bk], R_c], writes=[dreg])
                op("act", lambda e: e.activation(dv, dv, AF.Exp), reads=[dreg], writes=[dreg])
                op("act", lambda e: e.activation(dv, dv, AF.Ln, bias=1.0), reads=[dreg], writes=[dreg])
            return fn

        def stage123(B, tl, g, mode):
            hs = slice(g * 8, g * 8 + 8)
            ntl = len(tl)
            xtk, xdt, xdd, la, pex = B["xtk"], B["xdt"], B["xdd"], B["la"], B["pex"]
            for (ti, xsrc, xreg_, bsrc, breg_, dtap, trix, onx) in tl:
                bk = next_bank([0, 1])
                pv = psb(bk)
                for c in range(5):
                    srcap, sreg = (xsrc(c), xreg_) if c < 4 else (bsrc, breg_)
                    op("pe", lambda e: e.transpose(pv[:, c * 128:(c + 1) * 128], srcap, identb[:]),
                       reads=[sreg, R_c], writes=[PR[bk]], inc=(c == 4))
                op("dve", lambda e: e.tensor_copy(xtk[:, ti, :], pv[:, 0:640]), reads=[PR[bk]], writes=[B["R_xtk"]])
            for (ti, xsrc, xreg_, bsrc, breg_, dtap, trix, onx) in tl:
                op("dve", lambda e: e.tensor_tensor(la[:, ti, :], dtap, abc[:, hs], ALU.mult),
                   reads=[R_dtm, R_dtq, R_c], writes=[B["R_la"]])
            for n_, (ti, xsrc, xreg_, bsrc, breg_, dtap, trix, onx) in enumerate(tl):
                op("pe", lambda e: e.matmul(PS[2][:, ti * 16:ti * 16 + 8], trix, la[:, ti, :], start=True, stop=True),
                   reads=[B["R_la"], R_c], writes=[PR[2]], inc=False)
                op("pe", lambda e: e.matmul(PS[2][:, ti * 16 + 8:ti * 16 + 16], onx, la[:, ti, :], start=True, stop=True),
                   reads=[B["R_la"], R_c], writes=[PR[2]], inc=(n_ == ntl - 1))
            t0, t1 = tl[0][0], tl[-1][0] + 1
            pvv = PS[2][:, t0 * 16:t1 * 16].rearrange("p (t c) -> p t c", c=16)
            op("dve", lambda e: e.tensor_copy(pex[:, t0:t1, 0:16], pvv), reads=[PR[2]], writes=[B["R_pex"]])
            op("dve", lambda e: e.tensor_tensor(pex[:, t0:t1, 16:24], pex[:, t0:t1, 8:16], pex[:, t0:t1, 0:8], ALU.subtract),
               reads=[B["R_pex"]], writes=[B["R_pex"]])
            op("act", lambda e: e.activation(pex[:, t0:t1, :], pex[:, t0:t1, :], AF.Exp), reads=[B["R_pex"]], writes=[B["R_pex"]])
            for (ti, xsrc, xreg_, bsrc, breg_, dtap, trix, onx) in tl:
                xv = xtk[:, ti, 0:512].rearrange("p (h q) -> p h q", q=64)
                op("dve", lambda e: e.tensor_tensor(xdt[:, ti, :].rearrange("p (h q) -> p h q", q=64), xv,
                                                    dtap.unsqueeze(2).broadcast_to([128, 8, 64]), ALU.mult),
                   reads=[B["R_xtk"], R_dtm, R_dtq], writes=[B["R_xdt"]])
                op("pool", lambda e: e.tensor_tensor(xdd[:, ti, :].rearrange("p (h q) -> p h q", q=64),
                                                     xdt[:, ti, :].rearrange("p (h q) -> p h q", q=64),
                                                     pex[:, ti, 16:24].unsqueeze(2).broadcast_to([128, 8, 64]), ALU.mult),
                   reads=[B["R_xdt"], B["R_pex"]], writes=[B["R_xdd"]])

        def state_scan(B, tl, init):
            hcur, hbf, xtk, xdd, pex = B["hcur"], B["hbf"], B["xtk"], B["xdd"], B["pex"]
            init()
            for (ti, xsrc, xreg_, bsrc, breg_, dtap, trix, onx) in tl:
                op("act", lambda e: e.copy(hbf[:, ti, :], hcur[:]), reads=[B["R_hcur"]], writes=[B["R_hbf"]])
                bk = next_bank([3, 4])
                op("pe", lambda e: e.matmul(PS[bk][:, :], xtk[:, ti, 512:640], xdd[:, ti, :], start=True, stop=True),
                   reads=[B["R_xtk"], B["R_xdd"]], writes=[PR[bk]])
                hv = hcur[:].rearrange("p (h q) -> p h q", q=64)
                op("dve", lambda e: e.tensor_tensor(hv, hv, pex[:, ti, 8:16].unsqueeze(2).broadcast_to([128, 8, 64]), ALU.mult),
                   reads=[B["R_pex"]], writes=[B["R_hcur"]])
                op("dve", lambda e: e.tensor_tensor(hcur[:], hcur[:], PS[bk][:, :], ALU.add),
                   reads=[PR[bk]], writes=[B["R_hcur"]])

        with ExitStack() as pq:
            hTq = sb("hTq", [128, 16, NP_TOK], BF16, pq)
            R_hTq = Reg()
            phase0([(t * 128, 128, hTq, R_hTq, t * 128) for t in range(8)])
            xsTq = sb("xsTq", [128, 4, NP_TOK], BF16, pq)
            BTq = sb("BTq", [128, NP_TOK], BF16, pq)
            R_xsTq, R_BTq = Reg(), Reg()
            preq = [sb("preq%d" % i, [128, 4 + NP_TOK], F32, pq) for i in range(2)]
            accq = sb("accq", [128, NP_TOK], F32, pq)
            preqR, R_accq = [Reg(), Reg()], Reg()
            Bq = dict(xtk=sb("xtkq", [128, 8, 640], BF16, pq), xdt=sb("xdtq", [128, 8, 512], BF16, pq),
                      xdd=sb("xddq", [128, 8, 512], BF16, pq), la=sb("laq", [128, 8, 8], F32, pq),
                      pex=sb("pexq", [128, 8, 24], F32, pq), hcur=sb("hcurq", [128, 512], F32, pq),
                      hbf=sb("hbfq", [128, 8, 512], BF16, pq))
            for nm in ("xtk", "xdt", "xdd", "la", "pex", "hcur", "hbf"):
                Bq["R_" + nm] = Reg()
            cntq = dict(n=0)
            blocks_pre = [(lambda k: hTq[:, k, 0:512], 512, R_hTq), (lambda k: hTq[:, k, 512:1024], 512, R_hTq)]

            def xbc_pre(slot, cofs, j, dst, dreg):
                i = cntq["n"] % 2
                cntq["n"] += 1
                P_ = preq[i]
                wv = cst[:, K_CSW + 4 * j:K_CSW + 4 * j + 4]
                bv = cst[:, K_CSB + j:K_CSB + j + 1]
                op("pool", lambda e: e.memset(P_[:, 0:4], 0.0), writes=[preqR[i]])

                def evac(bi, bk, N):
                    op("act", lambda e: e.copy(P_[:, 4 + bi * 512:4 + bi * 512 + 512], PS[bk][:, 0:512]),
                       reads=[PR[bk]], writes=[preqR[i]])
                proj_fm(slot, cofs, blocks_pre, [3, 4, 5, 7], evac)
                op("dve", lambda e: e.tensor_scalar(accq[:], P_[:, 1:1 + NP_TOK], wv[:, 0:1], None, ALU.mult),
                   reads=[preqR[i], R_c], writes=[R_accq])
                for k in range(1, 4):
                    op("dve", lambda e: e.scalar_tensor_tensor(accq[:], P_[:, 1 + k:1 + k + NP_TOK], wv[:, k:k + 1], accq[:],
                                                               ALU.mult, ALU.add), reads=[preqR[i], R_c], writes=[R_accq])
                op("act", lambda e: e.activation(dst, accq[:], AF.Silu, bias=bv), reads=[R_accq, R_c], writes=[dreg])

            run_jobs([(C_DT, 32, job_dt(hTq, R_hTq, 8, dtq, R_dtq))])
            for g in range(4):
                def jB(slot, g=g):
                    xbc_pre(slot, 0, 16 + g, BTq[:], R_BTq)

                def jx(hf, g=g):
                    def fn(slot):
                        for c2 in range(2):
                            c = hf * 2 + c2
                            xbc_pre(slot, c2 * 128, g * 4 + c, xsTq[:, c, :], R_xsTq)
                    return fn
                run_jobs([(C_B + g * 128, 128, jB), (C_XS + g * 512, 256, jx(0)), (C_XS + g * 512 + 256, 256, jx(1))])
                tlq = [(t, (lambda c, t=t: xsTq[:, c, t * 128:(t + 1) * 128]), R_xsTq, BTq[:, t * 128:(t + 1) * 128], R_BTq,
                        dtq[:, t, g * 8:g * 8 + 8], TRI, ones[:]) for t in range(8)]
                stage123(Bq, tlq, g, "prefix")
                state_scan(Bq, tlq, lambda: op("dve", lambda e: e.memset(Bq["hcur"][:], 0.0), writes=[Bq["R_hcur"]]))
                op("dve", lambda e: e.tensor_scalar(hmid[:, g, :], Bq["hcur"][:], cst[:, K_FLAG:K_FLAG + 1], None, ALU.mult),
                   reads=[Bq["R_hcur"], R_c], writes=[R_hmid])

        kb.barrier()
        hT = sb("hT", [128, 16, NT], BF16)
        hTh = sb("hTh", [128, 16, 4], BF16)
        R_hT, R_hTh = Reg(), Reg()
        tiles0 = [(1024 + t * 128, 128, hT, R_hT, t * 128) for t in range(9)]
        tiles0.append((2176, 4, hTh, R_hTh, 0))
        phase0(tiles0)
        for sbk in range(2):
            NTL = 512 if sbk == 0 else 640
            ntile = 4 if sbk == 0 else 5
            goff = sbk * 512
            tb = sbk * 512
            tcol = [tb + t * 128 for t in range(4)] + [1024]
            blocks_main = [(lambda k, tb=tb: hT[:, k, tb:tb + 512], 512, R_hT)]
            if sbk == 1:
                blocks_main.append((lambda k: hT[:, k, 1024:1152], 128, R_hT))
                blocks_main.append((lambda k: hT[:, k, 508:512], 4, R_hT))
            else:
                blocks_main.append((lambda k: hTh[:, k, 0:4], 4, R_hTh))
            HALO = len(blocks_main) - 1
            with ExitStack() as pa:
                BT = sb("BT", [128, 640], BF16, pa)
                CT = sb("CT", [128, 640], BF16, pa)
                xsT = sb("xsT", [128, 4, 640], BF16, pa)
                szs = sb("szs", [128, 5, 512], BF16, pa)
                R_BT, R_CT, R_xsT, R_szs = Reg(), Reg(), Reg(), Reg()
                pre = [sb("pre%d" % i, [128, 516], F32, pa) for i in range(2)]
                pres = [sb("pres%d" % i, [128, 16, 11], F32, pa) for i in range(2)]
                acc = sb("acc", [128, 512], F32, pa)
                accs = sb("accs", [128, 16, 8], F32, pa)
                preR, R_acc = [Reg(), Reg()], Reg()
                stcs = sb("stcs", [128, 1536], F32, pa)
                R_stcs = Reg()
                if sbk == 1:
                    dma("sp", stcs[0:48, :], st_cs[:, 0:1536], cch, writes=[R_stcs])
                    dma("sp", stcs[64:112, :], st_cs[:, 1536:3072], cch, writes=[R_stcs])
                Bm_ = dict(xtk=sb("xtk", [128, 5, 640], BF16, pa), xdt=sb("xdt", [128, 5, 512], BF16, pa),
                           xdd=sb("xdd", [128, 5, 512], BF16, pa), la=sb("la", [128, 5, 8], F32, pa),
                           pex=sb("pex", [128, 5, 24], F32, pa), hcur=sb("hcur", [128, 512], F32, pa),
                           hbf=sb("hbf", [128, 4, 512], BF16, pa))
                for nm in ("xtk", "xdt", "xdd", "la", "pex", "hcur", "hbf"):
                    Bm_["R_" + nm] = Reg()
                xtk, xdt, xdd, la, pex, hcur, hbf = (Bm_[n_] for n_ in ("xtk", "xdt", "xdd", "la", "pex", "hcur", "hbf"))
                cbm = sb("cbm", [128, 5, 128], F32, pa)
                R_cbm = Reg()
                Lb2 = [sb("Lb%d" % i, [128, 2, 8, 128], BF16, pa) for i in range(2)]
                LbR = [Reg(), Reg()]
                Eb2 = [sb("Eb2_%d" % i, [128, 1024], BF16, pa) for i in range(2)]
                EbR = [Reg(), Reg()]
                lahi = sb("lahi", [128, 5, 8], BF16, pa)
                lalo = sb("lalo", [128, 5, 8], BF16, pa)
                latmp = sb("latmp", [128, 5, 8], F32, pa)
                R_lahl = Reg()
                DI = sb("DI", [128, 8, 128], BF16, pa)
                R_DI = Reg()
                MT = sb("MT", [128, 5, 1024], BF16, pa)
                R_MT = Reg()
                ytm = [sb("ytm%d" % i, [128, 512], F32, pa) for i in range(2)]
                ytR = [Reg(), Reg()]
                ynb = [sb("ynb%d" % i, [128, 512], BF16, pa) for i in range(2)]
                ynR = [Reg(), Reg()]
                yst = [sb("yst%d" % i, [128, 4, 128], BF16, pa) for i in range(2)]
                ystR = [Reg(), Reg()]
                tmp = sb("tmp", [128, 512], F32, pa)
                R_tmp = Reg()
                sst = sb("sst", [128, 5, 4], F32, pa)
                R_sst = Reg()
                junk2 = sb("junk2", [128, 512], BF16, pa)
                R_j2 = Reg()
                if sbk == 1:
                    h0 = [sb("h0_%d" % i, [128, 4, 128], F32, pa) for i in range(3)]
                    h0R = [Reg() for _ in range(3)]
                    h0C = [xc[0], xc[1], xc[2]]
                    h0b = [sb("h0b%d" % i, [128, 512], BF16, pa) for i in range(2)]
                    h0bR = [Reg(), Reg()]
                    h0T = [sb("h0T%d" % i, [128, 512], BF16, pa) for i in range(2)]
                    h0TR = [Reg(), Reg()]
                    hn = [sb("hn%d" % i, [128, 4, 128], F32, pa) for i in range(2)]
                    hnR = [Reg(), Reg()]
                    hnC = hnCh
                    CTm = sb("CTm", [128, 16, 128], BF16, pa)
                    Bmk = sb("Bmk", [128, 16, 128], BF16, pa)
                    larep = sb("larep", [128, 8, 64], F32, pa)
                    cdT = sb("cdT", [128, 4, 16], F32, pa)
                    R_CTm, R_Bmk, R_larep, R_cdT = Reg(), Reg(), Reg(), Reg()
                cnt = dict(n=0)

                def xbc_chunk(slot, cofs, j, dstP, dstS, dreg):
                    i = cnt["n"] % 2
                    cnt["n"] += 1
                    P_, PSm = pre[i], pres[i]
                    wv = cst[:, K_CSW + 4 * j:K_CSW + 4 * j + 4]
                    bv = cst[:, K_CSB + j:K_CSB + j + 1]
                    if sbk == 1:
                        hj, jj = j // 12, j % 12
                        bk = next_bank([6, 7])
                        op("pe", lambda e: e.matmul(PS[bk][:, 0:48], stcs[64 * hj:64 * hj + 48, jj * 128:(jj + 1) * 128],
                                                    IDF[64 * hj:64 * hj + 48, 64 * hj:64 * hj + 48], start=True, stop=True),
                           reads=[R_stcs, R_c], writes=[PR[bk]])
                        op("act", lambda e: e.copy(PSm[:, :, 0:3], PS[bk][:, 0:48].rearrange("p (b t) -> p b t", t=3)),
                           reads=[PR[bk]], writes=[preR[i]])

                    def evac(bi, bk, N):
                        if bi == HALO:
                            op("act", lambda e: e.copy(P_[:, 0:4], PS[bk][:, 0:4]), reads=[PR[bk]], writes=[preR[i]])
                        elif bi == 0:
                            op("act", lambda e: e.copy(P_[:, 4:516], PS[bk][:, 0:512]), reads=[PR[bk]], writes=[preR[i]])
                        else:
                            op("act", lambda e: e.copy(PSm[:, :, 3:11], PS[bk][:, 0:128].rearrange("p (b t) -> p b t", t=8)),
                               reads=[PR[bk]], writes=[preR[i]])
                    proj_fm(slot, cofs, blocks_main, [3, 4, 5], evac)
                    if sbk == 1:
                        op("pool", lambda e: e.tensor_copy(csT[:, j, :], P_[:, 512:516]), reads=[preR[i]], writes=[R_cs])
                        op("pool", lambda e: e.tensor_copy(csTs[:, j, :].rearrange("p (b t) -> p b t", t=3), PSm[:, :, 8:11]),
                           reads=[preR[i]], writes=[R_css])
                    op("dve", lambda e: e.tensor_scalar(acc[:], P_[:, 1:513], wv[:, 0:1], None, ALU.mult),
                       reads=[preR[i], R_c], writes=[R_acc])
                    for k in range(1, 4):
                        op("dve", lambda e: e.scalar_tensor_tensor(acc[:], P_[:, 1 + k:513 + k], wv[:, k:k + 1], acc[:],
                                                                   ALU.mult, ALU.add), reads=[preR[i], R_c], writes=[R_acc])
                    op("act", lambda e: e.activation(dstP, acc[:], AF.Silu, bias=bv), reads=[R_acc, R_c], writes=[dreg])
                    if sbk == 1:
                        op("dve", lambda e: e.tensor_scalar(accs[:], PSm[:, :, 0:8], wv[:, 0:1], None, ALU.mult),
                           reads=[preR[i], R_c], writes=[R_acc])
                        for k in range(1, 4):
                            op("dve", lambda e: e.scalar_tensor_tensor(accs[:], PSm[:, :, k:k + 8], wv[:, k:k + 1], accs[:],
                                                                        ALU.mult, ALU.add), reads=[preR[i], R_c, R_acc], writes=[R_acc])
                        op("act", lambda e: e.activation(dstS, accs[:].rearrange("p b t -> p (b t)"), AF.Silu, bias=bv),
                           reads=[R_acc, R_c], writes=[dreg])

                run_jobs([(C_DT, 32, job_dt(hT, R_hT, ntile, dtm, R_dtm, tcol))])

                for g in range(4):
                    hs = slice(g * 8, g * 8 + 8)

                    def jB(slot, g=g):
                        xbc_chunk(slot, 0, 16 + g, BT[:, 0:512], BT[:, 512:640], R_BT)

                    def jC(slot, g=g):
                        xbc_chunk(slot, 0, 20 + g, CT[:, 0:512], CT[:, 512:640], R_CT)

                    def jx(hf, g=g):
                        def fn(slot):
                            for c2 in range(2):
                                c = hf * 2 + c2
                                xbc_chunk(slot, c2 * 128, g * 4 + c, xsT[:, c, 0:512], xsT[:, c, 512:640], R_xsT)
                        return fn

                    def jz(hf, g=g):
                        def fn(slot):
                            for t in range(ntile):
                                bk = next_bank([3, 4, 5])
                                for k in range(16):
                                    op("pe", lambda e: e.matmul(PS[bk][:, 0:256], hT[:, k, tcol[t]:tcol[t] + 128], WB[slot][:, k, 0:256],
                                                                start=(k == 0), stop=(k == 15)),
                                       reads=[WR[slot], R_hT], writes=[PR[bk]], inc=(k == 15))
                                op("act", lambda e: e.activation(szs[:, t, hf * 256:(hf + 1) * 256], PS[bk][:, 0:256], AF.Silu),
                                   reads=[PR[bk]], writes=[R_szs])
                        return fn
                    run_jobs([(C_B + g * 128, 128, jB), (C_C + g * 128, 128, jC),
                              (C_XS + g * 512, 256, jx(0)), (C_XS + g * 512 + 256, 256, jx(1)),
                              (C_ZS + g * 512, 256, jz(0)), (C_ZS + g * 512 + 256, 256, jz(1))])

                    tlm = [(t, (lambda c, t=t: xsT[:, c, t * 128:(t + 1) * 128]), R_xsT, BT[:, t * 128:(t + 1) * 128], R_BT,
                            dtm[:, t, hs], (TRI if t < 4 else TRIB), (ones[:] if t < 4 else ONEB)) for t in range(ntile)]
                    stage123(Bm_, tlm, g, "main")
                    state_scan(Bm_, tlm[:4], lambda: op("dve", lambda e: e.tensor_copy(hcur[:], hmid[:, g, :]),
                                                        reads=[R_hmid], writes=[Bm_["R_hcur"]]))
                    op("dve", lambda e: e.tensor_copy(hmid[:, g, :], hcur[:]), reads=[Bm_["R_hcur"]], writes=[R_hmid])
                    if sbk == 1:
                        bk = next_bank([5])
                        for jj in range(4):
                            op("pe", lambda e: e.matmul(PS[bk][:, jj * 128:(jj + 1) * 128], hcur[:, jj * 128:(jj + 1) * 128], IDF,
                                                        start=True, stop=True), reads=[Bm_["R_hcur"], R_c], writes=[PR[bk]], inc=(jj == 3))
                        op("act", lambda e: e.copy(tmp[:], PS[bk][:, :]), reads=[PR[bk]], writes=[R_tmp])
                        dma("sp", ssm_p_o[g * 512:(g + 1) * 512, :].rearrange("(j p) n -> p j n", p=128),
                            tmp[:].rearrange("p (j n) -> p j n", n=128), ysC, reads=[R_tmp])

                    op("dve", lambda e: e.tensor_copy(lahi[:, 0:ntile, :], la[:, 0:ntile, :]), reads=[Bm_["R_la"]], writes=[R_lahl])
                    op("dve", lambda e: e.tensor_tensor(latmp[:, 0:ntile, :], la[:, 0:ntile, :], lahi[:, 0:ntile, :], ALU.subtract),
                       reads=[Bm_["R_la"], R_lahl], writes=[R_lahl])
                    op("dve", lambda e: e.tensor_copy(lalo[:, 0:ntile, :], latmp[:, 0:ntile, :]), reads=[R_lahl], writes=[R_lahl])
                    op("dve", lambda e: e.tensor_tensor(DI[:], identb[:, None, :].broadcast_to([128, 8, 128]),
                                                        cst[:, K_DSK + g * 8:K_DSK + g * 8 + 8].unsqueeze(2).broadcast_to([128, 8, 128]),
                                                        ALU.mult), reads=[R_c, R_c2], writes=[R_DI])
                    op("dve", lambda e: e.memset(sst[:], 0.0), writes=[R_sst])
                    def a4_gen():
                        for (ti, xsrc, xreg_, bsrc, breg_, dtap, trix, onx) in tlm:
                            bk = next_bank([0, 1])
                            op("pe", lambda e: e.matmul(PS[bk][:, 0:128], BT[:, ti * 128:(ti + 1) * 128],
                                                        CT[:, ti * 128:(ti + 1) * 128], start=True, stop=True),
                               reads=[R_BT, R_CT], writes=[PR[bk]])
                            op("dve", lambda e: e.tensor_tensor(cbm[:, ti, :], PS[bk][:, 0:128], trix, ALU.mult),
                               reads=[PR[bk], R_c], writes=[R_cbm])
                        def emit_L(n_):
                            (ti, xsrc, xreg_, bsrc, breg_, dtap, trix, onx) = tlm[n_]
                            L_ = Lb2[n_ % 2]
                            for q_, lsrc in enumerate((lahi, lalo)):
                                op("dve", lambda e: e.tensor_tensor(L_[:, q_, :, :], trix[:, None, :].broadcast_to([128, 8, 128]),
                                                                    lsrc[:, ti, :].unsqueeze(2).broadcast_to([128, 8, 128]), ALU.mult),
                                   reads=[R_lahl, R_c], writes=[LbR[n_ % 2]])
                        emit_L(0)
                        for n_, (ti, xsrc, xreg_, bsrc, breg_, dtap, trix, onx) in enumerate(tlm):
                            if n_ + 1 < ntile:
                                emit_L(n_ + 1)
                            L_, E_ = Lb2[n_ % 2], Eb2[n_ % 2]
                            bks = (4, 5) if n_ % 2 == 0 else (6, 3)
                            for hh in range(2):
                                bk = bks[hh]
                                for q_ in range(2):
                                    op("pe", lambda e: e.matmul(PS[bk][:, :], MSb[:], L_[:, q_, hh * 4:hh * 4 + 4, :].rearrange("p h l -> p (h l)"),
                                                                start=(q_ == 0), stop=(q_ == 1)), reads=[LbR[n_ % 2], R_c2], writes=[PR[bk]], inc=(q_ == 1))
                            for hh in range(2):
                                op("act", lambda e: e.activation(E_[:, hh * 512:(hh + 1) * 512], PS[bks[hh]][:, :], AF.Exp),
                                   reads=[PR[bks[hh]]], writes=[EbR[n_ % 2]])
                            op("dve", lambda e: e.tensor_tensor(MT[:, ti, :].rearrange("p (h l) -> p h l", l=128),
                                                                E_[:].rearrange("p (h l) -> p h l", l=128),
                                                                cbm[:, ti, None, :].broadcast_to([128, 8, 128]), ALU.mult),
                               reads=[EbR[n_ % 2], R_cbm], writes=[R_MT])
                            yield


                    def a6_gen():
                        def emit_mm(n_):
                            ti = tlm[n_][0]
                            i2 = n_ % 2
                            bd = 5 if i2 == 0 else 6
                            bo = (3 if i2 == 0 else 4) if ti < 4 else 7
                            for h in range(8):
                                op("pe", lambda e: e.matmul(PS[bd][:, h * 64:(h + 1) * 64], MT[:, ti, h * 128:(h + 1) * 128],
                                                            xdt[:, ti, h * 64:(h + 1) * 64], start=True, stop=False),
                                   reads=[R_MT, Bm_["R_xdt"]], writes=[PR[bd]], inc=False)
                                op("pe", lambda e: e.matmul(PS[bd][:, h * 64:(h + 1) * 64], DI[:, h, :],
                                                            xtk[:, ti, h * 64:(h + 1) * 64], start=False, stop=True),
                                   reads=[R_DI, Bm_["R_xtk"]], writes=[PR[bd]], inc=(h == 7))
                            if ti < 4:
                                op("pe", lambda e: e.matmul(PS[bo][:, :], CT[:, ti * 128:(ti + 1) * 128], hbf[:, ti, :],
                                                            start=True, stop=True), reads=[R_CT, Bm_["R_hbf"]], writes=[PR[bo]])
                        emit_mm(0)
                        for n_, (ti, xsrc, xreg_, bsrc, breg_, dtap, trix, onx) in enumerate(tlm):
                            i2 = n_ % 2
                            bd = 5 if i2 == 0 else 6
                            bo = (3 if i2 == 0 else 4) if ti < 4 else 7
                            if n_ + 1 < ntile:
                                emit_mm(n_ + 1)
                            Y = ytm[i2]
                            Yv = Y[:].rearrange("p (h q) -> p h q", q=64)
                            op("dve", lambda e: e.tensor_tensor(Yv, PS[bo][:, :].rearrange("p (h q) -> p h q", q=64),
                                                                pex[:, ti, 0:8].unsqueeze(2).broadcast_to([128, 8, 64]), ALU.mult),
                               reads=[PR[bo], Bm_["R_pex"]], writes=[ytR[i2]])
                            op("dve", lambda e: e.tensor_tensor(Y[:], Y[:], PS[bd][:, :], ALU.add), reads=[PR[bd]], writes=[ytR[i2]])
                            op("dve", lambda e: e.tensor_tensor(Y[:], Y[:], szs[:, ti, :], ALU.mult), reads=[R_szs], writes=[ytR[i2]])
                            op("act", lambda e: e.activation(junk2[:], Y[:], AF.Square, accum_out=sst[:, ti, 0:1]),
                               reads=[ytR[i2]], writes=[R_j2, R_sst])
                            op("act", lambda e: e.activation(sst[:, ti, 1:2], sst[:, ti, 0:1], AF.Ln, scale=1.0 / 512, bias=EPS),
                               reads=[R_sst], writes=[R_sst])
                            op("act", lambda e: e.activation(sst[:, ti, 2:3], sst[:, ti, 1:2], AF.Exp, scale=-0.5),
                               reads=[R_sst], writes=[R_sst])
                            op("act", lambda e: e.activation(ynb[i2][:], Y[:], AF.Copy, scale=sst[:, ti, 2:3]),
                               reads=[ytR[i2], R_sst], writes=[ynR[i2]])
                            bk = next_bank([0, 1])
                            pv = psb(bk)
                            for c in range(4):
                                op("pe", lambda e: e.transpose(pv[:, c * 128:(c + 1) * 128], ynb[i2][:, c * 128:(c + 1) * 128], identb[:]),
                                   reads=[ynR[i2], R_c], writes=[PR[bk]], inc=(c == 3))
                            op("dve", lambda e: e.tensor_tensor(yst[i2][:], pv[:, 0:512].rearrange("p (c t) -> p c t", t=128),
                                                                cst[:, K_SNW + g * 4:K_SNW + g * 4 + 4].unsqueeze(2).broadcast_to([128, 4, 128]),
                                                                ALU.mult), reads=[PR[bk], R_c], writes=[ystR[i2]])
                            dma("sp", ysd[g * 4:g * 4 + 4, :, goff + ti * 128:goff + (ti + 1) * 128].rearrange("c p t -> p c t"),
                                yst[i2][:], ystC[i2], reads=[ystR[i2]])
                            yield


                    a4 = a4_gen()
                    a6 = a6_gen()
                    if sbk == 0:
                        for _ in a4:
                            pass
                    if sbk == 1:
                        op("pool", lambda e: e.tensor_tensor(CTm[:], CT[:, None, 512:640].broadcast_to([128, 16, 128]),
                                                            selm[:].rearrange("p (b l) -> p b l", l=128), ALU.mult),
                           reads=[R_CT, R_c, R_c2], writes=[R_CTm])
                        op("pool", lambda e: e.tensor_tensor(Bmk[:], xtk[:, 4, None, 512:640].broadcast_to([128, 16, 128]),
                                                            BLK.unsqueeze(2).broadcast_to([128, 16, 128]), ALU.mult),
                           reads=[Bm_["R_xtk"], R_c], writes=[R_Bmk])
                        op("pool", lambda e: e.tensor_copy(larep[:], la[:, 4, :].unsqueeze(2).broadcast_to([128, 8, 64])),
                           reads=[Bm_["R_la"]], writes=[R_larep])
                        for jj in range(4):
                            op("pe", lambda e: e.matmul(PS[2][:, jj * 16:(jj + 1) * 16],
                                                        larep[:].rearrange("p h q -> p (h q)")[:, jj * 128:(jj + 1) * 128], BLK,
                                                        start=True, stop=True), reads=[R_larep, R_c], writes=[PR[2]], inc=(jj == 3))
                        op("act", lambda e: e.activation(cdT[:].rearrange("p j b -> p (j b)"), PS[2][:, 0:64], AF.Exp),
                           reads=[PR[2]], writes=[R_cdT])

                        def ld(b_):
                            i3_ = b_ % 3
                            dma("sp", h0[i3_][:], st_ssm[b_, g * 512:(g + 1) * 512, :].rearrange("(j p) n -> p j n", p=128),
                                h0C[i3_], writes=[h0R[i3_]])
                        def cast(b_):
                            op("act", lambda e: e.copy(h0b[b_ % 2][:], h0[b_ % 3][:].rearrange("p j n -> p (j n)")),
                               reads=[h0R[b_ % 3]], writes=[h0bR[b_ % 2]])
                        ld(0)
                        ld(1)
                        cast(0)
                        for b in range(16):
                            i2, i3 = b % 2, b % 3
                            if b + 2 < 16:
                                ld(b + 2)
                            if b + 1 < 16:
                                cast(b + 1)
                            bk = 0
                            pv = psb(bk)
                            for jj in range(4):
                                op("pe", lambda e: e.transpose(pv[:, jj * 128:(jj + 1) * 128], h0b[i2][:, jj * 128:(jj + 1) * 128], identb[:]),
                                   reads=[h0bR[i2], R_c2], writes=[PR[bk]], inc=(jj == 3))
                            op("act", lambda e: e.copy(h0T[i2][:], pv[:, 0:512]), reads=[PR[bk]], writes=[h0TR[i2]])
                            op("pe", lambda e: e.matmul(PS[7][:, :], CTm[:, b, :], h0T[i2][:], start=(b == 0), stop=(b == 15)),
                               reads=[R_CTm, h0TR[i2]], writes=[PR[7]], inc=True)
                            bk2 = 1
                            for jj in range(4):
                                op("pe", lambda e: e.matmul(PS[bk2][:, jj * 128:(jj + 1) * 128], xdd[:, 4, jj * 128:(jj + 1) * 128],
                                                            Bmk[:, b, :], start=True, stop=True),
                                   reads=[Bm_["R_xdd"], R_Bmk], writes=[PR[bk2]], inc=(jj == 3))
                            for jj in range(4):
                                op("dve", lambda e: e.scalar_tensor_tensor(hn[i2][:, jj, :], h0[i3][:, jj, :], cdT[:, jj, b:b + 1],
                                                                           PS[bk2][:, jj * 128:(jj + 1) * 128], ALU.mult, ALU.add),
                                   reads=[h0R[i3], R_cdT, PR[bk2]], writes=[hnR[i2]])
                            dma("sp", ssm_s_o[b, g * 512:(g + 1) * 512, :].rearrange("(j p) n -> p j n", p=128), hn[i2][:],
                                hnC[i2], reads=[hnR[i2]])
                            if b < 5:
                                next(a4, None)
                            elif b >= 6 and b % 2 == 0 and b <= 12:
                                next(a6, None)

                    for _ in a4:
                        pass
                    for _ in a6:
                        pass

                if sbk == 1:
                    for q in range(6):
                        bk = next_bank([0, 1, 2])
                        for c in range(4):
                            j = q * 4 + c
                            op("pe", lambda e: e.matmul(PS[bk][0:48, c * 128:(c + 1) * 128], csTs[:, j, :], IDF, start=True, stop=True),
                               reads=[R_css, R_c], writes=[PR[bk]], inc=(c == 3))
                        op("act", lambda e: e.copy(ytm[0][0:48, :], PS[bk][0:48, :]), reads=[PR[bk]], writes=[ytR[0]])
                        dma("sp", cs_s_o[:, q * 512:(q + 1) * 512], ytm[0][0:48, :], csC[0], reads=[ytR[0]])
                        bk = next_bank([0, 1, 2])
                        for c in range(4):
                            j = q * 4 + c
                            op("pe", lambda e: e.matmul(PS[bk][0:4, c * 128:(c + 1) * 128], csT[:, j, :], IDF, start=True, stop=True),
                               reads=[R_cs, R_c], writes=[PR[bk]], inc=(c == 3))
                        op("act", lambda e: e.copy(ytm[1][0:4, :], PS[bk][0:4, :]), reads=[PR[bk]], writes=[ytR[1]])
                        dma("sp", cs_p_o[:, q * 512:(q + 1) * 512], ytm[1][0:4, :], csC[1], reads=[ytR[1]])

            kb.barrier()
        kb.barrier()
        blocks_all = [(lambda k: hT[:, k, 0:512], 512, R_hT), (lambda k: hT[:, k, 512:1024], 512, R_hT),
                      (lambda k: hT[:, k, 1024:1152], 128, R_hT), (lambda k: hTh[:, k, 0:4], 4, R_hTh)]
        with ExitStack() as pb:
            cvp = sb("cvp", [128, 2, 4 + NP_TOK], F32, pb)
            cvs = sb("cvs", [128, 2, 16, 10], F32, pb)
            co = sb("co", [128, 2, NT], F32, pb)
            szb = [sb("szb%d" % i, [128, 512], BF16, pb) for i in range(2)]
            szR = [Reg(), Reg()]
            ycs = [sb("ycs%d" % i, [128, 512], BF16, pb) for i in range(2)]
            ycR = [Reg(), Reg()]
            R_cvp, R_co = Reg(), Reg()
            stch = sb("stch", [32, 2048], F32, pb)
            R_stch = Reg()
            dma("sp", stch[:], st_csh, cch, writes=[R_stch])
            rr = dict(n=0)
            BK6 = [0, 1, 2, 3, 4, 5]
            for jq in range(8):
                def job_c(slot, jq=jq):
                    for c in range(2):
                        j = jq * 2 + c
                        bk = next_bank([6, 7])
                        op("pe", lambda e: e.matmul(PS[bk][:, 0:32], stch[0:32, j * 128:(j + 1) * 128], IDF[0:32, 0:32],
                                                    start=True, stop=True), reads=[R_stch, R_c], writes=[PR[bk]])
                        op("act", lambda e: e.copy(cvs[:, c, :, 0:2], PS[bk][:, 0:32].rearrange("p (b t) -> p b t", t=2)),
                           reads=[PR[bk]], writes=[R_cvp])

                        def evac(bi, bk, N, c=c):
                            if bi == 3:
                                op("act", lambda e: e.copy(cvp[:, c, 0:4], PS[bk][:, 0:4]), reads=[PR[bk]], writes=[R_cvp])
                            elif bi < 2:
                                op("act", lambda e: e.copy(cvp[:, c, 4 + bi * 512:516 + bi * 512], PS[bk][:, 0:512]),
                                   reads=[PR[bk]], writes=[R_cvp])
                            else:
                                op("act", lambda e: e.copy(cvs[:, c, :, 2:10], PS[bk][:, 0:128].rearrange("p (b t) -> p b t", t=8)),
                                   reads=[PR[bk]], writes=[R_cvp])
                        proj_fm(slot, c * 128, blocks_all, BK6, evac)

                def job_v(slot, jq=jq):
                    for c in range(2):
                        j = jq * 2 + c
                        wv = cst[:, K_CHW + 3 * j:K_CHW + 3 * j + 3]

                        def evac(bi, bk, N, c=c):
                            if bi == 3:
                                d = cvp[:, c, 0:4]
                                op("dve", lambda e: e.tensor_tensor(d, d, PS[bk][:, 0:4], ALU.mult), reads=[PR[bk]], writes=[R_cvp])
                            elif bi < 2:
                                d = cvp[:, c, 4 + bi * 512:516 + bi * 512]
                                op("dve", lambda e: e.tensor_tensor(d, d, PS[bk][:, 0:512], ALU.mult), reads=[PR[bk]], writes=[R_cvp])
                            else:
                                d = cvs[:, c, :, 2:10]
                                op("dve", lambda e: e.tensor_tensor(d, d, PS[bk][:, 0:128].rearrange("p (b t) -> p b t", t=8), ALU.mult),
                                   reads=[PR[bk]], writes=[R_cvp])
                        proj_fm(slot, c * 128, blocks_all, BK6, evac)
                        op("pool", lambda e: e.tensor_copy(chT[:, j, :], cvp[:, c, NP_TOK:NP_TOK + 4]), reads=[R_cvp], writes=[R_ch])
                        op("pool", lambda e: e.tensor_copy(chTs[:, j, :].rearrange("p (b t) -> p b t", t=2), cvs[:, c, :, 8:10]),
                           reads=[R_cvp], writes=[R_chs])
                        for hb in range(2):
                            cp = co[:, c, hb * 512:(hb + 1) * 512]
                            o_ = 2 + hb * 512
                            op("dve", lambda e: e.tensor_scalar(cp, cvp[:, c, o_:o_ + 512], wv[:, 0:1], None, ALU.mult),
                               reads=[R_cvp, R_c], writes=[R_co])
                            for k in (1, 2):
                                op("dve", lambda e: e.scalar_tensor_tensor(cp, cvp[:, c, o_ + k:o_ + k + 512], wv[:, k:k + 1], cp,
                                                                           ALU.mult, ALU.add), reads=[R_cvp, R_c], writes=[R_co])
                        cs_ = co[:, c, NP_TOK:NT].rearrange("p (b t) -> p b t", t=8)
                        op("dve", lambda e: e.tensor_scalar(cs_, cvs[:, c, :, 0:8], wv[:, 0:1], None, ALU.mult),
                           reads=[R_cvp, R_c], writes=[R_co])
                        for k in (1, 2):
                            op("dve", lambda e: e.scalar_tensor_tensor(cs_, cvs[:, c, :, k:k + 8], wv[:, k:k + 1], cs_,
                                                                       ALU.mult, ALU.add), reads=[R_cvp, R_c, R_co], writes=[R_co])

                def job_b(slot, jq=jq):
                    for c in range(2):
                        def evac(bi, bk, N, c=c):
                            d = co[:, c, bi * 512:bi * 512 + N]
                            op("dve", lambda e: e.tensor_tensor(d, d, PS[bk][:, 0:N], ALU.mult), reads=[PR[bk]], writes=[R_co])
                        proj_fm(slot, c * 128, blocks_all[:3], BK6, evac)

                def job_z(slot, jq=jq):
                    for c in range(2):
                        j = jq * 2 + c

                        def evac(bi, bk, N, c=c, j=j):
                            i2 = rr["n"] % 2
                            rr["n"] += 1
                            op("act", lambda e: e.activation(szb[i2][:, 0:N], PS[bk][:, 0:N], AF.Silu),
                               reads=[PR[bk]], writes=[szR[i2]])
                            op("pool", lambda e: e.tensor_tensor(ycs[i2][:, 0:N], co[:, c, bi * 512:bi * 512 + N],
                                                                 szb[i2][:, 0:N], ALU.mult),
                               reads=[R_co, szR[i2]], writes=[ycR[i2]])
                            dma("sp", ysd[16 + j, :, bi * 512:bi * 512 + N], ycs[i2][:, 0:N], ystC[i2], reads=[ycR[i2]])
                        proj_fm(slot, c * 128, blocks_all[:3], BK6, evac)

                run_jobs([(C_CC + jq * 256, 256, job_c), (C_VC + jq * 256, 256, job_v),
                          (C_BC + jq * 256, 256, job_b), (C_ZC + jq * 256, 256, job_z)])
            for q in range(4):
                bk = next_bank([0, 1, 2, 3])
                for c in range(4):
                    j = q * 4 + c
                    op("pe", lambda e: e.matmul(PS[bk][0:32, c * 128:(c + 1) * 128], chTs[:, j, :], IDF, start=True, stop=True),
                       reads=[R_chs, R_c], writes=[PR[bk]], inc=(c == 3))
                op("act", lambda e: e.copy(co[0:32, 0, 0:512], PS[bk][0:32, :]), reads=[PR[bk]], writes=[R_co])
                dma("sp", csh_s_o[:, q * 512:(q + 1) * 512], co[0:32, 0, 0:512], ysC, reads=[R_co])
                bk = next_bank([0, 1, 2, 3])
                for c in range(4):
                    j = q * 4 + c
                    op("pe", lambda e: e.matmul(PS[bk][0:4, c * 128:(c + 1) * 128], chT[:, j, :], IDF, start=True, stop=True),
                       reads=[R_ch, R_c], writes=[PR[bk]], inc=(c == 3))
                op("act", lambda e: e.copy(co[0:4, 1, 0:512], PS[bk][0:4, :]), reads=[PR[bk]], writes=[R_co])
                dma("sp", csh_p_o[:, q * 512:(q + 1) * 512], co[0:4, 1, 0:512], ysC, reads=[R_co])
        kb.barrier()
        for sbk in range(2):
            NTL = 512 if sbk == 0 else 640
            ntile = 4 if sbk == 0 else 5
            goff = sbk * 512
            xrow0 = 1024 + sbk * 512
            with ExitStack() as pc:
                fnw = sb("fnw", [128, DM], F32, pc)
                R_fnw = Reg()
                dma("sp", fnw[:], fnw_d, cch, writes=[R_fnw])
                ypre = sb("ypre", [128, 5, DM], F32, pc)
                R_yp = [Reg() for _ in range(5)]
                NOB = 3
                wo = [WB[i][:].rearrange("p k c -> p (k c)").rearrange("p (k c) -> p k c", c=512) for i in range(NOB)]
                woR = WR
                woC = WC
                ytf = sb("ytf", [128, 32, 640], BF16, pc)
                R_ytf = Reg()
                ytpC = [xc[0], xc[1]]
                xres = [sb("xres%d" % i, [128, DM], F32, pc) for i in range(2)]
                xrR = [Reg(), Reg()]
                xrC = [xc[2], hfC]
                junk3 = sb("junk3", [128, DM], BF16, pc)
                R_j3 = Reg()
                sso = sb("sso", [128, 5, 4], F32, pc)
                R_sso = Reg()
                seq = [(n, eg) for n in range(4) for eg in range(4)]

                def c_load(q):
                    n, eg = seq[q]
                    i = q % NOB
                    po = (n * 4 + eg) * 4096
                    src = w_out[:, po:po + 4096].rearrange("p (k c) -> p k c", c=512)
                    dma("pool", wo[i], src, woC[i], writes=[woR[i]])
                for ch_ in ystC:
                    nc.sync.wait_ge(kb.semh[ch_.sem], ch_.cnt)
                c_load(0)
                c_load(1)
                for eg_ in range(4):
                    dma("sp", ytf[:, eg_ * 8:(eg_ + 1) * 8, 0:NTL], ysd[eg_ * 8:(eg_ + 1) * 8, :, goff:goff + NTL].rearrange("c p t -> p c t"),
                        ytpC[0], writes=[R_ytf])
                for q, (n, eg) in enumerate(seq):
                    i = q % NOB
                    if q + 2 < len(seq):
                        c_load(q + 2)
                    for t in range(ntile):
                        for k in range(8):
                            e_ = eg * 8 + k
                            op("pe", lambda e: e.matmul(PS[t][:, :], ytf[:, e_, t * 128:(t + 1) * 128], wo[i][:, k, :],
                                                        start=(e_ == 0), stop=(e_ == 31)),
                               reads=[R_ytf, woR[i]], writes=[PR[t]], inc=(k == 7))
                    if eg == 3:
                        for t in range(ntile):
                            if t % 2 == 0:
                                op("act", lambda e: e.copy(ypre[:, t, n * 512:(n + 1) * 512], PS[t][:, :]),
                                   reads=[PR[t]], writes=[R_yp[t]])
                            else:
                                op("dve", lambda e: e.tensor_copy(ypre[:, t, n * 512:(n + 1) * 512], PS[t][:, :]),
                                   reads=[PR[t]], writes=[R_yp[t]])
                op("pool", lambda e: e.memset(sso[:], 0.0), writes=[R_sso])
                for t in range(ntile):
                    i2 = t % 2
                    r0 = (xrow0 + t * 128) if t < 4 else 2048
                    o0 = (goff + t * 128) if t < 4 else 1024
                    dma("sp", xres[i2][:], xall[r0:r0 + 128, :], xrC[i2], writes=[xrR[i2]])
                    Y = ypre[:, t, :]
                    op("dve", lambda e: e.tensor_tensor(Y, Y, xres[i2][:], ALU.add), reads=[xrR[i2]], writes=[R_yp[t]])
                    op("act", lambda e: e.activation(junk3[:], Y, AF.Square, accum_out=sso[:, t, 0:1]),
                       reads=[R_yp[t]], writes=[R_j3, R_sso])
                    op("act", lambda e: e.activation(sso[:, t, 1:2], sso[:, t, 0:1], AF.Ln, scale=1.0 / DM, bias=EPS),
                       reads=[R_sso], writes=[R_sso])
                    op("act", lambda e: e.activation(sso[:, t, 2:3], sso[:, t, 1:2], AF.Exp, scale=-0.5), reads=[R_sso], writes=[R_sso])
                    op("dve", lambda e: e.scalar_tensor_tensor(xres[i2][:], Y, sso[:, t, 2:3], fnw[:], ALU.mult, ALU.mult),
                       reads=[R_yp[t], R_sso, R_fnw], writes=[xrR[i2]])
                    dma("sp", y_o[o0:o0 + 128, :], xres[i2][:], hnCh[i2], reads=[xrR[i2]])
            kb.barrier()
        for ch in kb.chans:
            if ch.cnt:
                nc.sync.wait_ge(kb.semh[ch.sem], ch.cnt)
    return nc


_NC_CACHE = {}


def _host_consts():
    l = np.arange(128)
    ident = np.eye(128, dtype=np.float32)
    tri = (l[:, None] <= l[None, :]).astype(np.float32)
    mstrict = (l[:, None] > l[None, :]).astype(np.float32)
    same = (l[:, None] // 8 == l[None, :] // 8).astype(np.float32)
    trib = tri * same
    blk = (l[:, None] // 8 == np.arange(16)[None, :]).astype(np.float32)
    msk = np.concatenate([ident, tri, mstrict, trib, same, blk], axis=1).astype(np.float32)
    selrow = (np.arange(16)[:, None] == (l[None, :] // 8)).astype(np.float32).reshape(1, 2048)
    selm = np.ascontiguousarray(np.broadcast_to(selrow, (128, 2048))).astype(np.float32)
    return msk, selm


def kernel(x_prompt, x_sample, state_ssm, state_conv_ssd, state_conv_short, norm_w, w_in, conv_ssd_w, conv_ssd_b,
           dt_bias, a_log, d_skip, ssd_norm_w, conv_short_w, w_out, final_norm_w):
    f = lambda a: np.ascontiguousarray(np.asarray(a, dtype=np.float32))
    x_prompt, x_sample = f(x_prompt), f(x_sample)
    state_ssm, state_conv_ssd, state_conv_short = f(state_ssm), f(state_conv_ssd), f(state_conv_short)
    w_in0, w_out0 = f(w_in)[0], f(w_out)[0]
    blocks = []
    for seg in (C_ZS, C_XS):
        blocks += [(seg + i * 256, 256) for i in range(8)]
    blocks += [(C_B + g * 128, 128) for g in range(4)] + [(C_C + g * 128, 128) for g in range(4)] + [(C_DT, 32)]
    for seg in (C_ZC, C_BC, C_CC, C_VC):
        blocks += [(seg + i * 256, 256) for i in range(8)]
    w3 = w_in0.reshape(16, 128, DIN)
    wpk = np.empty((128, 16 * DIN), np.float32)
    for (c0, ncl) in blocks:
        wpk[:, 16 * c0:16 * (c0 + ncl)] = w3[:, :, c0:c0 + ncl].transpose(1, 0, 2).reshape(128, 16 * ncl)
    w_in0 = wpk
    w_out0 = np.ascontiguousarray(w_out0.reshape(4, 8, 128, 4, 512).transpose(2, 3, 0, 1, 4).reshape(128, 16 * 4096))
    msk, selm = _host_consts()
    cstb = np.zeros((128, K_END), np.float32)
    cstb[:, K_NW:K_NW + 16] = f(norm_w)[0].reshape(16, 128).T
    cstb[:, K_CSW:K_CSW + 96] = f(conv_ssd_w)[0].reshape(4, 24, 128).transpose(2, 1, 0).reshape(128, 96)
    cstb[:, K_CSB:K_CSB + 24] = f(conv_ssd_b)[0].reshape(24, 128).T
    cstb[:, K_CHW:K_CHW + 48] = f(conv_short_w)[0].reshape(3, 16, 128).transpose(2, 1, 0).reshape(128, 48)
    cstb[:, K_SNW:K_SNW + 16] = f(ssd_norm_w)[0].reshape(16, 128).T
    cstb[:, K_DTB:K_DTB + 32] = f(dt_bias)[0][None, :]
    cstb[:, K_ALOG:K_ALOG + 32] = f(a_log)[0][None, :]
    cstb[:, K_DSK:K_DSK + 32] = f(d_skip)[0][None, :]
    fnw = np.ascontiguousarray(np.broadcast_to(f(final_norm_w)[None, :], (128, DM)))
    in_maps = []
    for c in range(8):
        b, half = c // 2, c % 2
        xall = np.zeros((2180, DM), np.float32)
        if half == 1:
            xall[0:1024] = x_prompt[b, 0:1024]
            xall[2176:2180] = x_prompt[b, 1020:1024]
        xall[1024:2048] = x_prompt[b, half * 1024:(half + 1) * 1024]
        xall[2048:2176] = x_sample[16 * c:16 * c + 16].reshape(128, DM)
        cc = cstb.copy()
        cc[:, K_FLAG] = float(half)
        in_maps.append({
            "xall": xall,
            "st_ssm": np.ascontiguousarray(state_ssm[0, 16 * c:16 * c + 16].reshape(16, 2048, 128)),
            "st_cs": np.ascontiguousarray(state_conv_ssd[0, 16 * c:16 * c + 16].reshape(48, 3072)),
            "st_csh": np.ascontiguousarray(state_conv_short[0, 16 * c:16 * c + 16].reshape(32, 2048)),
            "w_in": w_in0, "w_out": w_out0, "cst": cc, "msk": msk, "selm": selm, "fnw": fnw,
        })
    if "nc" not in _NC_CACHE:
        _NC_CACHE["nc"] = build_nc()
    res = run_bass_kernel_spmd(_NC_CACHE["nc"], in_maps, core_ids=list(range(8)))
    R = res.results
    y_prompt = np.zeros((4, 2048, DM), np.float32)
    y_sample = np.zeros((128, 8, DM), np.float32)
    ssm_p = np.zeros((1, 4, 32, 64, 128), np.float32)
    cs_p = np.zeros((1, 4, 3, 3072), np.float32)
    csh_p = np.zeros((1, 4, 2, 2048), np.float32)
    ssm_s = np.zeros((1, 128, 32, 64, 128), np.float32)
    cs_s = np.zeros((1, 128, 3, 3072), np.float32)
    csh_s = np.zeros((1, 128, 2, 2048), np.float32)
    for c in range(8):
        b, half = c // 2, c % 2
        r = R[c]
        y_prompt[b, half * 1024:(half + 1) * 1024] = r["y"][0:1024]
        y_sample[16 * c:16 * c + 16] = r["y"][1024:1152].reshape(16, 8, DM)
        ssm_s[0, 16 * c:16 * c + 16] = r["ssm_s"].reshape(16, 32, 64, 128)
        cs_s[0, 16 * c:16 * c + 16] = r["cs_s"].reshape(16, 3, 3072)
        csh_s[0, 16 * c:16 * c + 16] = r["csh_s"].reshape(16, 2, 2048)
        if half == 1:
            ssm_p[0, b] = r["ssm_p"].reshape(32, 64, 128)
            cs_p[0, b] = r["cs_p"][1:4]
            csh_p[0, b] = r["csh_p"][2:4]
    return (y_prompt, y_sample, ssm_p, cs_p, csh_p, ssm_s, cs_s, csh_s)
```

```python
import numpy as np
from contextlib import ExitStack
import concourse.bass as bass
import concourse.mybir as mybir
from concourse.bass_utils import run_bass_kernel_spmd

F32 = mybir.dt.float32
BF16 = mybir.dt.bfloat16
AF = mybir.ActivationFunctionType
ALU = mybir.AluOpType

DM = 2048
NP_TOK = 1024
NS_TOK = 128
NT = NP_TOK + NS_TOK
DIN = 13344
C_ZS, C_XS, C_B, C_C, C_DT, C_ZC, C_BC, C_CC, C_VC = 0, 2048, 4096, 4608, 5120, 5152, 7200, 9248, 11296
EPS = 1e-6
K_NW, K_CSW, K_CSB, K_CHW, K_SNW, K_DTB, K_ALOG, K_DSK, K_FLAG, K_END = 0, 16, 112, 136, 184, 200, 232, 264, 296, 297
M_ID, M_TRI, M_MS, M_TRIB, M_ONEB, M_BLK, M_END = 0, 128, 256, 384, 512, 640, 656


class Reg:
    __slots__ = ("w", "r")

    def __init__(self):
        self.w = None
        self.r = {}


class Chan:
    def __init__(self, sem, name):
        self.sem = sem
        self.name = name
        self.cnt = 0


class KB:
    def __init__(self, nc, es):
        self.nc = nc
        self.es = es
        self.semh = {}
        self.E = {}
        for name, h in (("pe", nc.tensor), ("act", nc.scalar), ("dve", nc.vector),
                        ("pool", nc.gpsimd), ("sp", nc.sync)):
            s = es.enter_context(nc.semaphore("e_" + name))
            self.semh["e_" + name] = s
            self.E[name] = dict(h=h, sem="e_" + name, cnt=0, waited={})
        self.nch = 0
        self.chans = []

    def chan(self):
        n = "c%d" % self.nch
        self.nch += 1
        s = self.es.enter_context(self.nc.semaphore(n))
        self.semh[n] = s
        c = Chan(n, n)
        self.chans.append(c)
        return c

    def _waits(self, e, reads, writes, skip=None):
        need = {}
        own = e["sem"]
        for d in reads:
            if d.w is not None:
                if d.w[0] == own and own == "e_pe":
                    continue
                need[d.w[0]] = max(need.get(d.w[0], 0), d.w[1])
        for d in writes:
            if d.w is not None and d.w[0] != own:
                need[d.w[0]] = max(need.get(d.w[0], 0), d.w[1])
            for s, v in d.r.items():
                if s != own:
                    need[s] = max(need.get(s, 0), v)
        for s, v in need.items():
            if s == skip:
                continue
            if e["waited"].get(s, 0) < v:
                e["h"].wait_ge(self.semh[s], v)
                e["waited"][s] = v

    def _mark(self, tok, reads, writes):
        for d in reads:
            d.r[tok[0]] = max(d.r.get(tok[0], 0), tok[1])
        for d in writes:
            d.w = tok
            d.r = {}

    def op(self, eng, fn, reads=(), writes=(), inc=True):
        e = self.E[eng]
        self._waits(e, reads, writes)
        ins = fn(e["h"])
        if inc:
            ins.then_inc(self.semh[e["sem"]], 1)
            e["cnt"] += 1
            tok = (e["sem"], e["cnt"])
        else:
            tok = (e["sem"], e["cnt"] + 1)
        self._mark(tok, reads, writes)

    def barrier(self):
        for name, e in self.E.items():
            for n2, e2 in self.E.items():
                if n2 != name and e2["cnt"] > e["waited"].get(e2["sem"], 0):
                    e["h"].wait_ge(self.semh[e2["sem"]], e2["cnt"])
                    e["waited"][e2["sem"]] = e2["cnt"]
            for ch in self.chans:
                if ch.cnt > e["waited"].get(ch.sem, 0):
                    e["h"].wait_ge(self.semh[ch.sem], ch.cnt)
                    e["waited"][ch.sem] = ch.cnt

    def dma(self, q, out, in_, ch, reads=(), writes=()):
        e = self.E[q]
        self._waits(e, reads, writes, skip=ch.sem)
        e["h"].dma_start(out=out, in_=in_).then_inc(self.semh[ch.sem], 16)
        ch.cnt += 16
        self._mark((ch.sem, ch.cnt), reads, writes)


def build_nc():
    nc = bass.Bass("TRN2", target_bir_lowering=False)

    def din(n, s):
        return nc.dram_tensor(n, s, F32, kind="ExternalInput").ap()

    def dout(n, s):
        return nc.dram_tensor(n, s, F32, kind="ExternalOutput").ap()

    xall = din("xall", [2180, DM])
    st_ssm = din("st_ssm", [16, 2048, 128])
    st_cs = din("st_cs", [48, 3072])
    st_csh = din("st_csh", [32, 2048])
    w_in = din("w_in", [128, 16 * DIN])
    w_out = din("w_out", [128, 16 * 4096])
    cst_d = din("cst", [128, K_END])
    msk_d = din("msk", [128, M_END])
    selm_d = din("selm", [128, 2048])
    fnw_d = din("fnw", [128, DM])
    y_o = dout("y", [NT, DM])
    ssm_p_o = dout("ssm_p", [2048, 128])
    cs_p_o = dout("cs_p", [4, 3072])
    csh_p_o = dout("csh_p", [4, 2048])
    ssm_s_o = dout("ssm_s", [16, 2048, 128])
    cs_s_o = dout("cs_s", [48, 3072])
    csh_s_o = dout("csh_s", [32, 2048])
    ysd = nc.dram_tensor("ysd", [32, 128, NT], BF16, kind="Internal").ap()

    with ExitStack() as es:
        kb = KB(nc, es)
        op, dma = kb.op, kb.dma

        nmc = dict(n=0)

        def sb(name, shape, dt, stack=es):
            nmc["n"] += 1
            return stack.enter_context(nc.sbuf_tensor("s%d_%s" % (nmc["n"], name), shape, dt))

        PS = [es.enter_context(nc.psum_tensor("ps%d" % i, [128, 512], F32)) for i in range(8)]
        PR = [Reg() for _ in range(8)]

        def psb(i):
            return PS[i][:].bitcast(BF16)

        cst = sb("cst", [128, K_END], F32)
        msk = sb("msk", [128, M_END], F32)
        identb = sb("identb", [128, 128], BF16)
        selm = sb("selm", [128, 2048], BF16)
        abc = sb("abc", [128, 32], F32)
        ones = sb("ones", [128, 128], F32)
        R_c = Reg()
        cch = kb.chan()
        cchp = kb.chan()
        R_c2 = Reg()
        dma("sp", cst[:], cst_d, cch, writes=[R_c])
        dma("sp", msk[:], msk_d, cch, writes=[R_c])
        dma("pool", identb[:], msk_d[:, M_ID:M_ID + 128], cchp, writes=[R_c2])
        dma("pool", selm[:], selm_d, cchp, writes=[R_c2])
        op("act", lambda e: e.activation(abc[:], cst[:, K_ALOG:K_ALOG + 32], AF.Exp), reads=[R_c, R_c2], writes=[R_c])
        op("dve", lambda e: e.tensor_scalar(abc[:], abc[:], -1.0, None, ALU.mult), reads=[R_c], writes=[R_c])
        op("dve", lambda e: e.memset(ones[:], 1.0), writes=[R_c])
        MSb = sb("MSb", [128, 128], BF16)
        op("dve", lambda e: e.tensor_copy(MSb[:], msk[:, M_MS:M_MS + 128]), reads=[R_c], writes=[R_c2])
        IDF = msk[:, M_ID:M_ID + 128]
        TRI = msk[:, M_TRI:M_TRI + 128]
        MS = msk[:, M_MS:M_MS + 128]
        TRIB = msk[:, M_TRIB:M_TRIB + 128]
        ONEB = msk[:, M_ONEB:M_ONEB + 128]
        BLK = msk[:, M_BLK:M_BLK + 16]

        dtm = sb("dtm", [128, 5, 32], F32)
        dtq = sb("dtq", [128, 8, 32], F32)
        R_dtm, R_dtq = Reg(), Reg()
        hmid = sb("hmid", [128, 4, 512], F32)
        R_hmid = Reg()
        csT = sb("csT", [128, 24, 4], F32)
        csTs = sb("csTs", [128, 24, 48], F32)
        chT = sb("chT", [128, 16, 4], F32)
        chTs = sb("chTs", [128, 16, 32], F32)
        R_cs, R_css, R_ch, R_chs = Reg(), Reg(), Reg(), Reg()

        NWB = 3
        WB = [sb("wb%d" % i, [128, 16, 256], BF16) for i in range(NWB)]
        WR = [Reg() for _ in range(NWB)]
        WC = [kb.chan() for _ in range(NWB)]
        wstate = dict(n=0)

        def w_load(col0, ncols):
            i = wstate["n"] % NWB
            wstate["n"] += 1
            src = w_in[:, 16 * col0:16 * (col0 + ncols)].rearrange("p (k c) -> p k c", c=ncols)
            for kk in range(0, 16, 8):
                dma("pool", WB[i][:, kk:kk + 8, 0:ncols], src[:, kk:kk + 8, :], WC[i], writes=[WR[i]])
            return i

        WSEQ = [(C_DT, 32)]
        for g_ in range(4):
            WSEQ += [(C_B + g_ * 128, 128), (C_XS + g_ * 512, 256), (C_XS + g_ * 512 + 256, 256)]
        for sb_ in range(2):
            WSEQ.append((C_DT, 32))
            for g_ in range(4):
                WSEQ += [(C_B + g_ * 128, 128), (C_C + g_ * 128, 128), (C_XS + g_ * 512, 256), (C_XS + g_ * 512 + 256, 256),
                         (C_ZS + g_ * 512, 256), (C_ZS + g_ * 512 + 256, 256)]
        for jq_ in range(8):
            WSEQ += [(C_CC + jq_ * 256, 256), (C_VC + jq_ * 256, 256), (C_BC + jq_ * 256, 256), (C_ZC + jq_ * 256, 256)]
        wq = dict(issued=0, used=0, slots={})
        PFW = 2

        def w_acquire(c0, ncl):
            k = wq["used"]
            assert WSEQ[k] == (c0, ncl), (k, WSEQ[k], c0, ncl)
            while wq["issued"] < min(len(WSEQ), k + 1 + PFW):
                q = wq["issued"]
                wq["slots"][q] = w_load(*WSEQ[q])
                wq["issued"] += 1
            wq["used"] += 1
            return wq["slots"].pop(k)

        def run_jobs(jobs, PF=2):
            slots = {}
            for j in range(min(PF, len(jobs))):
                slots[j] = w_load(jobs[j][0], jobs[j][1])
            for j, (c0, ncl, fn) in enumerate(jobs):
                if j + PF < len(jobs):
                    slots[j + PF] = w_load(jobs[j + PF][0], jobs[j + PF][1])
                fn(slots.pop(j))

        bank_rr = dict(n=0)

        def next_bank(pool):
            b = pool[bank_rr["n"] % len(pool)]
            bank_rr["n"] += 1
            return b

        xc = [kb.chan() for _ in range(3)]
        hfC = kb.chan()
        ysC = kb.chan()
        csC = [kb.chan(), kb.chan()]
        hnCh = [kb.chan(), kb.chan()]
        ystC = [kb.chan(), kb.chan()]
        R_ysd = Reg()

        def phase0(tiles):
            with ExitStack() as p0:
                xb_ = [sb("xt%d" % i, [128, DM], F32, p0) for i in range(3)]
                xr = [Reg() for _ in range(3)]
                junk = sb("junk0", [128, DM], BF16, p0)
                xn = [sb("xn%d" % i, [128, DM], BF16, p0) for i in range(2)]
                xnr = [Reg(), Reg()]
                ssq = sb("ssq0", [128, 32], F32, p0)
                R_ss, R_junk = Reg(), Reg()
                op("dve", lambda e: e.memset(ssq[:], 0.0), writes=[R_ss])
                for i, (r0, nr, dst, dreg, c0) in enumerate(tiles):
                    xt, xreg = xb_[i % 3], xr[i % 3]
                    dma("sp", xt[0:nr, :], xall[r0:r0 + nr, :], xc[i % 3], writes=[xreg])
                    op("act", lambda e: e.activation(junk[0:nr, :], xt[0:nr, :], AF.Square,
                                                     accum_out=ssq[0:nr, 3 * i:3 * i + 1]),
                       reads=[xreg], writes=[R_junk, R_ss])
                    op("act", lambda e: e.activation(ssq[0:nr, 3 * i + 1:3 * i + 2], ssq[0:nr, 3 * i:3 * i + 1], AF.Ln,
                                                     scale=1.0 / DM, bias=EPS), reads=[R_ss], writes=[R_ss])
                    op("act", lambda e: e.activation(ssq[0:nr, 3 * i + 2:3 * i + 3], ssq[0:nr, 3 * i + 1:3 * i + 2], AF.Exp,
                                                     scale=-0.5), reads=[R_ss], writes=[R_ss])
                    xnt, xnreg = xn[i % 2], xnr[i % 2]
                    op("dve", lambda e: e.tensor_scalar(xnt[0:nr, :], xt[0:nr, :], ssq[0:nr, 3 * i + 2:3 * i + 3], None, ALU.mult),
                       reads=[xreg, R_ss], writes=[xnreg])
                    for hf in range(2):
                        bk = (2 * i + hf) % 4
                        pv = psb(bk)
                        for k8 in range(8):
                            k = hf * 8 + k8
                            op("pe", lambda e: e.transpose(pv[:, k8 * 128:k8 * 128 + nr], xnt[0:nr, k * 128:(k + 1) * 128],
                                                           identb[0:nr, 0:nr]),
                               reads=[xnreg, R_c], writes=[PR[bk]], inc=(k8 == 7))
                        src = pv.rearrange("p (a b) -> p a b", b=128)[:, :, 0:nr]
                        nwv = cst[:, K_NW + hf * 8:K_NW + hf * 8 + 8].unsqueeze(2).broadcast_to([128, 8, nr])
                        op("dve", lambda e: e.tensor_tensor(dst[:, hf * 8:hf * 8 + 8, c0:c0 + nr], src, nwv, ALU.mult),
                           reads=[PR[bk], R_c], writes=[dreg])
            kb.barrier()

        def proj_fm(slot, cofs, blocks, pool, evac):
            for bi, (rhs, N, hreg) in enumerate(blocks):
                bk = next_bank(pool)
                for k in range(16):
                    op("pe", lambda e: e.matmul(PS[bk][:, 0:N], WB[slot][:, k, cofs:cofs + 128], rhs(k),
                                                start=(k == 0), stop=(k == 15)),
                       reads=[WR[slot], hreg], writes=[PR[bk]], inc=(k == 15))
                evac(bi, bk, N)

        def job_dt(src, hreg, ntl, dstt, dreg, tcol_=None):
            def fn(slot):
                bk = 6
                for t in range(ntl):
                    for k in range(16):
                        c_ = tcol_[t] if tcol_ else t * 128
                        op("pe", lambda e: e.matmul(PS[bk][:, t * 32:(t + 1) * 32], src[:, k, c_:c_ + 128],
                                                    WB[slot][:, k, 0:32], start=(k == 0), stop=(k == 15)),
                           reads=[WR[slot], hreg], writes=[PR[bk]], inc=(k == 15 and t == ntl - 1))
                dv = dstt[:, 0:ntl, :]
                pv = PS[bk][:, 0:ntl * 32].rearrange("p (t h) -> p t h", h=32)
                op("dve", lambda e: e.tensor_tensor(dv, pv, cst[:, None, K_DTB:K_DTB + 32].broadcast_to([128, ntl, 32]), ALU.add),
                   reads=[PR[bk], R_c], writes=[dreg])
                op("act", lambda e: e.activation(dv, dv, AF.Exp), reads=[dreg], writes=[dreg])
                op("act", lambda e: e.activation(dv, dv, AF.Ln, bias=1.0), reads=[dreg], writes=[dreg])
            return fn

        def stage123(B, tl, g, mode):
            hs = slice(g * 8, g * 8 + 8)
            ntl = len(tl)
            xtk, xdt, xdd, la, pex = B["xtk"], B["xdt"], B["xdd"], B["la"], B["pex"]
            for (ti, xsrc, xreg_, bsrc, breg_, dtap, trix, onx) in tl:
                bk = next_bank([0, 1])
                pv = psb(bk)
                for c in range(5):
                    srcap, sreg = (xsrc(c), xreg_) if c < 4 else (bsrc, breg_)
                    op("pe", lambda e: e.transpose(pv[:, c * 128:(c + 1) * 128], srcap, identb[:]),
                       reads=[sreg, R_c], writes=[PR[bk]], inc=(c == 4))
                op("dve", lambda e: e.tensor_copy(xtk[:, ti, :], pv[:, 0:640]), reads=[PR[bk]], writes=[B["R_xtk"]])
            for (ti, xsrc, xreg_, bsrc, breg_, dtap, trix, onx) in tl:
                op("dve", lambda e: e.tensor_tensor(la[:, ti, :], dtap, abc[:, hs], ALU.mult),
                   reads=[R_dtm, R_dtq, R_c], writes=[B["R_la"]])
            for n_, (ti, xsrc, xreg_, bsrc, breg_, dtap, trix, onx) in enumerate(tl):
                op("pe", lambda e: e.matmul(PS[2][:, ti * 16:ti * 16 + 8], trix, la[:, ti, :], start=True, stop=True),
                   reads=[B["R_la"], R_c], writes=[PR[2]], inc=False)
                op("pe", lambda e: e.matmul(PS[2][:, ti * 16 + 8:ti * 16 + 16], onx, la[:, ti, :], start=True, stop=True),
                   reads=[B["R_la"], R_c], writes=[PR[2]], inc=(n_ == ntl - 1))
            t0, t1 = tl[0][0], tl[-1][0] + 1
            pvv = PS[2][:, t0 * 16:t1 * 16].rearrange("p (t c) -> p t c", c=16)
            op("dve", lambda e: e.tensor_copy(pex[:, t0:t1, 0:16], pvv), reads=[PR[2]], writes=[B["R_pex"]])
            op("dve", lambda e: e.tensor_tensor(pex[:, t0:t1, 16:24], pex[:, t0:t1, 8:16], pex[:, t0:t1, 0:8], ALU.subtract),
               reads=[B["R_pex"]], writes=[B["R_pex"]])
            op("act", lambda e: e.activation(pex[:, t0:t1, :], pex[:, t0:t1, :], AF.Exp), reads=[B["R_pex"]], writes=[B["R_pex"]])
            for (ti, xsrc, xreg_, bsrc, breg_, dtap, trix, onx) in tl:
                xv = xtk[:, ti, 0:512].rearrange("p (h q) -> p h q", q=64)
                op("dve", lambda e: e.tensor_tensor(xdt[:, ti, :].rearrange("p (h q) -> p h q", q=64), xv,
                                                    dtap.unsqueeze(2).broadcast_to([128, 8, 64]), ALU.mult),
                   reads=[B["R_xtk"], R_dtm, R_dtq], writes=[B["R_xdt"]])
                op("dve", lambda e: e.tensor_tensor(xdd[:, ti, :].rearrange("p (h q) -> p h q", q=64),
                                                     xdt[:, ti, :].rearrange("p (h q) -> p h q", q=64),
                                                     pex[:, ti, 16:24].unsqueeze(2).broadcast_to([128, 8, 64]), ALU.mult),
                   reads=[B["R_xdt"], B["R_pex"]], writes=[B["R_xdd"]])

        def state_scan(B, tl, init, need_hbf=True):
            hcur, hbf, xtk, xdd, pex = B["hcur"], B["hbf"], B["xtk"], B["xdd"], B["pex"]
            init()
            for (ti, xsrc, xreg_, bsrc, breg_, dtap, trix, onx) in tl:
                if need_hbf:
                    op("act", lambda e: e.copy(hbf[:, ti, :], hcur[:]), reads=[B["R_hcur"]], writes=[B["R_hbf"]])
                bk = next_bank([3, 4])
                op("pe", lambda e: e.matmul(PS[bk][:, :], xtk[:, ti, 512:640], xdd[:, ti, :], start=True, stop=True),
                   reads=[B["R_xtk"], B["R_xdd"]], writes=[PR[bk]])
                hv = hcur[:].rearrange("p (h q) -> p h q", q=64)
                op("dve", lambda e: e.tensor_tensor(hv, hv, pex[:, ti, 8:16].unsqueeze(2).broadcast_to([128, 8, 64]), ALU.mult),
                   reads=[B["R_pex"]], writes=[B["R_hcur"]])
                op("dve", lambda e: e.tensor_tensor(hcur[:], hcur[:], PS[bk][:, :], ALU.add),
                   reads=[PR[bk]], writes=[B["R_hcur"]])

        with ExitStack() as pq:
            hTq = sb("hTq", [128, 16, NP_TOK], BF16, pq)
            R_hTq = Reg()
            phase0([(t * 128, 128, hTq, R_hTq, t * 128) for t in range(8)])
            xsTq = sb("xsTq", [128, 4, NP_TOK], BF16, pq)
            BTq = sb("BTq", [128, NP_TOK], BF16, pq)
            R_xsTq, R_BTq = Reg(), Reg()
            preq = [sb("preq%d" % i, [128, 4 + NP_TOK], F32, pq) for i in range(2)]
            accq = sb("accq", [128, NP_TOK], F32, pq)
            preqR, R_accq = [Reg(), Reg()], Reg()
            Bq = dict(xtk=sb("xtkq", [128, 8, 640], BF16, pq), xdt=sb("xdtq", [128, 8, 512], BF16, pq),
                      xdd=sb("xddq", [128, 8, 512], BF16, pq), la=sb("laq", [128, 8, 8], F32, pq),
                      pex=sb("pexq", [128, 8, 24], F32, pq), hcur=sb("hcurq", [128, 512], F32, pq),
                      hbf=sb("hbfq", [128, 8, 512], BF16, pq))
            for nm in ("xtk", "xdt", "xdd", "la", "pex", "hcur", "hbf"):
                Bq["R_" + nm] = Reg()
            cntq = dict(n=0)
            blocks_pre = [(lambda k: hTq[:, k, 0:512], 512, R_hTq), (lambda k: hTq[:, k, 512:1024], 512, R_hTq)]

            def xbc_pre(slot, cofs, j, dst, dreg):
                i = cntq["n"] % 2
                cntq["n"] += 1
                P_ = preq[i]
                wv = cst[:, K_CSW + 4 * j:K_CSW + 4 * j + 4]
                bv = cst[:, K_CSB + j:K_CSB + j + 1]
                op("pool", lambda e: e.memset(P_[:, 0:4], 0.0), writes=[preqR[i]])

                def evac(bi, bk, N):
                    op("act", lambda e: e.copy(P_[:, 4 + bi * 512:4 + bi * 512 + 512], PS[bk][:, 0:512]),
                       reads=[PR[bk]], writes=[preqR[i]])
                proj_fm(slot, cofs, blocks_pre, [3, 4, 5, 7], evac)
                op("dve", lambda e: e.tensor_scalar(accq[:], P_[:, 1:1 + NP_TOK], wv[:, 0:1], None, ALU.mult),
                   reads=[preqR[i], R_c], writes=[R_accq])
                for k in range(1, 4):
                    op("dve", lambda e: e.scalar_tensor_tensor(accq[:], P_[:, 1 + k:1 + k + NP_TOK], wv[:, k:k + 1], accq[:],
                                                               ALU.mult, ALU.add), reads=[preqR[i], R_c], writes=[R_accq])
                op("act", lambda e: e.activation(dst, accq[:], AF.Silu, bias=bv), reads=[R_accq, R_c], writes=[dreg])

            run_jobs([(C_DT, 32, job_dt(hTq, R_hTq, 8, dtq, R_dtq))])
            for g in range(4):
                def jB(slot, g=g):
                    xbc_pre(slot, 0, 16 + g, BTq[:], R_BTq)

                def jx(hf, g=g):
                    def fn(slot):
                        for c2 in range(2):
                            c = hf * 2 + c2
                            xbc_pre(slot, c2 * 128, g * 4 + c, xsTq[:, c, :], R_xsTq)
                    return fn
                run_jobs([(C_B + g * 128, 128, jB), (C_XS + g * 512, 256, jx(0)), (C_XS + g * 512 + 256, 256, jx(1))])
                tlq = [(t, (lambda c, t=t: xsTq[:, c, t * 128:(t + 1) * 128]), R_xsTq, BTq[:, t * 128:(t + 1) * 128], R_BTq,
                        dtq[:, t, g * 8:g * 8 + 8], TRI, ones[:]) for t in range(8)]
                stage123(Bq, tlq, g, "prefix")
                state_scan(Bq, tlq, lambda: op("dve", lambda e: e.memset(Bq["hcur"][:], 0.0), writes=[Bq["R_hcur"]]),
                           need_hbf=False)
                op("dve", lambda e: e.tensor_scalar(hmid[:, g, :], Bq["hcur"][:], cst[:, K_FLAG:K_FLAG + 1], None, ALU.mult),
                   reads=[Bq["R_hcur"], R_c], writes=[R_hmid])

        kb.barrier()
        hT = sb("hT", [128, 16, NT], BF16)
        hTh = sb("hTh", [128, 16, 4], BF16)
        R_hT, R_hTh = Reg(), Reg()
        tiles0 = [(1024 + t * 128, 128, hT, R_hT, t * 128) for t in range(9)]
        tiles0.append((2176, 4, hTh, R_hTh, 0))
        phase0(tiles0)
        for sbk in range(2):
            NTL = 512 if sbk == 0 else 640
            ntile = 4 if sbk == 0 else 5
            goff = sbk * 512
            tb = sbk * 512
            tcol = [tb + t * 128 for t in range(4)] + [1024]
            blocks_main = [(lambda k, tb=tb: hT[:, k, tb:tb + 512], 512, R_hT)]
            if sbk == 1:
                blocks_main.append((lambda k: hT[:, k, 1024:1152], 128, R_hT))
                blocks_main.append((lambda k: hT[:, k, 508:512], 4, R_hT))
            else:
                blocks_main.append((lambda k: hTh[:, k, 0:4], 4, R_hTh))
            HALO = len(blocks_main) - 1
            with ExitStack() as pa:
                BT = sb("BT", [128, 640], BF16, pa)
                CT = sb("CT", [128, 640], BF16, pa)
                xsT = sb("xsT", [128, 4, 640], BF16, pa)
                szs = sb("szs", [128, 5, 512], BF16, pa)
                R_BT, R_CT, R_xsT, R_szs = Reg(), Reg(), Reg(), Reg()
                pre = [sb("pre%d" % i, [128, 516], F32, pa) for i in range(2)]
                pres = [sb("pres%d" % i, [128, 16, 11], F32, pa) for i in range(2)]
                acc = sb("acc", [128, 512], F32, pa)
                accs = sb("accs", [128, 16, 8], F32, pa)
                preR, R_acc = [Reg(), Reg()], Reg()
                stcs = sb("stcs", [128, 1536], F32, pa)
                R_stcs = Reg()
                if sbk == 1:
                    dma("sp", stcs[0:48, :], st_cs[:, 0:1536], cch, writes=[R_stcs])
                    dma("sp", stcs[64:112, :], st_cs[:, 1536:3072], cch, writes=[R_stcs])
                Bm_ = dict(xtk=sb("xtk", [128, 5, 640], BF16, pa), xdt=sb("xdt", [128, 5, 512], BF16, pa),
                           xdd=sb("xdd", [128, 5, 512], BF16, pa), la=sb("la", [128, 5, 8], F32, pa),
                           pex=sb("pex", [128, 5, 24], F32, pa), hcur=sb("hcur", [128, 512], F32, pa),
                           hbf=sb("hbf", [128, 4, 512], BF16, pa))
                for nm in ("xtk", "xdt", "xdd", "la", "pex", "hcur", "hbf"):
                    Bm_["R_" + nm] = Reg()
                xtk, xdt, xdd, la, pex, hcur, hbf = (Bm_[n_] for n_ in ("xtk", "xdt", "xdd", "la", "pex", "hcur", "hbf"))
                cbm = sb("cbm", [128, 5, 128], F32, pa)
                R_cbm = Reg()
                Lb2 = [sb("Lb%d" % i, [128, 2, 8, 128], BF16, pa) for i in range(2)]
                LbR = [Reg(), Reg()]
                Eb2 = [sb("Eb2_%d" % i, [128, 1024], BF16, pa) for i in range(2)]
                EbR = [Reg(), Reg()]
                lahi = sb("lahi", [128, 5, 8], BF16, pa)
                lalo = sb("lalo", [128, 5, 8], BF16, pa)
                latmp = sb("latmp", [128, 5, 8], F32, pa)
                R_lahl = Reg()
                DI = sb("DI", [128, 8, 128], BF16, pa)
                R_DI = Reg()
                MT = sb("MT", [128, 5, 1024], BF16, pa)
                R_MT = Reg()
                ytm = [sb("ytm%d" % i, [128, 512], F32, pa) for i in range(2)]
                ytR = [Reg(), Reg()]
                ynb = [sb("ynb%d" % i, [128, 512], BF16, pa) for i in range(2)]
                ynR = [Reg(), Reg()]
                yst = [sb("yst%d" % i, [128, 4, 128], BF16, pa) for i in range(2)]
                ystR = [Reg(), Reg()]
                tmp = sb("tmp", [128, 512], F32, pa)
                R_tmp = Reg()
                sst = sb("sst", [128, 5, 4], F32, pa)
                R_sst = Reg()
                junk2 = sb("junk2", [128, 512], BF16, pa)
                R_j2 = Reg()
                if sbk == 1:
                    h0 = [sb("h0_%d" % i, [128, 4, 128], F32, pa) for i in range(3)]
                    h0R = [Reg() for _ in range(3)]
                    h0C = [xc[0], xc[1], xc[2]]
                    h0b = [sb("h0b%d" % i, [128, 512], BF16, pa) for i in range(2)]
                    h0bR = [Reg(), Reg()]
                    h0T = [sb("h0T%d" % i, [128, 512], BF16, pa) for i in range(2)]
                    h0TR = [Reg(), Reg()]
                    hn = [sb("hn%d" % i, [128, 4, 128], F32, pa) for i in range(2)]
                    hnR = [Reg(), Reg()]
                    hnC = hnCh
                    CTm = sb("CTm", [128, 16, 128], BF16, pa)
                    Bmk = sb("Bmk", [128, 16, 128], BF16, pa)
                    larep = sb("larep", [128, 8, 64], F32, pa)
                    cdT = sb("cdT", [128, 4, 16], F32, pa)
                    R_CTm, R_Bmk, R_larep, R_cdT = Reg(), Reg(), Reg(), Reg()
                cnt = dict(n=0)

                def xbc_chunk(slot, cofs, j, dstP, dstS, dreg):
                    i = cnt["n"] % 2
                    cnt["n"] += 1
                    P_, PSm = pre[i], pres[i]
                    wv = cst[:, K_CSW + 4 * j:K_CSW + 4 * j + 4]
                    bv = cst[:, K_CSB + j:K_CSB + j + 1]
                    if sbk == 1:
                        hj, jj = j // 12, j % 12
                        bk = next_bank([6, 7])
                        op("pe", lambda e: e.matmul(PS[bk][:, 0:48], stcs[64 * hj:64 * hj + 48, jj * 128:(jj + 1) * 128],
                                                    IDF[64 * hj:64 * hj + 48, 64 * hj:64 * hj + 48], start=True, stop=True),
                           reads=[R_stcs, R_c], writes=[PR[bk]])
                        op("act", lambda e: e.copy(PSm[:, :, 0:3], PS[bk][:, 0:48].rearrange("p (b t) -> p b t", t=3)),
                           reads=[PR[bk]], writes=[preR[i]])

                    def evac(bi, bk, N):
                        if bi == HALO:
                            op("act", lambda e: e.copy(P_[:, 0:4], PS[bk][:, 0:4]), reads=[PR[bk]], writes=[preR[i]])
                        elif bi == 0:
                            op("act", lambda e: e.copy(P_[:, 4:516], PS[bk][:, 0:512]), reads=[PR[bk]], writes=[preR[i]])
                        else:
                            op("act", lambda e: e.copy(PSm[:, :, 3:11], PS[bk][:, 0:128].rearrange("p (b t) -> p b t", t=8)),
                               reads=[PR[bk]], writes=[preR[i]])
                    proj_fm(slot, cofs, blocks_main, [3, 4, 5], evac)
                    if sbk == 1:
                        op("pool", lambda e: e.tensor_copy(csT[:, j, :], P_[:, 512:516]), reads=[preR[i]], writes=[R_cs])
                        op("pool", lambda e: e.tensor_copy(csTs[:, j, :].rearrange("p (b t) -> p b t", t=3), PSm[:, :, 8:11]),
                           reads=[preR[i]], writes=[R_css])
                    op("dve", lambda e: e.tensor_scalar(acc[:], P_[:, 1:513], wv[:, 0:1], None, ALU.mult),
                       reads=[preR[i], R_c], writes=[R_acc])
                    for k in range(1, 4):
                        op("dve", lambda e: e.scalar_tensor_tensor(acc[:], P_[:, 1 + k:513 + k], wv[:, k:k + 1], acc[:],
                                                                   ALU.mult, ALU.add), reads=[preR[i], R_c], writes=[R_acc])
                    op("act", lambda e: e.activation(dstP, acc[:], AF.Silu, bias=bv), reads=[R_acc, R_c], writes=[dreg])
                    if sbk == 1:
                        op("dve", lambda e: e.tensor_scalar(accs[:], PSm[:, :, 0:8], wv[:, 0:1], None, ALU.mult),
                           reads=[preR[i], R_c], writes=[R_acc])
                        for k in range(1, 4):
                            op("dve", lambda e: e.scalar_tensor_tensor(accs[:], PSm[:, :, k:k + 8], wv[:, k:k + 1], accs[:],
                                                                        ALU.mult, ALU.add), reads=[preR[i], R_c, R_acc], writes=[R_acc])
                        op("act", lambda e: e.activation(dstS, accs[:].rearrange("p b t -> p (b t)"), AF.Silu, bias=bv),
                           reads=[R_acc, R_c], writes=[dreg])

                run_jobs([(C_DT, 32, job_dt(hT, R_hT, ntile, dtm, R_dtm, tcol))])

                for g in range(4):
                    hs = slice(g * 8, g * 8 + 8)

                    def jB(slot, g=g):
                        xbc_chunk(slot, 0, 16 + g, BT[:, 0:512], BT[:, 512:640], R_BT)

                    def jC(slot, g=g):
                        xbc_chunk(slot, 0, 20 + g, CT[:, 0:512], CT[:, 512:640], R_CT)

                    def jx(hf, g=g):
                        def fn(slot):
                            for c2 in range(2):
                                c = hf * 2 + c2
                                xbc_chunk(slot, c2 * 128, g * 4 + c, xsT[:, c, 0:512], xsT[:, c, 512:640], R_xsT)
                        return fn

                    def jz(hf, g=g):
                        def fn(slot):
                            for t in range(ntile):
                                bk = next_bank([3, 4, 5])
                                for k in range(16):
                                    op("pe", lambda e: e.matmul(PS[bk][:, 0:256], hT[:, k, tcol[t]:tcol[t] + 128], WB[slot][:, k, 0:256],
                                                                start=(k == 0), stop=(k == 15)),
                                       reads=[WR[slot], R_hT], writes=[PR[bk]], inc=(k == 15))
                                op("act", lambda e: e.activation(szs[:, t, hf * 256:(hf + 1) * 256], PS[bk][:, 0:256], AF.Silu),
                                   reads=[PR[bk]], writes=[R_szs])
                        return fn
                    run_jobs([(C_B + g * 128, 128, jB), (C_C + g * 128, 128, jC),
                              (C_XS + g * 512, 256, jx(0)), (C_XS + g * 512 + 256, 256, jx(1)),
                              (C_ZS + g * 512, 256, jz(0)), (C_ZS + g * 512 + 256, 256, jz(1))])

                    tlm = [(t, (lambda c, t=t: xsT[:, c, t * 128:(t + 1) * 128]), R_xsT, BT[:, t * 128:(t + 1) * 128], R_BT,
                            dtm[:, t, hs], (TRI if t < 4 else TRIB), (ones[:] if t < 4 else ONEB)) for t in range(ntile)]
                    stage123(Bm_, tlm, g, "main")
                    state_scan(Bm_, tlm[:4], lambda: op("dve", lambda e: e.tensor_copy(hcur[:], hmid[:, g, :]),
                                                        reads=[R_hmid], writes=[Bm_["R_hcur"]]))
                    op("dve", lambda e: e.tensor_copy(hmid[:, g, :], hcur[:]), reads=[Bm_["R_hcur"]], writes=[R_hmid])
                    if sbk == 1:
                        bk = next_bank([5])
                        for jj in range(4):
                            op("pe", lambda e: e.matmul(PS[bk][:, jj * 128:(jj + 1) * 128], hcur[:, jj * 128:(jj + 1) * 128], IDF,
                                                        start=True, stop=True), reads=[Bm_["R_hcur"], R_c], writes=[PR[bk]], inc=(jj == 3))
                        op("act", lambda e: e.copy(tmp[:], PS[bk][:, :]), reads=[PR[bk]], writes=[R_tmp])
                        dma("sp", ssm_p_o[g * 512:(g + 1) * 512, :].rearrange("(j p) n -> p j n", p=128),
                            tmp[:].rearrange("p (j n) -> p j n", n=128), ysC, reads=[R_tmp])

                    op("dve", lambda e: e.tensor_copy(lahi[:, 0:ntile, :], la[:, 0:ntile, :]), reads=[Bm_["R_la"]], writes=[R_lahl])
                    op("dve", lambda e: e.tensor_tensor(latmp[:, 0:ntile, :], la[:, 0:ntile, :], lahi[:, 0:ntile, :], ALU.subtract),
                       reads=[Bm_["R_la"], R_lahl], writes=[R_lahl])
                    op("dve", lambda e: e.tensor_copy(lalo[:, 0:ntile, :], latmp[:, 0:ntile, :]), reads=[R_lahl], writes=[R_lahl])
                    op("dve", lambda e: e.tensor_tensor(DI[:], identb[:, None, :].broadcast_to([128, 8, 128]),
                                                        cst[:, K_DSK + g * 8:K_DSK + g * 8 + 8].unsqueeze(2).broadcast_to([128, 8, 128]),
                                                        ALU.mult), reads=[R_c, R_c2], writes=[R_DI])
                    op("dve", lambda e: e.memset(sst[:], 0.0), writes=[R_sst])
                    def a4_gen():
                        for (ti, xsrc, xreg_, bsrc, breg_, dtap, trix, onx) in tlm:
                            bk = next_bank([0, 1])
                            op("pe", lambda e: e.matmul(PS[bk][:, 0:128], BT[:, ti * 128:(ti + 1) * 128],
                                                        CT[:, ti * 128:(ti + 1) * 128], start=True, stop=True),
                               reads=[R_BT, R_CT], writes=[PR[bk]])
                            op("dve", lambda e: e.tensor_tensor(cbm[:, ti, :], PS[bk][:, 0:128], trix, ALU.mult),
                               reads=[PR[bk], R_c], writes=[R_cbm])
                        def emit_L(n_):
                            (ti, xsrc, xreg_, bsrc, breg_, dtap, trix, onx) = tlm[n_]
                            L_ = Lb2[n_ % 2]
                            for q_, lsrc in enumerate((lahi, lalo)):
                                op("dve", lambda e: e.tensor_tensor(L_[:, q_, :, :], trix[:, None, :].broadcast_to([128, 8, 128]),
                                                                    lsrc[:, ti, :].unsqueeze(2).broadcast_to([128, 8, 128]), ALU.mult),
                                   reads=[R_lahl, R_c], writes=[LbR[n_ % 2]])
                        emit_L(0)
                        for n_, (ti, xsrc, xreg_, bsrc, breg_, dtap, trix, onx) in enumerate(tlm):
                            if n_ + 1 < ntile:
                                emit_L(n_ + 1)
                            L_, E_ = Lb2[n_ % 2], Eb2[n_ % 2]
                            bks = (4, 5) if n_ % 2 == 0 else (6, 3)
                            for hh in range(2):
                                bk = bks[hh]
                                for q_ in range(2):
                                    op("pe", lambda e: e.matmul(PS[bk][:, :], MSb[:], L_[:, q_, hh * 4:hh * 4 + 4, :].rearrange("p h l -> p (h l)"),
                                                                start=(q_ == 0), stop=(q_ == 1)), reads=[LbR[n_ % 2], R_c2], writes=[PR[bk]], inc=(q_ == 1))
                            for hh in range(2):
                                op("act", lambda e: e.activation(E_[:, hh * 512:(hh + 1) * 512], PS[bks[hh]][:, :], AF.Exp),
                                   reads=[PR[bks[hh]]], writes=[EbR[n_ % 2]])
                            op("dve", lambda e: e.tensor_tensor(MT[:, ti, :].rearrange("p (h l) -> p h l", l=128),
                                                                E_[:].rearrange("p (h l) -> p h l", l=128),
                                                                cbm[:, ti, None, :].broadcast_to([128, 8, 128]), ALU.mult),
                               reads=[EbR[n_ % 2], R_cbm], writes=[R_MT])
                            yield


                    def a6_gen():
                        def emit_mm(n_):
                            ti = tlm[n_][0]
                            i2 = n_ % 2
                            bd = 5 if i2 == 0 else 6
                            bo = (3 if i2 == 0 else 4) if ti < 4 else 7
                            for h in range(8):
                                op("pe", lambda e: e.matmul(PS[bd][:, h * 64:(h + 1) * 64], MT[:, ti, h * 128:(h + 1) * 128],
                                                            xdt[:, ti, h * 64:(h + 1) * 64], start=True, stop=False),
                                   reads=[R_MT, Bm_["R_xdt"]], writes=[PR[bd]], inc=False)
                                op("pe", lambda e: e.matmul(PS[bd][:, h * 64:(h + 1) * 64], DI[:, h, :],
                                                            xtk[:, ti, h * 64:(h + 1) * 64], start=False, stop=True),
                                   reads=[R_DI, Bm_["R_xtk"]], writes=[PR[bd]], inc=(h == 7))
                            if ti < 4:
                                op("pe", lambda e: e.matmul(PS[bo][:, :], CT[:, ti * 128:(ti + 1) * 128], hbf[:, ti, :],
                                                            start=True, stop=True), reads=[R_CT, Bm_["R_hbf"]], writes=[PR[bo]])
                        emit_mm(0)
                        for n_, (ti, xsrc, xreg_, bsrc, breg_, dtap, trix, onx) in enumerate(tlm):
                            i2 = n_ % 2
                            bd = 5 if i2 == 0 else 6
                            bo = (3 if i2 == 0 else 4) if ti < 4 else 7
                            if n_ + 1 < ntile:
                                emit_mm(n_ + 1)
                            Y = ytm[i2]
                            Yv = Y[:].rearrange("p (h q) -> p h q", q=64)
                            op("dve", lambda e: e.tensor_tensor(Yv, PS[bo][:, :].rearrange("p (h q) -> p h q", q=64),
                                                                pex[:, ti, 0:8].unsqueeze(2).broadcast_to([128, 8, 64]), ALU.mult),
                               reads=[PR[bo], Bm_["R_pex"]], writes=[ytR[i2]])
                            op("dve", lambda e: e.tensor_tensor(Y[:], Y[:], PS[bd][:, :], ALU.add), reads=[PR[bd]], writes=[ytR[i2]])
                            op("dve", lambda e: e.tensor_tensor(Y[:], Y[:], szs[:, ti, :], ALU.mult), reads=[R_szs], writes=[ytR[i2]])
                            op("act", lambda e: e.activation(junk2[:], Y[:], AF.Square, accum_out=sst[:, ti, 0:1]),
                               reads=[ytR[i2]], writes=[R_j2, R_sst])
                            op("act", lambda e: e.activation(sst[:, ti, 1:2], sst[:, ti, 0:1], AF.Ln, scale=1.0 / 512, bias=EPS),
                               reads=[R_sst], writes=[R_sst])
                            op("act", lambda e: e.activation(sst[:, ti, 2:3], sst[:, ti, 1:2], AF.Exp, scale=-0.5),
                               reads=[R_sst], writes=[R_sst])
                            op("act", lambda e: e.activation(ynb[i2][:], Y[:], AF.Copy, scale=sst[:, ti, 2:3]),
                               reads=[ytR[i2], R_sst], writes=[ynR[i2]])
                            bk = next_bank([0, 1])
                            pv = psb(bk)
                            for c in range(4):
                                op("pe", lambda e: e.transpose(pv[:, c * 128:(c + 1) * 128], ynb[i2][:, c * 128:(c + 1) * 128], identb[:]),
                                   reads=[ynR[i2], R_c], writes=[PR[bk]], inc=(c == 3))
                            op("dve", lambda e: e.tensor_tensor(yst[i2][:], pv[:, 0:512].rearrange("p (c t) -> p c t", t=128),
                                                                cst[:, K_SNW + g * 4:K_SNW + g * 4 + 4].unsqueeze(2).broadcast_to([128, 4, 128]),
                                                                ALU.mult), reads=[PR[bk], R_c], writes=[ystR[i2]])
                            dma("sp", ysd[g * 4:g * 4 + 4, :, goff + ti * 128:goff + (ti + 1) * 128].rearrange("c p t -> p c t"),
                                yst[i2][:], ystC[i2], reads=[ystR[i2]])
                            yield


                    a4 = a4_gen()
                    a6 = a6_gen()
                    if sbk == 0:
                        for _ in a4:
                            pass
                    if sbk == 1:
                        op("pool", lambda e: e.tensor_tensor(CTm[:], CT[:, None, 512:640].broadcast_to([128, 16, 128]),
                                                            selm[:].rearrange("p (b l) -> p b l", l=128), ALU.mult),
                           reads=[R_CT, R_c, R_c2], writes=[R_CTm])
                        op("pool", lambda e: e.tensor_tensor(Bmk[:], xtk[:, 4, None, 512:640].broadcast_to([128, 16, 128]),
                                                            BLK.unsqueeze(2).broadcast_to([128, 16, 128]), ALU.mult),
                           reads=[Bm_["R_xtk"], R_c], writes=[R_Bmk])
                        op("pool", lambda e: e.tensor_copy(larep[:], la[:, 4, :].unsqueeze(2).broadcast_to([128, 8, 64])),
                           reads=[Bm_["R_la"]], writes=[R_larep])
                        for jj in range(4):
                            op("pe", lambda e: e.matmul(PS[2][:, jj * 16:(jj + 1) * 16],
                                                        larep[:].rearrange("p h q -> p (h q)")[:, jj * 128:(jj + 1) * 128], BLK,
                                                        start=True, stop=True), reads=[R_larep, R_c], writes=[PR[2]], inc=(jj == 3))
                        op("act", lambda e: e.activation(cdT[:].rearrange("p j b -> p (j b)"), PS[2][:, 0:64], AF.Exp),
                           reads=[PR[2]], writes=[R_cdT])

                        def ld(b_):
                            i3_ = b_ % 3
                            dma("sp", h0[i3_][:], st_ssm[b_, g * 512:(g + 1) * 512, :].rearrange("(j p) n -> p j n", p=128),
                                h0C[i3_], writes=[h0R[i3_]])
                        def cast(b_):
                            op("act", lambda e: e.copy(h0b[b_ % 2][:], h0[b_ % 3][:].rearrange("p j n -> p (j n)")),
                               reads=[h0R[b_ % 3]], writes=[h0bR[b_ % 2]])
                        ld(0)
                        ld(1)
                        cast(0)
                        for b in range(16):
                            i2, i3 = b % 2, b % 3
                            if b + 2 < 16:
                                ld(b + 2)
                            if b + 1 < 16:
                                cast(b + 1)
                            bk = 0
                            pv = psb(bk)
                            for jj in range(4):
                                op("pe", lambda e: e.transpose(pv[:, jj * 128:(jj + 1) * 128], h0b[i2][:, jj * 128:(jj + 1) * 128], identb[:]),
                                   reads=[h0bR[i2], R_c2], writes=[PR[bk]], inc=(jj == 3))
                            op("act", lambda e: e.copy(h0T[i2][:], pv[:, 0:512]), reads=[PR[bk]], writes=[h0TR[i2]])
                            op("pe", lambda e: e.matmul(PS[7][:, :], CTm[:, b, :], h0T[i2][:], start=(b == 0), stop=(b == 15)),
                               reads=[R_CTm, h0TR[i2]], writes=[PR[7]], inc=True)
                            bk2 = 1
                            for jj in range(4):
                                op("pe", lambda e: e.matmul(PS[bk2][:, jj * 128:(jj + 1) * 128], xdd[:, 4, jj * 128:(jj + 1) * 128],
                                                            Bmk[:, b, :], start=True, stop=True),
                                   reads=[Bm_["R_xdd"], R_Bmk], writes=[PR[bk2]], inc=(jj == 3))
                            for jj in range(4):
                                op("dve", lambda e: e.scalar_tensor_tensor(hn[i2][:, jj, :], h0[i3][:, jj, :], cdT[:, jj, b:b + 1],
                                                                           PS[bk2][:, jj * 128:(jj + 1) * 128], ALU.mult, ALU.add),
                                   reads=[h0R[i3], R_cdT, PR[bk2]], writes=[hnR[i2]])
                            dma("sp", ssm_s_o[b, g * 512:(g + 1) * 512, :].rearrange("(j p) n -> p j n", p=128), hn[i2][:],
                                hnC[i2], reads=[hnR[i2]])
                            if b < 5:
                                next(a4, None)
                            elif b >= 6 and b % 2 == 0 and b <= 12:
                                next(a6, None)

                    for _ in a4:
                        pass
                    for _ in a6:
                        pass

                if sbk == 1:
                    for q in range(6):
                        bk = next_bank([0, 1, 2])
                        for c in range(4):
                            j = q * 4 + c
                            op("pe", lambda e: e.matmul(PS[bk][0:48, c * 128:(c + 1) * 128], csTs[:, j, :], IDF, start=True, stop=True),
                               reads=[R_css, R_c], writes=[PR[bk]], inc=(c == 3))
                        op("act", lambda e: e.copy(ytm[0][0:48, :], PS[bk][0:48, :]), reads=[PR[bk]], writes=[ytR[0]])
                        dma("sp", cs_s_o[:, q * 512:(q + 1) * 512], ytm[0][0:48, :], csC[0], reads=[ytR[0]])
                        bk = next_bank([0, 1, 2])
                        for c in range(4):
                            j = q * 4 + c
                            op("pe", lambda e: e.matmul(PS[bk][0:4, c * 128:(c + 1) * 128], csT[:, j, :], IDF, start=True, stop=True),
                               reads=[R_cs, R_c], writes=[PR[bk]], inc=(c == 3))
                        op("act", lambda e: e.copy(ytm[1][0:4, :], PS[bk][0:4, :]), reads=[PR[bk]], writes=[ytR[1]])
                        dma("sp", cs_p_o[:, q * 512:(q + 1) * 512], ytm[1][0:4, :], csC[1], reads=[ytR[1]])

            kb.barrier()
        kb.barrier()
        blocks_all = [(lambda k: hT[:, k, 0:512], 512, R_hT), (lambda k: hT[:, k, 512:1024], 512, R_hT),
                      (lambda k: hT[:, k, 1024:1152], 128, R_hT), (lambda k: hTh[:, k, 0:4], 4, R_hTh)]
        with ExitStack() as pb:
            cvp = sb("cvp", [128, 2, 4 + NP_TOK], F32, pb)
            cvs = sb("cvs", [128, 2, 16, 10], F32, pb)
            co = sb("co", [128, 2, NT], F32, pb)
            szb = [sb("szb%d" % i, [128, 512], BF16, pb) for i in range(2)]
            szR = [Reg(), Reg()]
            ycs = [sb("ycs%d" % i, [128, 512], BF16, pb) for i in range(2)]
            ycR = [Reg(), Reg()]
            R_cvp, R_co = Reg(), Reg()
            stch = sb("stch", [32, 2048], F32, pb)
            R_stch = Reg()
            dma("sp", stch[:], st_csh, cch, writes=[R_stch])
            rr = dict(n=0)
            BK6 = [0, 1, 2, 3, 4, 5]
            for jq in range(8):
                def job_c(slot, jq=jq):
                    for c in range(2):
                        j = jq * 2 + c
                        bk = next_bank([6, 7])
                        op("pe", lambda e: e.matmul(PS[bk][:, 0:32], stch[0:32, j * 128:(j + 1) * 128], IDF[0:32, 0:32],
                                                    start=True, stop=True), reads=[R_stch, R_c], writes=[PR[bk]])
                        op("act", lambda e: e.copy(cvs[:, c, :, 0:2], PS[bk][:, 0:32].rearrange("p (b t) -> p b t", t=2)),
                           reads=[PR[bk]], writes=[R_cvp])

                        def evac(bi, bk, N, c=c):
                            if bi == 3:
                                op("act", lambda e: e.copy(cvp[:, c, 0:4], PS[bk][:, 0:4]), reads=[PR[bk]], writes=[R_cvp])
                            elif bi < 2:
                                op("act", lambda e: e.copy(cvp[:, c, 4 + bi * 512:516 + bi * 512], PS[bk][:, 0:512]),
                                   reads=[PR[bk]], writes=[R_cvp])
                            else:
                                op("act", lambda e: e.copy(cvs[:, c, :, 2:10], PS[bk][:, 0:128].rearrange("p (b t) -> p b t", t=8)),
                                   reads=[PR[bk]], writes=[R_cvp])
                        proj_fm(slot, c * 128, blocks_all, BK6, evac)

                def job_v(slot, jq=jq):
                    for c in range(2):
                        j = jq * 2 + c
                        wv = cst[:, K_CHW + 3 * j:K_CHW + 3 * j + 3]

                        def evac(bi, bk, N, c=c):
                            if bi == 3:
                                d = cvp[:, c, 0:4]
                                op("dve", lambda e: e.tensor_tensor(d, d, PS[bk][:, 0:4], ALU.mult), reads=[PR[bk]], writes=[R_cvp])
                            elif bi < 2:
                                d = cvp[:, c, 4 + bi * 512:516 + bi * 512]
                                op("dve", lambda e: e.tensor_tensor(d, d, PS[bk][:, 0:512], ALU.mult), reads=[PR[bk]], writes=[R_cvp])
                            else:
                                d = cvs[:, c, :, 2:10]
                                op("dve", lambda e: e.tensor_tensor(d, d, PS[bk][:, 0:128].rearrange("p (b t) -> p b t", t=8), ALU.mult),
                                   reads=[PR[bk]], writes=[R_cvp])
                        proj_fm(slot, c * 128, blocks_all, BK6, evac)
                        op("pool", lambda e: e.tensor_copy(chT[:, j, :], cvp[:, c, NP_TOK:NP_TOK + 4]), reads=[R_cvp], writes=[R_ch])
                        op("pool", lambda e: e.tensor_copy(chTs[:, j, :].rearrange("p (b t) -> p b t", t=2), cvs[:, c, :, 8:10]),
                           reads=[R_cvp], writes=[R_chs])
                        for hb in range(2):
                            cp = co[:, c, hb * 512:(hb + 1) * 512]
                            o_ = 2 + hb * 512
                            op("dve", lambda e: e.tensor_scalar(cp, cvp[:, c, o_:o_ + 512], wv[:, 0:1], None, ALU.mult),
                               reads=[R_cvp, R_c], writes=[R_co])
                            for k in (1, 2):
                                op("dve", lambda e: e.scalar_tensor_tensor(cp, cvp[:, c, o_ + k:o_ + k + 512], wv[:, k:k + 1], cp,
                                                                           ALU.mult, ALU.add), reads=[R_cvp, R_c], writes=[R_co])
                        cs_ = co[:, c, NP_TOK:NT].rearrange("p (b t) -> p b t", t=8)
                        op("dve", lambda e: e.tensor_scalar(cs_, cvs[:, c, :, 0:8], wv[:, 0:1], None, ALU.mult),
                           reads=[R_cvp, R_c], writes=[R_co])
                        for k in (1, 2):
                            op("dve", lambda e: e.scalar_tensor_tensor(cs_, cvs[:, c, :, k:k + 8], wv[:, k:k + 1], cs_,
                                                                       ALU.mult, ALU.add), reads=[R_cvp, R_c, R_co], writes=[R_co])

                def job_b(slot, jq=jq):
                    for c in range(2):
                        def evac(bi, bk, N, c=c):
                            d = co[:, c, bi * 512:bi * 512 + N]
                            op("dve", lambda e: e.tensor_tensor(d, d, PS[bk][:, 0:N], ALU.mult), reads=[PR[bk]], writes=[R_co])
                        proj_fm(slot, c * 128, blocks_all[:3], BK6, evac)

                def job_z(slot, jq=jq):
                    for c in range(2):
                        j = jq * 2 + c

                        def evac(bi, bk, N, c=c, j=j):
                            i2 = rr["n"] % 2
                            rr["n"] += 1
                            op("act", lambda e: e.activation(szb[i2][:, 0:N], PS[bk][:, 0:N], AF.Silu),
                               reads=[PR[bk]], writes=[szR[i2]])
                            op("pool", lambda e: e.tensor_tensor(ycs[i2][:, 0:N], co[:, c, bi * 512:bi * 512 + N],
                                                                 szb[i2][:, 0:N], ALU.mult),
                               reads=[R_co, szR[i2]], writes=[ycR[i2]])
                            dma("sp", ysd[16 + j, :, bi * 512:bi * 512 + N], ycs[i2][:, 0:N], ystC[i2], reads=[ycR[i2]])
                        proj_fm(slot, c * 128, blocks_all[:3], BK6, evac)

                run_jobs([(C_CC + jq * 256, 256, job_c), (C_VC + jq * 256, 256, job_v),
                          (C_BC + jq * 256, 256, job_b), (C_ZC + jq * 256, 256, job_z)])
            for q in range(4):
                bk = next_bank([0, 1, 2, 3])
                for c in range(4):
                    j = q * 4 + c
                    op("pe", lambda e: e.matmul(PS[bk][0:32, c * 128:(c + 1) * 128], chTs[:, j, :], IDF, start=True, stop=True),
                       reads=[R_chs, R_c], writes=[PR[bk]], inc=(c == 3))
                op("act", lambda e: e.copy(co[0:32, 0, 0:512], PS[bk][0:32, :]), reads=[PR[bk]], writes=[R_co])
                dma("sp", csh_s_o[:, q * 512:(q + 1) * 512], co[0:32, 0, 0:512], ysC, reads=[R_co])
                bk = next_bank([0, 1, 2, 3])
                for c in range(4):
                    j = q * 4 + c
                    op("pe", lambda e: e.matmul(PS[bk][0:4, c * 128:(c + 1) * 128], chT[:, j, :], IDF, start=True, stop=True),
                       reads=[R_ch, R_c], writes=[PR[bk]], inc=(c == 3))
                op("act", lambda e: e.copy(co[0:4, 1, 0:512], PS[bk][0:4, :]), reads=[PR[bk]], writes=[R_co])
                dma("sp", csh_p_o[:, q * 512:(q + 1) * 512], co[0:4, 1, 0:512], ysC, reads=[R_co])
        kb.barrier()
        for sbk in range(2):
            NTL = 512 if sbk == 0 else 640
            ntile = 4 if sbk == 0 else 5
            goff = sbk * 512
            xrow0 = 1024 + sbk * 512
            with ExitStack() as pc:
                fnw = sb("fnw", [128, DM], F32, pc)
                R_fnw = Reg()
                dma("sp", fnw[:], fnw_d, cch, writes=[R_fnw])
                ypre = sb("ypre", [128, 5, DM], F32, pc)
                R_yp = [Reg() for _ in range(5)]
                NOB = 3
                wo = [WB[i][:].rearrange("p k c -> p (k c)").rearrange("p (k c) -> p k c", c=512) for i in range(NOB)]
                woR = WR
                woC = WC
                ytf = sb("ytf", [128, 32, 640], BF16, pc)
                R_ytf = Reg()
                ytpC = [xc[0], xc[1]]
                xres = [sb("xres%d" % i, [128, DM], F32, pc) for i in range(2)]
                xrR = [Reg(), Reg()]
                xrC = [xc[2], hfC]
                junk3 = sb("junk3", [128, DM], BF16, pc)
                R_j3 = Reg()
                sso = sb("sso", [128, 5, 4], F32, pc)
                R_sso = Reg()
                seq = [(n, eg) for n in range(4) for eg in range(4)]

                def c_load(q):
                    n, eg = seq[q]
                    i = q % NOB
                    po = (n * 4 + eg) * 4096
                    src = w_out[:, po:po + 4096].rearrange("p (k c) -> p k c", c=512)
                    dma("pool", wo[i], src, woC[i], writes=[woR[i]])
                for ch_ in ystC:
                    nc.sync.wait_ge(kb.semh[ch_.sem], ch_.cnt)
                c_load(0)
                c_load(1)
                for eg_ in range(4):
                    dma("sp", ytf[:, eg_ * 8:(eg_ + 1) * 8, 0:NTL], ysd[eg_ * 8:(eg_ + 1) * 8, :, goff:goff + NTL].rearrange("c p t -> p c t"),
                        ytpC[0], writes=[R_ytf])
                for q, (n, eg) in enumerate(seq):
                    i = q % NOB
                    if q + 2 < len(seq):
                        c_load(q + 2)
                    for t in range(ntile):
                        for k in range(8):
                            e_ = eg * 8 + k
                            op("pe", lambda e: e.matmul(PS[t][:, :], ytf[:, e_, t * 128:(t + 1) * 128], wo[i][:, k, :],
                                                        start=(e_ == 0), stop=(e_ == 31)),
                               reads=[R_ytf, woR[i]], writes=[PR[t]], inc=(k == 7))
                    if eg == 3:
                        for t in range(ntile):
                            if t % 2 == 0:
                                op("act", lambda e: e.copy(ypre[:, t, n * 512:(n + 1) * 512], PS[t][:, :]),
                                   reads=[PR[t]], writes=[R_yp[t]])
                            else:
                                op("dve", lambda e: e.tensor_copy(ypre[:, t, n * 512:(n + 1) * 512], PS[t][:, :]),
                                   reads=[PR[t]], writes=[R_yp[t]])
                op("pool", lambda e: e.memset(sso[:], 0.0), writes=[R_sso])
                for t in range(ntile):
                    i2 = t % 2
                    r0 = (xrow0 + t * 128) if t < 4 else 2048
                    o0 = (goff + t * 128) if t < 4 else 1024
                    dma("sp", xres[i2][:], xall[r0:r0 + 128, :], xrC[i2], writes=[xrR[i2]])
                    Y = ypre[:, t, :]
                    op("dve", lambda e: e.tensor_tensor(Y, Y, xres[i2][:], ALU.add), reads=[xrR[i2]], writes=[R_yp[t]])
                    op("act", lambda e: e.activation(junk3[:], Y, AF.Square, accum_out=sso[:, t, 0:1]),
                       reads=[R_yp[t]], writes=[R_j3, R_sso])
                    op("act", lambda e: e.activation(sso[:, t, 1:2], sso[:, t, 0:1], AF.Ln, scale=1.0 / DM, bias=EPS),
                       reads=[R_sso], writes=[R_sso])
                    op("act", lambda e: e.activation(sso[:, t, 2:3], sso[:, t, 1:2], AF.Exp, scale=-0.5), reads=[R_sso], writes=[R_sso])
                    op("dve", lambda e: e.scalar_tensor_tensor(xres[i2][:], Y, sso[:, t, 2:3], fnw[:], ALU.mult, ALU.mult),
                       reads=[R_yp[t], R_sso, R_fnw], writes=[xrR[i2]])
                    dma("sp", y_o[o0:o0 + 128, :], xres[i2][:], hnCh[i2], reads=[xrR[i2]])
            kb.barrier()
        for ch in kb.chans:
            if ch.cnt:
                nc.sync.wait_ge(kb.semh[ch.sem], ch.cnt)
    return nc


_NC_CACHE = {}


def _host_consts():
    l = np.arange(128)
    ident = np.eye(128, dtype=np.float32)
    tri = (l[:, None] <= l[None, :]).astype(np.float32)
    mstrict = (l[:, None] > l[None, :]).astype(np.float32)
    same = (l[:, None] // 8 == l[None, :] // 8).astype(np.float32)
    trib = tri * same
    blk = (l[:, None] // 8 == np.arange(16)[None, :]).astype(np.float32)
    msk = np.concatenate([ident, tri, mstrict, trib, same, blk], axis=1).astype(np.float32)
    selrow = (np.arange(16)[:, None] == (l[None, :] // 8)).astype(np.float32).reshape(1, 2048)
    selm = np.ascontiguousarray(np.broadcast_to(selrow, (128, 2048))).astype(np.float32)
    return msk, selm


def kernel(x_prompt, x_sample, state_ssm, state_conv_ssd, state_conv_short, norm_w, w_in, conv_ssd_w, conv_ssd_b,
           dt_bias, a_log, d_skip, ssd_norm_w, conv_short_w, w_out, final_norm_w):
    f = lambda a: np.ascontiguousarray(np.asarray(a, dtype=np.float32))
    x_prompt, x_sample = f(x_prompt), f(x_sample)
    state_ssm, state_conv_ssd, state_conv_short = f(state_ssm), f(state_conv_ssd), f(state_conv_short)
    w_in0, w_out0 = f(w_in)[0], f(w_out)[0]
    blocks = []
    for seg in (C_ZS, C_XS):
        blocks += [(seg + i * 256, 256) for i in range(8)]
    blocks += [(C_B + g * 128, 128) for g in range(4)] + [(C_C + g * 128, 128) for g in range(4)] + [(C_DT, 32)]
    for seg in (C_ZC, C_BC, C_CC, C_VC):
        blocks += [(seg + i * 256, 256) for i in range(8)]
    w3 = w_in0.reshape(16, 128, DIN)
    wpk = np.empty((128, 16 * DIN), np.float32)
    for (c0, ncl) in blocks:
        wpk[:, 16 * c0:16 * (c0 + ncl)] = w3[:, :, c0:c0 + ncl].transpose(1, 0, 2).reshape(128, 16 * ncl)
    w_in0 = wpk
    w_out0 = np.ascontiguousarray(w_out0.reshape(4, 8, 128, 4, 512).transpose(2, 3, 0, 1, 4).reshape(128, 16 * 4096))
    msk, selm = _host_consts()
    cstb = np.zeros((128, K_END), np.float32)
    cstb[:, K_NW:K_NW + 16] = f(norm_w)[0].reshape(16, 128).T
    cstb[:, K_CSW:K_CSW + 96] = f(conv_ssd_w)[0].reshape(4, 24, 128).transpose(2, 1, 0).reshape(128, 96)
    cstb[:, K_CSB:K_CSB + 24] = f(conv_ssd_b)[0].reshape(24, 128).T
    cstb[:, K_CHW:K_CHW + 48] = f(conv_short_w)[0].reshape(3, 16, 128).transpose(2, 1, 0).reshape(128, 48)
    cstb[:, K_SNW:K_SNW + 16] = f(ssd_norm_w)[0].reshape(16, 128).T
    cstb[:, K_DTB:K_DTB + 32] = f(dt_bias)[0][None, :]
    cstb[:, K_ALOG:K_ALOG + 32] = f(a_log)[0][None, :]
    cstb[:, K_DSK:K_DSK + 32] = f(d_skip)[0][None, :]
    fnw = np.ascontiguousarray(np.broadcast_to(f(final_norm_w)[None, :], (128, DM)))
    in_maps = []
    for c in range(8):
        b, half = c // 2, c % 2
        xall = np.zeros((2180, DM), np.float32)
        if half == 1:
            xall[0:1024] = x_prompt[b, 0:1024]
            xall[2176:2180] = x_prompt[b, 1020:1024]
        xall[1024:2048] = x_prompt[b, half * 1024:(half + 1) * 1024]
        xall[2048:2176] = x_sample[16 * c:16 * c + 16].reshape(128, DM)
        cc = cstb.copy()
        cc[:, K_FLAG] = float(half)
        in_maps.append({
            "xall": xall,
            "st_ssm": np.ascontiguousarray(state_ssm[0, 16 * c:16 * c + 16].reshape(16, 2048, 128)),
            "st_cs": np.ascontiguousarray(state_conv_ssd[0, 16 * c:16 * c + 16].reshape(48, 3072)),
            "st_csh": np.ascontiguousarray(state_conv_short[0, 16 * c:16 * c + 16].reshape(32, 2048)),
            "w_in": w_in0, "w_out": w_out0, "cst": cc, "msk": msk, "selm": selm, "fnw": fnw,
        })
    if "nc" not in _NC_CACHE:
        _NC_CACHE["nc"] = build_nc()
    res = run_bass_kernel_spmd(_NC_CACHE["nc"], in_maps, core_ids=list(range(8)))
    R = res.results
    y_prompt = np.zeros((4, 2048, DM), np.float32)
    y_sample = np.zeros((128, 8, DM), np.float32)
    ssm_p = np.zeros((1, 4, 32, 64, 128), np.float32)
    cs_p = np.zeros((1, 4, 3, 3072), np.float32)
    csh_p = np.zeros((1, 4, 2, 2048), np.float32)
    ssm_s = np.zeros((1, 128, 32, 64, 128), np.float32)
    cs_s = np.zeros((1, 128, 3, 3072), np.float32)
    csh_s = np.zeros((1, 128, 2, 2048), np.float32)
    for c in range(8):
        b, half = c // 2, c % 2
        r = R[c]
        y_prompt[b, half * 1024:(half + 1) * 1024] = r["y"][0:1024]
        y_sample[16 * c:16 * c + 16] = r["y"][1024:1152].reshape(16, 8, DM)
        ssm_s[0, 16 * c:16 * c + 16] = r["ssm_s"].reshape(16, 32, 64, 128)
        cs_s[0, 16 * c:16 * c + 16] = r["cs_s"].reshape(16, 3, 3072)
        csh_s[0, 16 * c:16 * c + 16] = r["csh_s"].reshape(16, 2, 2048)
        if half == 1:
            ssm_p[0, b] = r["ssm_p"].reshape(32, 64, 128)
            cs_p[0, b] = r["cs_p"][1:4]
            csh_p[0, b] = r["csh_p"][2:4]
    return (y_prompt, y_sample, ssm_p, cs_p, csh_p, ssm_s, cs_s, csh_s)
```
